# Optimizing a Trainium2 kernel written in Bass

```python
import jax, jax.numpy as jnp
from jax import lax
import numpy as np

D_MODEL = 1024
BATCH = 8
SEQ = 2048
DEPTH = 4
DEC_BATCH = 32
DEC_SEQ = 4
PAST_LEN = 16384
PAGE_SIZE = 128

N_MIXERS = 3
N_A_LAYERS = (DEPTH + 2) // 3
N_B_LAYERS = (DEPTH + 1) // 3
N_C_LAYERS = DEPTH // 3

D_FF = (-(-8 * D_MODEL // (3 * 256))) * 256

A_CHUNK = 128
A_WIDTH = D_MODEL
A_GROUPS = 8
A_GROUP_DIM = A_WIDTH // A_GROUPS
LN_EPS = 1e-5

MLA_HEADS = 8
MLA_Q_LORA = 384
MLA_KV_LORA = 256
MLA_NOPE = 128
MLA_ROPE = 64
MLA_V = 128
MLA_LAT = MLA_KV_LORA + MLA_ROPE
MLA_SCALE = (MLA_NOPE + MLA_ROPE) ** -0.5
ROPE_THETA = 10000.0
Q_BLOCK = 128

LRU_WIDTH = 1280
LRU_BLOCKS = 10
LRU_BLOCK_DIM = LRU_WIDTH // LRU_BLOCKS
CONV_WIDTH = 4
LRU_C = 8.0

RMS_EPS = 1e-6

kernel_name = 'hybrid_gmlp_mla_rglru_step'


def rmsnorm(x, g):
    xf = x.astype(jnp.float32)
    y = xf * lax.rsqrt(jnp.mean(xf * xf, axis=-1, keepdims=True) + RMS_EPS)
    return (y * g.astype(jnp.float32)).astype(x.dtype)


def layernorm(x, g, b):
    xf = x.astype(jnp.float32)
    mu = jnp.mean(xf, axis=-1, keepdims=True)
    xc = xf - mu
    y = xc * lax.rsqrt(jnp.mean(xc * xc, axis=-1, keepdims=True) + LN_EPS)
    return (y * g.astype(jnp.float32) + b.astype(jnp.float32)).astype(x.dtype)


def swiglu(x, w_gate, w_up, w_down):
    return (jax.nn.silu(x @ w_gate) * (x @ w_up)) @ w_down


def chunk_mlp(x, w_in, ln_g, ln_b, w_s, b_s, w_out):
    bsz, seqlen, _ = x.shape
    z = jax.nn.gelu(x @ w_in)
    u, v = jnp.split(z, 2, axis=-1)
    v = layernorm(v, ln_g, ln_b)
    n_chunks = -(-seqlen // A_CHUNK)
    pad = n_chunks * A_CHUNK - seqlen
    vp = jnp.pad(v, ((0, 0), (0, pad), (0, 0))).reshape(bsz, n_chunks, A_CHUNK, A_GROUPS, A_GROUP_DIM)
    mask = jnp.tril(jnp.ones((A_CHUNK, A_CHUNK), dtype=bool))
    w = jnp.where(mask[None], w_s, 0)
    mixed = jnp.einsum('gts,bcsgd->bctgd', w, vp) + b_s.T[None, None, :, :, None]
    mixed = mixed.reshape(bsz, n_chunks * A_CHUNK, A_WIDTH)[:, :seqlen]
    return (u * mixed) @ w_out, v


def rope_angles(pos):
    inv = 1.0 / (ROPE_THETA ** (jnp.arange(0, MLA_ROPE, 2, dtype=jnp.float32) / MLA_ROPE))
    ang = pos.astype(jnp.float32)[:, None] * inv[None, :]
    return jnp.cos(ang), jnp.sin(ang)


def apply_rope(x, cos, sin):
    xf = x.astype(jnp.float32)
    x1, x2 = jnp.split(xf, 2, axis=-1)
    return jnp.concatenate([x1 * cos - x2 * sin, x1 * sin + x2 * cos], axis=-1).astype(x.dtype)


def mla_project(x, cos, sin, w_dq, q_norm, w_uq, w_dkv, kv_norm, w_uk):
    bsz, seqlen, _ = x.shape
    cq = rmsnorm(x @ w_dq, q_norm)
    q = (cq @ w_uq).reshape(bsz, seqlen, MLA_HEADS, MLA_NOPE + MLA_ROPE)
    q_nope, q_pe = q[..., :MLA_NOPE], q[..., MLA_NOPE:]
    kv = x @ w_dkv
    c_kv = rmsnorm(kv[..., :MLA_KV_LORA], kv_norm)
    k_pe = apply_rope(kv[..., MLA_KV_LORA:], cos, sin)
    q_pe = apply_rope(q_pe, cos[:, None, :], sin[:, None, :])
    q_lat = jnp.einsum('blhn,chn->blhc', q_nope, w_uk)
    q_full = jnp.concatenate([q_lat, q_pe], axis=-1)
    kv_row = jnp.concatenate([c_kv, k_pe], axis=-1)
    return q_full, kv_row


def mla_attend(q, q_pos, keys, k_pos):
    s = jnp.einsum('bqhc,bkc->bhqk', q, keys, preferred_element_type=jnp.float32) * MLA_SCALE
    mask = k_pos[None, :] <= q_pos[:, None]
    s = jnp.where(mask, s, -jnp.inf)
    p = jax.nn.softmax(s, axis=-1).astype(keys.dtype)
    return jnp.einsum('bhqk,bkc->bqhc', p, keys[..., :MLA_KV_LORA])


def mla_prompt_attention(q, kv, pos):
    bsz, seqlen, _, _ = q.shape
    nb = seqlen // Q_BLOCK
    qb = q.reshape(bsz, nb, Q_BLOCK, MLA_HEADS, MLA_LAT).transpose(1, 0, 2, 3, 4)
    qpos = pos.reshape(nb, Q_BLOCK)
    o = lax.map(lambda a: mla_attend(a[0], a[1], kv, pos), (qb, qpos))
    return o.transpose(1, 0, 2, 3, 4).reshape(bsz, seqlen, MLA_HEADS, MLA_KV_LORA)


def mla_output(o, w_uv, w_out):
    bsz, seqlen = o.shape[0], o.shape[1]
    ov = jnp.einsum('blhc,chv->blhv', o, w_uv).reshape(bsz, seqlen, MLA_HEADS * MLA_V)
    return ov @ w_out


def causal_conv(x, buf, w, b):
    seqlen = x.shape[1]
    xx = jnp.concatenate([buf.astype(x.dtype), x], axis=1)
    out = b + sum(w[k] * xx[:, k:k + seqlen] for k in range(CONV_WIDTH))
    return out, xx[:, -(CONV_WIDTH - 1):]


def block_diag(x, w, b):
    bsz, seqlen, _ = x.shape
    xb = x.reshape(bsz, seqlen, LRU_BLOCKS, LRU_BLOCK_DIM)
    return jnp.einsum('blni,nij->blnj', xb, w).reshape(bsz, seqlen, LRU_WIDTH) + b


def rglru(x, h0, w_a, b_a, w_i, b_i, lam):
    r = jax.nn.sigmoid(block_diag(x, w_a, b_a).astype(jnp.float32))
    gi = jax.nn.sigmoid(block_diag(x, w_i, b_i).astype(jnp.float32))
    log_a = -LRU_C * r * jax.nn.softplus(-lam.astype(jnp.float32))
    a = jnp.exp(log_a)
    bterm = jnp.sqrt(-jnp.expm1(2.0 * log_a)) * gi * x.astype(jnp.float32)
    bterm = bterm.at[:, 0].add(a[:, 0] * h0.astype(jnp.float32))

    def combine(left, right):
        a1, b1 = left
        a2, b2 = right
        return a1 * a2, a2 * b1 + b2

    _, h = lax.associative_scan(combine, (a, bterm), axis=1)
    return h.astype(x.dtype), h[:, -1].astype(x.dtype)


def recurrent_block(x, h0, conv_buf, w_x, w_gate, conv_w, conv_b, w_a, b_a, w_i, b_i, lam, w_out):
    gate = jax.nn.gelu(x @ w_gate)
    xc, new_buf = causal_conv(x @ w_x, conv_buf, conv_w, conv_b)
    h, h_last = rglru(xc, h0, w_a, b_a, w_i, b_i, lam)
    return (gate * h) @ w_out, h_last, new_buf


def setup_inputs(seed: int = 0) -> dict:
    key = jax.random.key(seed)
    ks = iter(jax.random.split(key, 64))

    def nrm(shape, fan_in=1):
        return jax.random.normal(next(ks), shape, jnp.float32) * (fan_in ** -0.5)

    def gain(shape):
        return 1.0 + 0.02 * jax.random.normal(next(ks), shape, jnp.float32)

    def small(shape):
        return 0.01 * jax.random.normal(next(ks), shape, jnp.float32)

    n_pages = PAST_LEN // PAGE_SIZE
    n_used = DEC_BATCH * n_pages
    n_pool = n_used + n_used // 4

    x_prompt = nrm((BATCH, SEQ, D_MODEL))
    x_sample = nrm((DEC_BATCH, DEC_SEQ, D_MODEL))
    cache_mla = nrm((N_B_LAYERS, n_pool, PAGE_SIZE, MLA_LAT))
    state_lru_h = 0.5 * nrm((N_C_LAYERS, DEC_BATCH, LRU_WIDTH))
    state_lru_conv = nrm((N_C_LAYERS, DEC_BATCH, CONV_WIDTH - 1, LRU_WIDTH))
    page_table = jax.random.permutation(next(ks), n_pool)[:n_used].reshape(DEC_BATCH, n_pages).astype(jnp.int32)

    a0 = jax.random.uniform(next(ks), (N_C_LAYERS, LRU_WIDTH), jnp.float32, 0.9, 0.999)
    a_base = a0 ** (1.0 / LRU_C)
    c_lambda = jnp.log(a_base) - jnp.log1p(-a_base)

    return {
        'x_prompt': x_prompt,
        'x_sample': x_sample,
        'cache_mla': cache_mla,
        'state_lru_h': state_lru_h,
        'state_lru_conv': state_lru_conv,
        'page_table': page_table,
        'norm_mix': gain((DEPTH, D_MODEL)),
        'norm_ffn': gain((DEPTH, D_MODEL)),
        'norm_out': gain((D_MODEL,)),
        'a_w_in': nrm((N_A_LAYERS, D_MODEL, 2 * A_WIDTH), D_MODEL),
        'a_ln_g': gain((N_A_LAYERS, A_WIDTH)),
        'a_ln_b': small((N_A_LAYERS, A_WIDTH)),
        'a_w_s': nrm((N_A_LAYERS, A_GROUPS, A_CHUNK, A_CHUNK), A_CHUNK),
        'a_b_s': gain((N_A_LAYERS, A_GROUPS, A_CHUNK)),
        'a_w_out': nrm((N_A_LAYERS, A_WIDTH, D_MODEL), A_WIDTH),
        'b_w_dq': nrm((N_B_LAYERS, D_MODEL, MLA_Q_LORA), D_MODEL),
        'b_q_norm': gain((N_B_LAYERS, MLA_Q_LORA)),
        'b_w_uq': nrm((N_B_LAYERS, MLA_Q_LORA, MLA_HEADS * (MLA_NOPE + MLA_ROPE)), MLA_Q_LORA),
        'b_w_dkv': nrm((N_B_LAYERS, D_MODEL, MLA_LAT), D_MODEL),
        'b_kv_norm': gain((N_B_LAYERS, MLA_KV_LORA)),
        'b_w_uk': nrm((N_B_LAYERS, MLA_KV_LORA, MLA_HEADS, MLA_NOPE), MLA_KV_LORA),
        'b_w_uv': nrm((N_B_LAYERS, MLA_KV_LORA, MLA_HEADS, MLA_V), MLA_KV_LORA),
        'b_w_out': nrm((N_B_LAYERS, MLA_HEADS * MLA_V, D_MODEL), MLA_HEADS * MLA_V),
        'c_w_x': nrm((N_C_LAYERS, D_MODEL, LRU_WIDTH), D_MODEL),
        'c_w_gate': nrm((N_C_LAYERS, D_MODEL, LRU_WIDTH), D_MODEL),
        'c_conv_w': nrm((N_C_LAYERS, CONV_WIDTH, LRU_WIDTH), CONV_WIDTH),
        'c_conv_b': small((N_C_LAYERS, LRU_WIDTH)),
        'c_w_a': nrm((N_C_LAYERS, LRU_BLOCKS, LRU_BLOCK_DIM, LRU_BLOCK_DIM), LRU_BLOCK_DIM),
        'c_b_a': small((N_C_LAYERS, LRU_WIDTH)),
        'c_w_i': nrm((N_C_LAYERS, LRU_BLOCKS, LRU_BLOCK_DIM, LRU_BLOCK_DIM), LRU_BLOCK_DIM),
        'c_b_i': small((N_C_LAYERS, LRU_WIDTH)),
        'c_lambda': c_lambda,
        'c_w_out': nrm((N_C_LAYERS, LRU_WIDTH, D_MODEL), LRU_WIDTH),
        'f_w_gate': nrm((DEPTH, D_MODEL, D_FF), D_MODEL),
        'f_w_up': nrm((DEPTH, D_MODEL, D_FF), D_MODEL),
        'f_w_down': nrm((DEPTH, D_FF, D_MODEL), D_FF),
    }


def reference(x_prompt, x_sample, cache_mla, state_lru_h, state_lru_conv, page_table,
              norm_mix, norm_ffn, norm_out,
              a_w_in, a_ln_g, a_ln_b, a_w_s, a_b_s, a_w_out,
              b_w_dq, b_q_norm, b_w_uq, b_w_dkv, b_kv_norm, b_w_uk, b_w_uv, b_w_out,
              c_w_x, c_w_gate, c_conv_w, c_conv_b, c_w_a, c_b_a, c_w_i, c_b_i, c_lambda, c_w_out,
              f_w_gate, f_w_up, f_w_down):
    past_len = page_table.shape[1] * PAGE_SIZE
    n_pr, len_p = x_prompt.shape[0], x_prompt.shape[1]
    n_dec, len_s = x_sample.shape[0], x_sample.shape[1]
    pos_p = jnp.arange(len_p, dtype=jnp.int32)
    pos_s = past_len + jnp.arange(len_s, dtype=jnp.int32)
    k_pos_s = jnp.arange(past_len + len_s, dtype=jnp.int32)
    cos_p, sin_p = rope_angles(pos_p)
    cos_s, sin_s = rope_angles(pos_s)

    hp, hs = x_prompt, x_sample
    chunk_v_s, mla_p, mla_s = [], [], []
    lru_h_p, lru_h_s, lru_c_p, lru_c_s = [], [], [], []
    for layer in range(DEPTH):
        kind, j = layer % N_MIXERS, layer // N_MIXERS
        zp = rmsnorm(hp, norm_mix[layer])
        zs = rmsnorm(hs, norm_mix[layer])
        if kind == 0:
            a_args = (a_w_in[j], a_ln_g[j], a_ln_b[j], a_w_s[j], a_b_s[j], a_w_out[j])
            yp, _ = chunk_mlp(zp, *a_args)
            ys, v_s = chunk_mlp(zs, *a_args)
            chunk_v_s.append(v_s)
        elif kind == 1:
            b_args = (b_w_dq[j], b_q_norm[j], b_w_uq[j], b_w_dkv[j], b_kv_norm[j], b_w_uk[j])
            q_p, kv_p = mla_project(zp, cos_p, sin_p, *b_args)
            o_p = mla_prompt_attention(q_p, kv_p, pos_p)
            q_s, kv_s = mla_project(zs, cos_s, sin_s, *b_args)
            past = cache_mla[j][page_table].reshape(n_dec, past_len, MLA_LAT)
            keys = jnp.concatenate([past, kv_s.astype(past.dtype)], axis=1)
            o_s = mla_attend(q_s, pos_s, keys, k_pos_s)
            yp = mla_output(o_p, b_w_uv[j], b_w_out[j])
            ys = mla_output(o_s, b_w_uv[j], b_w_out[j])
            mla_p.append(kv_p)
            mla_s.append(kv_s)
        else:
            c_args = (c_w_x[j], c_w_gate[j], c_conv_w[j], c_conv_b[j], c_w_a[j], c_b_a[j],
                      c_w_i[j], c_b_i[j], c_lambda[j], c_w_out[j])
            h0_p = jnp.zeros((n_pr, LRU_WIDTH), x_prompt.dtype)
            buf0_p = jnp.zeros((n_pr, CONV_WIDTH - 1, LRU_WIDTH), x_prompt.dtype)
            yp, hl_p, cb_p = recurrent_block(zp, h0_p, buf0_p, *c_args)
            ys, hl_s, cb_s = recurrent_block(zs, state_lru_h[j], state_lru_conv[j], *c_args)
            lru_h_p.append(hl_p)
            lru_h_s.append(hl_s)
            lru_c_p.append(cb_p)
            lru_c_s.append(cb_s)
        hp = hp + yp
        hs = hs + ys
        hp = hp + swiglu(rmsnorm(hp, norm_ffn[layer]), f_w_gate[layer], f_w_up[layer], f_w_down[layer])
        hs = hs + swiglu(rmsnorm(hs, norm_ffn[layer]), f_w_gate[layer], f_w_up[layer], f_w_down[layer])

    y_prompt = rmsnorm(hp, norm_out)
    y_sample = rmsnorm(hs, norm_out)
    return (y_prompt, y_sample, jnp.stack(chunk_v_s), jnp.stack(mla_p), jnp.stack(mla_s),
            jnp.stack(lru_h_p), jnp.stack(lru_h_s), jnp.stack(lru_c_p), jnp.stack(lru_c_s))
```

```python
import numpy as np
import concourse.bass as bass
import concourse.mybir as mybir
from concourse.bass_utils import run_bass_kernel_spmd

F32 = mybir.dt.float32
BF16 = mybir.dt.bfloat16
I32 = mybir.dt.int32
U32 = mybir.dt.uint32
U8 = mybir.dt.uint8
AF = mybir.ActivationFunctionType
ALU = mybir.AluOpType
AX = mybir.AxisListType
DTSIZE = {F32: 4, BF16: 2, I32: 4, U32: 4, U8: 1}
ENGS = ['pe', 'act', 'dve', 'pool', 'sp']


class Op:
    __slots__ = ('eng', 'kind', 'fn', 'deps', 'needed', 'idx', 'sem', 'count', 'pos')

    def __init__(self, eng, kind, fn):
        self.eng = eng
        self.kind = kind
        self.fn = fn
        self.deps = []
        self.needed = False
        self.idx = 0
        self.sem = None
        self.count = 0
        self.pos = 0


class Buf:
    def __init__(self, name, ap, start, end):
        self.name = name
        self.ap = ap
        self.start = start
        self.end = end
        self.last_write = None
        self.readers = []
        self.dsem = None
        self.dcount = 0

    def __getitem__(self, k):
        return self.ap[k]


def _prune(lst):
    best = {}
    for r in lst:
        if r is None:
            continue
        if r.kind == 'c':
            key = ('e', r.eng)
            if key not in best or best[key].pos < r.pos:
                best[key] = r
        else:
            key = ('d', r.sem)
            if key not in best or best[key].count < r.count:
                best[key] = r
    return list(best.values())


class Prog:
    def __init__(self, nc, arena_bytes=206 * 1024, n_dsem=70):
        self.nc = nc
        self.arena = nc.alloc_sbuf_tensor("arena", [128, arena_bytes], U8)
        self.arena_bytes = arena_bytes
        self.top = 0
        self.live = []
        self.retired = []
        self.ops = {e: [] for e in ENGS}
        self.n_dsem = n_dsem
        self.free_dsem = [(i, 0) for i in range(n_dsem)]
        self.dma_bufs = []
        self.banks = []
        self.PB = []
        self.ps_all = nc.alloc_psum_tensor("ps_all", [128, 4096], F32)
        for i in range(8):
            self.PB.append(Buf("psb%d" % i, self.ps_all[:, i * 512:(i + 1) * 512], i * 2048, (i + 1) * 2048))
        self.maxtop = 0

    def sb(self, name, shape, dtype, align=64):
        nbytes = int(np.prod(shape[1:])) * DTSIZE[dtype]
        start = (self.top + align - 1) // align * align
        end = start + nbytes
        assert end <= self.arena_bytes, "SBUF arena overflow at %s: %d" % (name, end)
        self.top = end
        self.maxtop = max(self.maxtop, end)
        np_ = shape[0]
        v = self.arena[0:np_, start:end].bitcast(dtype)
        if len(shape) > 2:
            names = ["d%d" % i for i in range(len(shape) - 1)]
            pat = "p (" + " ".join(names) + ") -> p " + " ".join(names)
            kw = {names[i]: shape[i + 1] for i in range(len(names) - 1)}
            v = v.rearrange(pat, **kw)
        b = Buf(name, v, start, end)
        inh = []
        keep = []
        for o in self.retired:
            if o.start < end and start < o.end:
                inh.append(o.last_write)
                inh.extend(o.readers)
                if not (start <= o.start and o.end <= end):
                    keep.append(o)
            else:
                keep.append(o)
        self.retired = keep
        b.readers = _prune(inh)
        self.live.append(b)
        return b

    def mark(self):
        return (self.top, len(self.live))

    def release(self, m):
        top, n = m
        for b in self.live[n:]:
            self.retired.append(b)
            if b.dsem is not None:
                self.free_dsem.append((b.dsem, b.dcount))
                b.dsem_released = True
        self.live = self.live[:n]
        self.top = top

    def _dep_on(self, O, P):
        if P is None:
            return
        if P.kind == 'c':
            if P.eng == 'pe' and O.eng == 'pe' and O.kind == 'c':
                return
            P.needed = True
        O.deps.append(P)

    def _record(self, O, reads, writes):
        for b in reads:
            self._dep_on(O, b.last_write)
        for b in writes:
            self._dep_on(O, b.last_write)
            for r in b.readers:
                self._dep_on(O, r)
        for b in writes:
            b.last_write = O
            b.readers = []
        for b in reads:
            if b in writes:
                continue
            if O.kind == 'c':
                b.readers = [r for r in b.readers if not (r.kind == 'c' and r.eng == O.eng)]
            else:
                b.readers = [r for r in b.readers if not (r.kind == 'd' and r.sem == O.sem)]
            b.readers.append(O)
        O.pos = len(self.ops[O.eng])
        self.ops[O.eng].append(O)

    def op(self, eng, fn, reads=(), writes=()):
        O = Op(eng, 'c', fn)
        self._record(O, list(reads), list(writes))
        return O

    def _buf_dsem(self, b):
        if b.dsem is None:
            assert self.free_dsem, "out of dma semaphores"
            i, c = self.free_dsem.pop(0)
            b.dsem = i
            b.dcount = c
            if c > 0:
                m = Op('sp', 'd', None)
                m.sem = i
                m.count = c
                b.readers.append(m)
            self.dma_bufs.append(b)

    def dma(self, q, out_ap, in_ap, buf, is_write, reads=(), writes=(), **kw):
        self._buf_dsem(buf)
        buf.dcount += 1

        def fn(eng):
            return eng.dma_start(out=out_ap, in_=in_ap, **kw)
        O = Op(q, 'd', fn)
        O.sem = buf.dsem
        O.count = buf.dcount
        rd = list(reads)
        wr = list(writes)
        if is_write:
            wr.append(buf)
        else:
            rd.append(buf)
        self._record(O, rd, wr)
        return O

    def dma_custom(self, q, fn, buf, is_write, reads=(), writes=()):
        self._buf_dsem(buf)
        buf.dcount += 1
        O = Op(q, 'd', fn)
        O.sem = buf.dsem
        O.count = buf.dcount
        rd = list(reads)
        wr = list(writes)
        if is_write:
            wr.append(buf)
        else:
            rd.append(buf)
        self._record(O, rd, wr)
        return O

    def finish(self):
        O = Op('sp', 'c', lambda eng: eng.nop())
        seen = {}
        for b in self.dma_bufs:
            m = Op('sp', 'd', None)
            m.sem = b.dsem
            m.count = b.dcount
            if b.dsem not in seen or seen[b.dsem].count < m.count:
                seen[b.dsem] = m
        O.deps.extend(seen.values())
        O.pos = len(self.ops['sp'])
        self.ops['sp'].append(O)

    def mm(self, out_buf, out_ap, pairs, reads):
        pairs = list(pairs)

        def fn(pe):
            n = len(pairs)
            ins = None
            for i, (l, r) in enumerate(pairs):
                ins = pe.matmul(out_ap, l, r, start=(i == 0), stop=(i == n - 1))
            return ins
        return self.op('pe', fn, reads=reads, writes=[out_buf])

    def tr(self, out_buf, out_ap, in_ap, ident_ap, reads):
        def fn(pe):
            return pe.transpose(out_ap, in_ap, ident_ap)
        return self.op('pe', fn, reads=reads, writes=[out_buf])

    def emit(self):
        nc = self.nc
        for e in ENGS:
            n = 0
            for o in self.ops[e]:
                if o.kind == 'c' and o.needed:
                    n += 1
                    o.idx = n
        from contextlib import ExitStack
        with ExitStack() as st:
            esem = {e: st.enter_context(nc.semaphore("es_" + e)) for e in ENGS}
            dsem = [st.enter_context(nc.semaphore("ds_%d" % i)) for i in range(self.n_dsem)]
            block = st.enter_context(nc.Block())

            def run(ename, eng):
                waited = {}
                for o in self.ops[ename]:
                    need = {}
                    for p in o.deps:
                        if p.kind == 'c':
                            key = ('e', p.eng)
                            val = p.idx
                        else:
                            key = ('d', p.sem)
                            val = p.count * 16
                        if need.get(key, 0) < val:
                            need[key] = val
                    for key, val in need.items():
                        if waited.get(key, 0) >= val:
                            continue
                        waited[key] = val
                        sem = esem[key[1]] if key[0] == 'e' else dsem[key[1]]
                        eng.wait_ge(sem, val)
                    ins = o.fn(eng)
                    if o.kind == 'c':
                        if o.needed:
                            ins.then_inc(esem[ename], 1)
                    else:
                        ins.then_inc(dsem[o.sem], 16)

            @block.tensor
            def _(e):
                run('pe', e)

            @block.scalar
            def _(e):
                run('act', e)

            @block.vector
            def _(e):
                run('dve', e)

            @block.gpsimd
            def _(e):
                run('pool', e)

            @block.sync
            def _(e):
                run('sp', e)

import math

D = 1024
DFF = 2816
NT = 17
ST = 16
SEQ = 2048
NPAGES = 128
NPOOL = 5120
TSEG = 16
MLA_SCALE = (128 + 64) ** -0.5
NEG = -30000.0
TWO_PI = 2.0 * math.pi
C1 = 6.28125
C2 = TWO_PI - 6.28125


class K:
    def __init__(self, depth=4):
        self.depth = depth
        nc = bass.Bass("TRN2", target_bir_lowering=False)
        self.nc = nc
        self.p = Prog(nc)
        self.bank_rr = list(range(8))
        self.bank_i = 0
        self.ev_i = 0
        self.din = {}
        self.dout = {}

    def inp(self, name, shape, dt=F32):
        a = self.nc.dram_tensor(name, list(shape), dt, kind="ExternalInput").ap()
        self.din[name] = a
        return a

    def outp(self, name, shape):
        a = self.nc.dram_tensor(name, list(shape), F32, kind="ExternalOutput").ap()
        self.dout[name] = a
        return a

    def bank(self):
        b = self.p.PB[self.bank_rr[self.bank_i % len(self.bank_rr)]]
        self.bank_i += 1
        return b

    def act(self, out, in_, func, reads, writes, bias=None, scale=None, accum=None):
        kw = {}
        if bias is not None:
            kw['bias'] = bias
        if scale is not None:
            kw['scale'] = scale
        if accum is not None:
            kw['accum_out'] = accum
        return self.p.op('act', lambda e: e.activation(out, in_, func, **kw), reads=reads, writes=writes)

    def ts(self, out, in0, s1, s2, op0, op1, reads, writes, eng='dve'):
        if s2 is None:
            return self.p.op(eng, lambda e: e.tensor_scalar(out, in0, s1, None, op0), reads=reads, writes=writes)
        return self.p.op(eng, lambda e: e.tensor_scalar(out, in0, s1, s2, op0, op1), reads=reads, writes=writes)

    def tt(self, out, a, b, op, reads, writes, eng='dve'):
        return self.p.op(eng, lambda e: e.tensor_tensor(out, a, b, op), reads=reads, writes=writes)

    def stt(self, out, in0, scalar, in1, op0, op1, reads, writes, eng='dve'):
        return self.p.op(eng, lambda e: e.scalar_tensor_tensor(out, in0, scalar, in1, op0, op1), reads=reads, writes=writes)

    def cp(self, out, in_, reads, writes, eng=None):
        if eng is None:
            eng = 'act' if (self.ev_i % 2 == 0) else 'dve'
            self.ev_i += 1
        if eng == 'act':
            return self.p.op('act', lambda e: e.copy(out, in_), reads=reads, writes=writes)
        return self.p.op(eng, lambda e: e.tensor_copy(out, in_), reads=reads, writes=writes)

    def mm(self, ob, out_ap, pairs, reads):
        return self.p.mm(ob, out_ap, pairs, reads)

    def mm_multi(self, ob, triples, reads):
        triples = list(triples)

        def fn(pe):
            ins = None
            for (o, l, r, st, sp) in triples:
                ins = pe.matmul(o, l, r, start=st, stop=sp)
            return ins
        return self.p.op('pe', fn, reads=reads, writes=[ob])

    def load_w(self, name, src, ktiles, cols, q='pool', split=None):
        b = self.p.sb(name, [128, ktiles, cols], BF16)
        v = src.rearrange("(k p) n -> p k n", p=128)
        if split is None:
            split = max(1, (ktiles * cols * 128 * 4) // (2 << 20))
        split = min(split, ktiles)
        step = (ktiles + split - 1) // split
        for k0 in range(0, ktiles, step):
            k1 = min(ktiles, k0 + step)
            self.p.dma(q, b[:, k0:k1, :], v[:, k0:k1, :], b, True)
        return b

    def load_w_into(self, b, src, ktiles, q='pool'):
        v = src.rearrange("(k p) n -> p k n", p=128)
        self.p.dma(q, b[:, 0:ktiles, 0:v.shape[2]], v, b, True)

    def load_bc(self, name, row_ap, n, np_=128):
        b = self.p.sb(name, [128, n], F32)
        self.p.dma('sp', b[0:np_, :], row_ap.partition_broadcast(np_), b, True)
        return b

    def load_col(self, name, vec_ap, kt):
        b = self.p.sb(name, [128, kt], F32)

        def fn(e):
            return e.dma_start(out=b[:, :], in_=vec_ap.rearrange("(t p) -> p t", p=128), allow_slow_non_contiguous=True)
        self.p.dma_custom('sp', fn, b, True)
        return b

    def setup_consts(self):
        p = self.p
        ones = p.sb("ones", [128, 128], F32)
        identf = self.identf = p.sb("identf", [128, 128], F32)
        self.identb = p.sb("identb", [128, 128], BF16)
        p.op('pool', lambda e: e.memset(ones[:, :], 1.0), writes=[ones])
        p.op('pool', lambda e: e.affine_select(identf[:, :], ones[:, :], [[-1, 128]], ALU.is_equal, 0.0, base=0, channel_multiplier=1), reads=[ones], writes=[identf])
        p.op('pool', lambda e: e.tensor_copy(self.identb[:, :], identf[:, :]), reads=[identf], writes=[self.identb])
        zer = p.sb("zer", [128, 128], F32)
        maskf = p.sb("maskf", [128, 128], F32)
        self.maskb = p.sb("maskb", [128, 128], BF16)
        p.op('pool', lambda e: e.memset(zer[:, :], 0.0), writes=[zer])
        p.op('pool', lambda e: e.affine_select(maskf[:, :], zer[:, :], [[-1, 128]], ALU.is_ge, NEG, base=0, channel_multiplier=1), reads=[zer], writes=[maskf])
        p.op('pool', lambda e: e.tensor_copy(self.maskb[:, :], maskf[:, :]), reads=[maskf], writes=[self.maskb])
        self.mask_s = p.sb("mask_s", [128, 4], F32)
        p.op('pool', lambda e: e.affine_select(self.mask_s[:, :], zer[:, 0:4], [[-8, 4]], ALU.is_ge, NEG, base=0, channel_multiplier=1), reads=[zer], writes=[self.mask_s])
        self.cst = p.sb("cst", [128, 8], F32)
        vals = [1e-6, 1e-5, 1.0, 0.0, -0.5, 0.5, 0.0, 0.0]
        for i, v in enumerate(vals):
            p.op('pool', (lambda i, v: (lambda e: e.memset(self.cst[:, i:i + 1], v)))(i, v), writes=[self.cst])
        self.eps_rms = self.cst[:, 0:1]
        self.eps_ln = self.cst[:, 1:2]
        self.one_col = self.cst[:, 2:3]
        self.junk = p.sb("junk", [128, 1024], BF16)
        self.ss = p.sb("ss", [128, 24], F32)
        self.sd = p.sb("sd", [128, 24], F32)
        self.rstd = p.sb("rstd", [128, 24], F32)
        p.op('pool', lambda e: e.memset(self.ss[:, :], 1.0), writes=[self.ss])

    def rope_tables(self, name, np_, nt, base, chan_mul, tstep):
        p = self.p
        m = p.mark()
        posi = p.sb(name + "_pi", [128, nt], I32)
        posf = p.sb(name + "_pf", [128, nt], F32)
        ang = p.sb(name + "_ang", [128, nt, 32], F32)
        q = p.sb(name + "_q", [128, nt, 32], F32)
        qi = p.sb(name + "_qi", [128, nt, 32], I32)
        r = p.sb(name + "_r", [128, nt, 32], F32)
        t1 = p.sb(name + "_t1", [128, nt, 32], F32)
        rc = p.sb(name + "_rc", [128, nt, 32], F32)
        cos = self.p_persist(name + "_cos", [128, nt, 32], F32)
        sin = self.p_persist(name + "_sin", [128, nt, 32], F32)
        P = slice(0, np_)
        p.op('pool', lambda e: e.iota(posi[P, :], [[tstep, nt]], base=base, channel_multiplier=chan_mul), writes=[posi])
        p.op('dve', lambda e: e.tensor_copy(posf[P, :], posi[P, :]), reads=[posi], writes=[posf])
        invb = self.invf_bc
        self.tt(ang[P, :, :], posf[P, :].unsqueeze(2).broadcast_to([np_, nt, 32]),
                invb[P, :].unsqueeze(1).broadcast_to([np_, nt, 32]), ALU.mult, [posf, invb], [ang])

        def reduce_to(dst, src_buf, src, shift):
            self.ts(q[P], src, shift, 1.0 / TWO_PI, ALU.add, ALU.mult, [src_buf], [q])
            p.op('dve', lambda e: e.tensor_copy(qi[P], q[P]), reads=[q], writes=[qi])
            p.op('dve', lambda e: e.tensor_copy(q[P], qi[P]), reads=[qi], writes=[q])
            self.ts(t1[P], src, shift, None, ALU.add, None, [src_buf], [t1])
            self.stt(t1[P], q[P], -C1, t1[P], ALU.mult, ALU.add, [q, t1], [t1])
            self.stt(t1[P], q[P], -C2, t1[P], ALU.mult, ALU.add, [q, t1], [t1])
            self.ts(q[P], t1[P], math.pi, -TWO_PI, ALU.is_gt, ALU.mult, [t1], [q])
            self.tt(t1[P], t1[P], q[P], ALU.add, [t1, q], [t1])
            self.ts(q[P], t1[P], -math.pi, TWO_PI, ALU.is_lt, ALU.mult, [t1], [q])
            self.tt(t1[P], t1[P], q[P], ALU.add, [t1, q], [t1])
            self.ts(dst[P], t1[P], 3.141592, -3.141592, ALU.min, ALU.max, [t1], [dst])
        reduce_to(r, ang, ang[P], 0.0)
        self.act(sin[P], r[P], AF.Sin, [r], [sin])
        reduce_to(rc, ang, ang[P], math.pi / 2)
        self.act(cos[P], rc[P], AF.Sin, [rc], [cos])
        return cos, sin

    def p_persist(self, name, shape, dt):
        return self.p.sb(name, shape, dt)

    def norm_tiles(self, tiles, gain_bc, dst, col0s):
        p = self.p
        HB = self.HB
        for t in tiles:
            np_ = 16 if t == ST else 128
            self.act(self.junk[0:np_, :], HB[t][0:np_, :], AF.Square, [HB[t]], [self.junk, self.ss], accum=self.ss[0:np_, t:t + 1])
        lo, hi = min(tiles), max(tiles) + 1
        self.act(self.sd[:, lo:hi], self.ss[:, lo:hi], AF.Sqrt, [self.ss], [self.sd], bias=self.eps_rms, scale=1.0 / D)
        p.op('dve', lambda e: e.reciprocal(self.rstd[:, lo:hi], self.sd[:, lo:hi]), reads=[self.sd], writes=[self.rstd])
        for t, c0 in zip(tiles, col0s):
            np_ = 16 if t == ST else 128
            xn = self.xn_bufs[self.xn_i % 2]
            self.xn_i += 1
            self.stt(xn[0:np_, :], HB[t][0:np_, :], self.rstd[0:np_, t:t + 1], gain_bc[0:np_, :], ALU.mult, ALU.mult,
                     [HB[t], self.rstd, gain_bc], [xn])
            pb = self.bank()
            pv = pb.ap.bitcast(BF16).rearrange("p (k n) -> p k n", k=8)
            for k in range(8):
                p.tr(pb, pv[:, k, 0:np_], xn[0:np_, k * 128:(k + 1) * 128], self.identb[0:np_, 0:np_], [xn, self.identb])
            self.cp(dst[:, :, c0:c0 + np_], pv[:, :, 0:np_], [pb], [dst])

    def add_H(self, t, np_, hf, pb):
        HB = self.HB
        self.tt(HB[t][0:np_, hf * 512:(hf + 1) * 512], HB[t][0:np_, hf * 512:(hf + 1) * 512], pb[0:np_, :], ALU.add,
                [HB[t], pb], [HB[t]])

    def ffn(self, layer):
        p = self.p
        m = p.mark()
        Wg0 = p.sb("Wg0", [128, 8, 512], BF16)
        Wu0 = p.sb("Wu0", [128, 8, 512], BF16)
        Wd0 = p.sb("Wd0", [128, 4, 1024], BF16)
        gain = self.load_bc("gF", self.din['norm_ffn'][layer, :], D)
        xnT = p.sb("xnT", [128, 8, 2064], BF16)
        chunks = [(0, 4), (4, 4), (8, 4), (12, 4), (16, 3), (19, 3)]
        Wg = [Wg0, p.sb("Wg1", [128, 8, 512], BF16)]
        Wu = [Wu0, p.sb("Wu1", [128, 8, 512], BF16)]
        Wd = [Wd0, p.sb("Wd1", [128, 4, 1024], BF16)]
        actT = p.sb("actT", [128, 4, 2064], BF16)
        sg = [p.sb("sg%d" % i, [128, 512], F32) for i in range(2)]
        wg_d = self.din['f_w_gate'][layer]
        wu_d = self.din['f_w_up'][layer]
        wd_d = self.din['f_w_down'][layer]
        groups = [(g * 512, 512) for g in range(4)] + [(2048, 16)]

        def load(ci):
            f0, nf = chunks[ci]
            s = ci % 2
            self.load_w_into(Wg[s], wg_d[:, f0 * 128:(f0 + nf) * 128], 8)
            self.load_w_into(Wu[s], wu_d[:, f0 * 128:(f0 + nf) * 128], 8)
            self.load_w_into(Wd[s], wd_d[f0 * 128:(f0 + nf) * 128, :], nf)
        load(0)
        self.norm_tiles(list(range(NT)), gain, xnT, [t * 128 for t in range(NT)])
        si = 0
        for ci, (f0, nf) in enumerate(chunks):
            s = ci % 2
            if ci + 1 < len(chunks):
                load(ci + 1)
            if ci == 3 and self.mid_hook is not None:
                self.mid_hook()
                self.mid_hook = None
            for fi in range(nf):
                for (c0, n) in groups:
                    pg = self.bank()
                    pu = self.bank()
                    self.mm(pg, pg[:, 0:n], [(Wg[s][:, k, fi * 128:(fi + 1) * 128], xnT[:, k, c0:c0 + n]) for k in range(8)], [Wg[s], xnT])
                    self.mm(pu, pu[:, 0:n], [(Wu[s][:, k, fi * 128:(fi + 1) * 128], xnT[:, k, c0:c0 + n]) for k in range(8)], [Wu[s], xnT])
                    sgb = sg[si % 2]
                    si += 1
                    self.act(sgb[:, 0:n], pg[:, 0:n], AF.Silu, [pg], [sgb])
                    self.tt(actT[:, fi, c0:c0 + n], sgb[:, 0:n], pu[:, 0:n], ALU.mult, [sgb, pu], [actT])
            for t in range(NT):
                np_ = 16 if t == ST else 128
                for hf in range(2):
                    pb = self.bank()
                    self.mm(pb, pb[0:np_, :], [(actT[:, fi, t * 128:t * 128 + np_], Wd[s][:, fi, hf * 512:(hf + 1) * 512]) for fi in range(nf)], [actT, Wd[s]])
                    self.add_H(t, np_, hf, pb)
        p.release(m)

    def mixer_A(self, layer, j):
        p = self.p
        m = p.mark()
        Win = self.load_w("Win", self.din['a_w_in'][j], 8, 2048, split=4)
        Wout = self.load_w("WoA", self.din['a_w_out'][j], 8, 1024, split=2)
        gain = self.load_bc("gA", self.din['norm_mix'][layer, :], D)
        lng = self.load_bc("lng", self.din['a_ln_g'][j, :], D)
        lnb = self.load_bc("lnb", self.din['a_ln_b'][j, :], D)
        bsb = self.load_bc("bsb", self.din['a_b_s'][j].rearrange("g t -> (g t)"), 1024)
        bsv = bsb.ap.rearrange("p (g t) -> p g t", g=8)
        wsn = p.sb("wsn", [128, 8, 128], F32)
        wsm = p.sb("wsm", [128, 8, 128], F32)
        wsb = p.sb("wsb", [128, 8, 128], BF16)
        WsT = p.sb("WsT", [128, 8, 128], BF16)
        p.dma('sp', wsn[:, :, :], self.din['a_w_s'][j].rearrange("g t s -> t g s"), wsn, True)
        p.op('pool', lambda e: e.affine_select(wsm[:, :, :], wsn[:, :, :], [[0, 8], [-1, 128]], ALU.is_ge, 0.0, base=0, channel_multiplier=1), reads=[wsn], writes=[wsm])
        p.op('pool', lambda e: e.tensor_copy(wsb[:, :, :], wsm[:, :, :]), reads=[wsm], writes=[wsb])
        pb = self.bank()
        pv = pb.ap.bitcast(BF16).rearrange("p (k n) -> p k n", k=8)
        for g in range(8):
            p.tr(pb, pv[:, g, :], wsb[:, g, :], self.identb[:, :], [wsb, self.identb])
        self.cp(WsT[:, :, :], pv[:, :, :], [pb], [WsT])
        n4 = p.sb("n4", [4, 8, 4], F32)
        n44 = p.sb("n44", [4, 8, 4, 4], F32)
        wk32 = p.sb("wk32", [16, 8, 4, 4], F32)
        wkm = p.sb("wkm", [16, 8, 4, 4], F32)
        wkm2 = p.sb("wkm2", [16, 8, 4, 4], F32)
        Wblk = p.sb("Wblk", [16, 8, 16], BF16)
        p.dma('sp', n4[:, :, :], self.din['a_w_s'][j][:, 0:4, 0:4].rearrange("g t s -> t g s"), n4, True)
        p.op('dve', lambda e: e.tensor_copy(n44[:, :, :, :], n4[:, :, :].unsqueeze(2).broadcast_to([4, 8, 4, 4])), reads=[n4], writes=[n44])
        pbw = self.bank()
        for g in range(8):
            self.mm(pbw, pbw[0:16, g * 4:(g + 1) * 4], [(n44[0:4, g, :, :].rearrange("p b s -> p (b s)"), self.identf[0:4, 0:4])], [n44, self.identf])
        p.op('dve', lambda e: e.tensor_copy(wk32[:, :, :, :], pbw[0:16, 0:32].rearrange("p (g t) -> p g t", g=8).unsqueeze(2).broadcast_to([16, 8, 4, 4])), reads=[pbw], writes=[wk32])
        p.op('pool', lambda e: e.affine_select(wkm[:, :, :, :], wk32[:, :, :, :], [[0, 8], [-4, 4], [0, 4]], ALU.is_ge, 0.0, base=0, channel_multiplier=1), reads=[wk32], writes=[wkm])
        p.op('pool', lambda e: e.affine_select(wkm2[:, :, :, :], wkm[:, :, :, :], [[0, 8], [4, 4], [1, 4]], ALU.is_ge, 0.0, base=0, channel_multiplier=-1), reads=[wkm], writes=[wkm2])
        p.op('pool', lambda e: e.tensor_copy(Wblk[:, :, :], wkm2[:, :, :, :].rearrange("p g b t -> p g (b t)")), reads=[wkm2], writes=[Wblk])
        bss = p.sb("bss", [128, 8, 4, 4], F32)
        p.op('dve', lambda e: e.tensor_copy(bss[:, :, :, :], bsv[:, :, 0:4].unsqueeze(2).broadcast_to([128, 8, 4, 4])), reads=[bsb], writes=[bss])
        bssv = bss.ap.rearrange("p g b t -> p g (b t)")

        xg = p.sb("xgA", [128, 8, 512], BF16)
        uT = p.sb("uT", [128, 8, 512], BF16)
        v32 = [p.sb("v32_%d" % i, [128, 1024], F32) for i in range(2)]
        vtmp = p.sb("vtmp", [128, 1024], F32)
        vnb = [p.sb("vnb%d" % i, [128, 1024], BF16) for i in range(2)]
        gT = [p.sb("gT%d" % i, [128, 8, 128], BF16) for i in range(2)]
        mix = [p.sb("mix%d" % i, [128, 512], F32) for i in range(2)]
        st = p.sb("lnst", [128, 8], F32)
        ti_ = 0
        for grp in range(5):
            if grp < 4:
                tiles = [4 * grp + i for i in range(4)]
                n = 512
            else:
                tiles = [ST]
                n = 16
            self.norm_tiles(tiles, gain, xg, [i * 128 for i in range(len(tiles))])
            for ft in range(8):
                pb = self.bank()
                self.mm(pb, pb[:, 0:n], [(Win[:, k, ft * 128:(ft + 1) * 128], xg[:, k, 0:n]) for k in range(8)], [Win, xg])
                self.act(uT[:, ft, 0:n], pb[:, 0:n], AF.Gelu_apprx_tanh, [pb], [uT])
            def ctx(i, t):
                np_ = 16 if t == ST else 128
                c0 = i * 128
                k_ = (ti_base + i) % 2
                return np_, c0, v32[k_], vnb[k_], gT[k_]

            def stV(i, t):
                np_, c0, v, vb, g_ = ctx(i, t)
                for hf in range(2):
                    pb = self.bank()
                    self.mm(pb, pb[0:np_, :], [(xg[:, k, c0:c0 + np_], Win[:, k, 1024 + hf * 512:1024 + (hf + 1) * 512]) for k in range(8)], [xg, Win])
                    self.act(v[0:np_, hf * 512:(hf + 1) * 512], pb[0:np_, :], AF.Gelu_apprx_tanh, [pb], [v, st], accum=st[0:np_, hf:hf + 1])
                P_ = slice(0, np_)
                self.act(self.junk[P_, :], v[P_, :], AF.Square, [v], [self.junk, st], accum=st[P_, 2:3])
                self.tt(st[P_, 3:4], st[P_, 0:1], st[P_, 1:2], ALU.add, [st], [st])
                self.ts(st[P_, 3:4], st[P_, 3:4], 1.0 / D, None, ALU.mult, None, [st], [st])
                self.tt(st[P_, 4:5], st[P_, 3:4], st[P_, 3:4], ALU.mult, [st], [st])
                self.stt(st[P_, 5:6], st[P_, 2:3], 1.0 / D, st[P_, 4:5], ALU.mult, ALU.subtract, [st], [st])
                self.act(st[P_, 6:7], st[P_, 5:6], AF.Sqrt, [st], [st], bias=self.eps_ln[P_], scale=1.0)
                p.op('dve', (lambda P_: (lambda e: e.reciprocal(st[P_, 7:8], st[P_, 6:7])))(P_), reads=[st], writes=[st])
                self.ts(vtmp[P_, :], v[P_, :], st[P_, 3:4], st[P_, 7:8], ALU.subtract, ALU.mult, [v, st], [vtmp])
                self.tt(vtmp[P_, :], vtmp[P_, :], lng[P_, :], ALU.mult, [vtmp, lng], [vtmp])
                if t == ST:
                    self.tt(v[P_, :], vtmp[P_, :], lnb[P_, :], ALU.add, [vtmp, lnb], [v])
                    p.dma('sp', self.dout['chunk_v'][j], v[P_, :], v, False)
                    self.cp(vb[P_, :], v[P_, :], [v], [vb], eng='dve')
                else:
                    self.tt(vb[P_, :], vtmp[P_, :], lnb[P_, :], ALU.add, [vtmp, lnb], [vb])

            def stS(i, t):
                np_, c0, v, vb, g_ = ctx(i, t)
                for gq in range(2):
                    pb = self.bank()
                    mx = mix[gq]
                    for gi in range(4):
                        g = gq * 4 + gi
                        if t == ST:
                            self.mm(pb, pb[:, gi * 16:(gi + 1) * 16], [(vb[0:16, g * 128:(g + 1) * 128], Wblk[0:16, g, :])], [vb, Wblk])
                        else:
                            self.mm(pb, pb[:, gi * 128:(gi + 1) * 128], [(vb[:, g * 128:(g + 1) * 128], WsT[:, g, :])], [vb, WsT])
                    if t == ST:
                        pv3 = pb[:, 0:64].rearrange("p (g n) -> p g n", g=4)
                        mv3 = mx[:, 0:64].rearrange("p (g n) -> p g n", g=4)
                        self.tt(mv3, pv3, bssv[:, gq * 4:gq * 4 + 4, :], ALU.add, [pb, bss], [mx])
                        self.tt(g_[:, gq * 4:gq * 4 + 4, 0:16], mv3, uT[:, gq * 4:gq * 4 + 4, 0:16], ALU.mult, [mx, uT], [g_])
                    else:
                        pv3 = pb[:, :].rearrange("p (g n) -> p g n", g=4)
                        mv3 = mx[:, :].rearrange("p (g n) -> p g n", g=4)
                        self.tt(mv3, pv3, bsv[:, gq * 4:gq * 4 + 4, :], ALU.add, [pb, bsb], [mx])
                        self.tt(g_[:, gq * 4:gq * 4 + 4, :], mv3, uT[:, gq * 4:gq * 4 + 4, c0:c0 + 128], ALU.mult, [mx, uT], [g_])

            def stO(i, t):
                np_, c0, v, vb, g_ = ctx(i, t)
                for hf in range(2):
                    pb = self.bank()
                    self.mm(pb, pb[0:np_, :], [(g_[:, g, 0:np_], Wout[:, g, hf * 512:(hf + 1) * 512]) for g in range(8)], [g_, Wout])
                    self.add_H(t, np_, hf, pb)

            ti_base = ti_
            ti_ += len(tiles)
            nt_ = len(tiles)
            for step in range(nt_ + 2):
                if step < nt_:
                    stV(step, tiles[step])
                if 0 <= step - 1 < nt_:
                    stS(step - 1, tiles[step - 1])
                if 0 <= step - 2 < nt_:
                    stO(step - 2, tiles[step - 2])
        p.release(m)

    def final(self):
        p = self.p
        m = p.mark()
        gain = self.load_bc("gO", self.din['norm_out'], D)
        yb = [p.sb("yb%d" % i, [128, 1024], F32) for i in range(2)]
        HB = self.HB
        tiles = list(range(NT))
        for t in tiles:
            np_ = 16 if t == ST else 128
            self.act(self.junk[0:np_, :], HB[t][0:np_, :], AF.Square, [HB[t]], [self.junk, self.ss], accum=self.ss[0:np_, t:t + 1])
        self.act(self.sd[:, 0:NT], self.ss[:, 0:NT], AF.Sqrt, [self.ss], [self.sd], bias=self.eps_rms, scale=1.0 / D)
        p.op('dve', lambda e: e.reciprocal(self.rstd[:, 0:NT], self.sd[:, 0:NT]), reads=[self.sd], writes=[self.rstd])
        for t in tiles:
            np_ = 16 if t == ST else 128
            y = yb[t % 2]
            self.stt(y[0:np_, :], HB[t][0:np_, :], self.rstd[0:np_, t:t + 1], gain[0:np_, :], ALU.mult, ALU.mult,
                     [HB[t], self.rstd, gain], [y])
            if t == ST:
                p.dma('sp', self.dout['y_s'], y[0:16, :], y, False)
            else:
                p.dma('sp', self.dout['y_p'][t * 128:(t + 1) * 128, :], y[:, :], y, False)
        p.release(m)


def _add_methods(cls):
    def deco(f):
        setattr(cls, f.__name__, f)
        return f
    return deco


@_add_methods(K)
def mixer_C(self, layer, j):
    p = self.p
    din = self.din
    m = p.mark()
    Wx = self.load_w("Wx", din['c_w_x'][j], 8, 1280, split=2)
    Wg = self.load_w("WgC", din['c_w_gate'][j], 8, 1280, split=2)
    Wo = self.load_w("WoC", din['c_w_out'][j], 10, 1024, split=2)
    Wa = p.sb("Wa", [128, 10, 128], BF16)
    Wi = p.sb("Wi", [128, 10, 128], BF16)
    p.dma('pool', Wa[:, :, :], din['c_w_a'][j].rearrange("n i j -> i n j"), Wa, True)
    p.dma('pool', Wi[:, :, :], din['c_w_i'][j].rearrange("n i j -> i n j"), Wi, True)
    gain = self.load_bc("gC", din['norm_mix'][layer, :], D)
    cw, cb, ba, bi, lam = self.c_cw, self.c_cb, self.c_ba, self.c_bi, self.c_lam
    ex = p.sb("lam_e", [128, 10], F32)
    cl = p.sb("cl", [128, 10], F32)
    self.act(ex[:, :], lam[:, :], AF.Exp, [lam], [ex], scale=-1.0)
    self.act(cl[:, :], ex[:, :], AF.Ln, [ex], [cl], bias=self.one_col, scale=1.0)
    self.ts(cl[:, :], cl[:, :], -8.0, None, ALU.mult, None, [cl], [cl])
    tail_p, hst_p, tail_s, hst_s = self.c_tail_p, self.c_hst_p, self.c_tail_s, self.c_hst_s
    xg = p.sb("xgC", [128, 8, 512], BF16)
    ghT = p.sb("ghT", [128, 10, 512], BF16)
    NB = 2
    T = {nm: [p.sb("%s%d" % (nm, i), [128, 520], F32) for i in range(NB)] for nm in ['xx']}
    TH = {nm: [[p.sb("%s%d_%d" % (nm, i, hf), [128, 260], F32) for hf in range(2)] for i in range(NB)] for nm in ['gate', 'xc', 'gi', 'a', 's', 'hh']}
    xcb = [p.sb("xcb%d" % i, [128, 512], BF16) for i in range(NB)]
    xg_s = p.sb("xgCs", [128, 8, 16], BF16)
    ghT_s = p.sb("ghTs", [128, 10, 16], BF16)
    T_s = {'xx': [p.sb("xxs%d" % i, [128, 32], F32) for i in range(NB)]}
    TH_s = {nm: [[p.sb("%ss%d" % (nm, i), [128, 16], F32)] for i in range(NB)] for nm in ['gate', 'xc', 'gi', 'a', 's', 'hh']}
    xcb_s = [p.sb("xcbs%d" % i, [128, 16], BF16) for i in range(NB)]
    T_p, TH_p, xcb_p, xg_p, ghT_p = T, TH, xcb, xg, ghT

    def build_ctx(grp):
        if grp < 4:
            tiles = [4 * grp + i for i in range(4)]
            n, nseq, L = 512, 1, 512
            tail, hst = tail_p, hst_p
            T, TH, xcb, xg, ghT = T_p, TH_p, xcb_p, xg_p, ghT_p
        else:
            tiles = [ST]
            n, nseq, L = 16, 4, 4
            tail, hst = tail_s, hst_s
            T, TH, xcb, xg, ghT = T_s, TH_s, xcb_s, xg_s, ghT_s
        self.norm_tiles(tiles, gain, xg, [i * 128 for i in range(len(tiles))])
        if grp < 4:
            HV = [(0, 256), (256, 256)]
        else:
            HV = [(0, 16)]

        def stA(ft, s_):
            xx = T['xx'][s_]
            gate, xc = TH['gate'][s_], TH['xc'][s_]
            xb = xcb[s_]
            px = self.bank()
            self.mm(px, px[:, 0:n], [(Wx[:, k, ft * 128:(ft + 1) * 128], xg[:, k, 0:n]) for k in range(8)], [Wx, xg])
            pg = self.bank()
            self.mm(pg, pg[:, 0:n], [(Wg[:, k, ft * 128:(ft + 1) * 128], xg[:, k, 0:n]) for k in range(8)], [Wg, xg])
            for hv, (c0, w) in enumerate(HV):
                self.act(gate[hv][:, 0:w], pg[:, c0:c0 + w], AF.Gelu_apprx_tanh, [pg], [gate[hv]])
            xx3 = xx[:, 0:nseq * (L + 3)].rearrange("p (s l) -> p s l", s=nseq)
            self.cp(xx3[:, :, 0:3], tail[:, ft, :, :], [tail], [xx], eng='dve')
            if grp < 4:
                for hv, (c0, w) in enumerate(HV):
                    self.cp(xx[:, 3 + c0:3 + c0 + w], px[:, c0:c0 + w], [px], [xx], eng='dve')
            else:
                self.cp(xx3[:, :, 3:3 + L], px[:, 0:n].rearrange("p (s l) -> p s l", s=nseq), [px], [xx], eng='dve')

            def xin(hv, k):
                c0, w = HV[hv]
                if grp < 4:
                    return xx[:, c0 + k:c0 + k + w], xc[hv][:, 0:w]
                return xx3[:, :, k:k + L], xc[hv][:, 0:n].rearrange("p (s l) -> p s l", s=nseq)
            for hv in range(len(HV)):
                i_, o_ = xin(hv, 0)
                self.ts(o_, i_, cw[0][:, ft:ft + 1], cb[:, ft:ft + 1], ALU.mult, ALU.add, [xx, cw[0], cb], [xc[hv]])
            for k in range(1, 4):
                for hv in range(len(HV)):
                    i_, o_ = xin(hv, k)
                    self.stt(o_, i_, cw[k][:, ft:ft + 1], o_, ALU.mult, ALU.add, [xx, cw[k], xc[hv]], [xc[hv]])
            self.cp(tail[:, ft, :, :], xx3[:, :, L:L + 3], [xx], [tail], eng='dve')
            for hv, (c0, w) in enumerate(HV):
                self.cp(xb[:, c0:c0 + w], xc[hv][:, 0:w], [xc[hv]], [xb], eng='act')

        def stB(ft, s_):
            gate, xc, gi, a, s, hh = [TH[nm][s_] for nm in ['gate', 'xc', 'gi', 'a', 's', 'hh']]
            xb = xcb[s_]
            pa = self.bank()
            self.mm(pa, pa[:, 0:n], [(Wa[:, ft, :], xb[:, 0:n])], [Wa, xb])
            pi_ = self.bank()
            self.mm(pi_, pi_[:, 0:n], [(Wi[:, ft, :], xb[:, 0:n])], [Wi, xb])
            for hv, (c0, w) in enumerate(HV):
                self.act(a[hv][:, 0:w], pa[:, c0:c0 + w], AF.Sigmoid, [pa, ba], [a[hv]], bias=ba[:, ft:ft + 1], scale=1.0)
            for hv, (c0, w) in enumerate(HV):
                self.act(gi[hv][:, 0:w], pi_[:, c0:c0 + w], AF.Sigmoid, [pi_, bi], [gi[hv]], bias=bi[:, ft:ft + 1], scale=1.0)
            for hv, (c0, w) in enumerate(HV):
                self.act(a[hv][:, 0:w], a[hv][:, 0:w], AF.Exp, [a[hv], cl], [a[hv]], scale=cl[:, ft:ft + 1])
            for hv, (c0, w) in enumerate(HV):
                self.tt(s[hv][:, 0:w], a[hv][:, 0:w], a[hv][:, 0:w], ALU.mult, [a[hv]], [s[hv]])
            for hv, (c0, w) in enumerate(HV):
                self.act(s[hv][:, 0:w], s[hv][:, 0:w], AF.Sqrt, [s[hv]], [s[hv]], bias=self.one_col, scale=-1.0)
            for hv, (c0, w) in enumerate(HV):
                self.tt(s[hv][:, 0:w], s[hv][:, 0:w], gi[hv][:, 0:w], ALU.mult, [s[hv], gi[hv]], [s[hv]])
            for hv, (c0, w) in enumerate(HV):
                self.tt(s[hv][:, 0:w], s[hv][:, 0:w], xc[hv][:, 0:w], ALU.mult, [s[hv], xc[hv]], [s[hv]])
            if grp < 4:
                for hv, (c0, w) in enumerate(HV):
                    ini = hst[:, ft, 0:1] if hv == 0 else hh[hv - 1][:, 255:256]
                    rd = [a[hv], s[hv], hst] if hv == 0 else [a[hv], s[hv], hh[hv - 1]]

                    def sfn(e, hv=hv, w=w, ini=ini):
                        return e.tensor_tensor_scan(hh[hv][:, 0:w], a[hv][:, 0:w], s[hv][:, 0:w], ini, ALU.mult, ALU.add)
                    p.op('dve', sfn, reads=rd, writes=[hh[hv]])
                self.cp(hst[:, ft, :], hh[1][:, 255:256], [hh[1]], [hst], eng='dve')
            else:
                for q in range(nseq):
                    def sfn(e, q=q, L=L, hst=hst, hh=hh, a=a, s=s, ft=ft):
                        return e.tensor_tensor_scan(hh[0][:, q * L:(q + 1) * L], a[0][:, q * L:(q + 1) * L], s[0][:, q * L:(q + 1) * L],
                                                    hst[:, ft, q:q + 1], ALU.mult, ALU.add)
                    p.op('dve', sfn, reads=[a[0], s[0], hst], writes=[hh[0]])
                hh3 = hh[0][:, 0:n].rearrange("p (s l) -> p s l", s=nseq)
                self.cp(hst[:, ft, :], hh3[:, :, L - 1], [hh[0]], [hst], eng='dve')
            for hv, (c0, w) in enumerate(HV):
                self.tt(ghT[:, ft, c0:c0 + w], gate[hv][:, 0:w], hh[hv][:, 0:w], ALU.mult, [gate[hv], hh[hv]], [ghT])

        def outp():
            for i, t in enumerate(tiles):
                np_ = 16 if t == ST else 128
                for hf in range(2):
                    pb = self.bank()
                    self.mm(pb, pb[0:np_, :], [(ghT[:, ft, i * 128:i * 128 + np_], Wo[:, ft, hf * 512:(hf + 1) * 512]) for ft in range(10)], [ghT, Wo])
                    self.add_H(t, np_, hf, pb)
        return stA, stB, outp

    sA, sB, sO = build_ctx(4)
    it = 0
    for grp in range(4):
        A_, B_, O_ = build_ctx(grp)
        sets = [(it + f) % NB for f in range(10)]
        it += 10
        ex = (grp == 0)
        A_(0, sets[0])
        if ex:
            sA(0, 0)
        for ft in range(1, 10):
            A_(ft, sets[ft])
            if ex:
                sA(ft, ft % NB)
            B_(ft - 1, sets[ft - 1])
            if ex:
                sB(ft - 1, (ft - 1) % NB)
        B_(9, sets[9])
        if ex:
            sB(9, 9 % NB)
        O_()
        if ex:
            sO()
    p.release(m)


@_add_methods(K)
def rope_apply(self, np_, d1, d2, x1, x2, c, s, ta, tb, rbufs, wbufs, tbufs):
    self.tt(ta, x1, c, ALU.mult, rbufs, [tbufs[0]])
    self.tt(tb, x2, s, ALU.mult, rbufs, [tbufs[1]])
    self.tt(d1, ta, tb, ALU.subtract, tbufs, wbufs)
    self.tt(ta, x1, s, ALU.mult, rbufs, [tbufs[0]])
    self.tt(tb, x2, c, ALU.mult, rbufs, [tbufs[1]])
    self.tt(d2, ta, tb, ALU.add, tbufs, wbufs)


@_add_methods(K)
def rope_tables2(self, name, np_, nt, posf):
    p = self.p
    cos = p.sb(name + "_cos", [128, nt, 32], F32)
    sin = p.sb(name + "_sin", [128, nt, 32], F32)
    m = p.mark()
    ang = p.sb(name + "_ang", [128, nt, 32], F32)
    q = p.sb(name + "_q", [128, nt, 32], F32)
    qi = p.sb(name + "_qi", [128, nt, 32], I32)
    t1 = p.sb(name + "_t1", [128, nt, 32], F32)
    r = p.sb(name + "_r", [128, nt, 32], F32)
    P = slice(0, np_)
    invb = self.invf_bc
    self.tt(ang[P, :, :], posf[P, :].unsqueeze(2).broadcast_to([np_, nt, 32]),
            invb[P, :].unsqueeze(1).broadcast_to([np_, nt, 32]), ALU.mult, [posf, invb], [ang])

    def reduce_to(dst, shift):
        self.ts(q[P], ang[P], shift, 1.0 / TWO_PI, ALU.add, ALU.mult, [ang], [q])
        p.op('dve', lambda e: e.tensor_copy(qi[P], q[P]), reads=[q], writes=[qi])
        p.op('dve', lambda e: e.tensor_copy(q[P], qi[P]), reads=[qi], writes=[q])
        self.ts(t1[P], ang[P], shift, None, ALU.add, None, [ang], [t1])
        self.stt(t1[P], q[P], -C1, t1[P], ALU.mult, ALU.add, [q, t1], [t1])
        self.stt(t1[P], q[P], -C2, t1[P], ALU.mult, ALU.add, [q, t1], [t1])
        self.ts(q[P], t1[P], math.pi, -TWO_PI, ALU.is_gt, ALU.mult, [t1], [q])
        self.tt(t1[P], t1[P], q[P], ALU.add, [t1, q], [t1])
        self.ts(q[P], t1[P], -math.pi, TWO_PI, ALU.is_lt, ALU.mult, [t1], [q])
        self.tt(t1[P], t1[P], q[P], ALU.add, [t1, q], [t1])
        self.ts(dst[P], t1[P], 3.141592, -3.141592, ALU.min, ALU.max, [t1], [dst])
    reduce_to(r, 0.0)
    self.act(sin[P], r[P], AF.Sin, [r], [sin])
    reduce_to(r, math.pi / 2)
    self.act(cos[P], r[P], AF.Sin, [r], [cos])
    p.release(m)
    return cos, sin


@_add_methods(K)
def mixer_B(self, layer, j):
    p = self.p
    din = self.din
    dout = self.dout
    PB = p.PB
    m = p.mark()
    Wuv = self.load_w("Wuv", din['b_w_uv'][j].rearrange("c h v -> c (h v)"), 2, 1024)
    Wo = self.load_w("WoB", din['b_w_out'][j], 8, 1024, split=2)
    QlT_s = p.sb("QlT_s", [128, 8, 2, 16], BF16)
    QpeT_s = p.sb("QpeT_s", [128, 4, 16], BF16)
    Qpad_s = p.sb("Qpad_s", [128, 8, 16], BF16)
    KTs = p.sb("KTs", [128, 3, 16], BF16)
    Vs = p.sb("Vs", [16, 1, 256], BF16)
    ovT_s = p.sb("ovT_s", [128, 8, 16], BF16)
    stats = [p.sb("stat%d" % i, [128, 16], F32) for i in range(4)]
    mask4 = p.sb("mask4", [32, 4, 16], F32)
    mtmp = p.sb("mtmp", [32, 16], F32)
    zer32 = p.sb("zer32", [32, 16], F32)
    p.op('pool', lambda e: e.memset(zer32[:, :], 0.0), writes=[zer32])
    for b in range(4):
        p.op('pool', (lambda b: (lambda e: e.affine_select(mtmp[:, :], zer32[:, :], [[0, 4], [-8, 4]], ALU.is_ge, NEG, base=0, channel_multiplier=1)))(b), reads=[zer32], writes=[mtmp])
        p.op('pool', (lambda b: (lambda e: e.affine_select(mask4[:, b, :], mtmp[:, :], [[1, 4], [0, 4]], ALU.is_equal, NEG, base=-b, channel_multiplier=0)))(b), reads=[mtmp], writes=[mask4])
    ptT = p.sb("ptT", [128, 4], I32)

    def ptfn(e):
        return e.dma_start(out=ptT[:, :], in_=din['pt'].rearrange("b j -> j b"), allow_slow_non_contiguous=True)
    p.dma_custom('sp', ptfn, ptT, True)
    ptf = p.sb("ptf", [128, 4], F32)
    si = p.sb("si", [128, 8], I32)
    sf = p.sb("sf", [128, 8], F32)
    idxf = p.sb("idxf", [128, 4, 8], F32)
    idx = p.sb("idx", [128, 4, 8], I32)
    p.op('dve', lambda e: e.tensor_copy(ptf[:, :], ptT[:, :]), reads=[ptT], writes=[ptf])
    self.ts(ptf[:, :], ptf[:, :], 8.0, None, ALU.mult, None, [ptf], [ptf])
    p.op('pool', lambda e: e.iota(si[:, :], [[1, 8]], base=0, channel_multiplier=0), writes=[si])
    p.op('dve', lambda e: e.tensor_copy(sf[:, :], si[:, :]), reads=[si], writes=[sf])
    self.tt(idxf[:, :, :], ptf[:, :].unsqueeze(2).broadcast_to([128, 4, 8]), sf[:, :].unsqueeze(1).broadcast_to([128, 4, 8]), ALU.add, [ptf, sf], [idxf])
    p.op('dve', lambda e: e.tensor_copy(idx[:, :, :], idxf[:, :, :]), reads=[idxf], writes=[idx])

    mP = p.mark()
    Wdq = self.load_w("Wdq", din['b_w_dq'][j], 8, 384)
    Wuq = self.load_w("Wuq", din['b_w_uq'][j], 3, 1536)
    Wdkv = self.load_w("Wdkv", din['b_w_dkv'][j], 8, 320)
    WukT = p.sb("WukT", [128, 8, 256], BF16)
    gain = self.load_bc("gB", din['norm_mix'][layer, :], D)
    kvg = self.load_bc("kvg", din['b_kv_norm'][j, :], 256)
    qg = self.load_col("qg", din['b_q_norm'][j, :], 3)
    self.invf_bc = self.load_bc("invf", din['invf'], 32)
    posi = p.sb("posi", [128, 16], I32)
    posf = p.sb("posf", [128, 16], F32)
    p.op('pool', lambda e: e.iota(posi[:, :], [[128, 16]], base=0, channel_multiplier=1), writes=[posi])
    p.op('dve', lambda e: e.tensor_copy(posf[:, :], posi[:, :]), reads=[posi], writes=[posf])
    cos_p, sin_p = self.rope_tables2("rp", 128, 16, posf)
    prow_i = p.sb("prow_i", [1, 16], I32)
    prow_f = p.sb("prow_f", [1, 16], F32)
    posf_s = p.sb("posf_s", [128, 1], F32)
    p.op('pool', lambda e: e.iota(prow_i[:, :], [[0, 4], [1, 4]], base=NPAGES * 128, channel_multiplier=0), writes=[prow_i])
    p.op('dve', lambda e: e.tensor_copy(prow_f[:, :], prow_i[:, :]), reads=[prow_i], writes=[prow_f])
    pbp = self.bank()
    self.mm(pbp, pbp[0:16, 0:1], [(prow_f[0:1, 0:16], self.identf[0:1, 0:1])], [prow_f, self.identf])
    self.cp(posf_s[0:16, :], pbp[0:16, 0:1], [pbp], [posf_s], eng='dve')
    cos_s, sin_s = self.rope_tables2("rs", 16, 1, posf_s)
    mW = p.mark()
    Wukn = self.load_w("Wukn", din['b_w_uk'][j].rearrange("c h n -> c (h n)"), 2, 1024)
    for cc in range(2):
        pb = self.bank()
        pv = pb.ap.bitcast(BF16).rearrange("p (k n) -> p k n", k=8)
        for h in range(8):
            p.tr(pb, pv[:, h, :], Wukn[:, cc, h * 128:(h + 1) * 128], self.identb[:, :], [Wukn, self.identb])
        self.cp(WukT[:, :, cc * 128:(cc + 1) * 128], pv[:, :, :], [pb], [WukT])
    p.release(mW)
    KT = p.sb("KT", [128, 3, 2048], BF16)
    V = p.sb("V", [128, 16, 256], BF16)
    xg = p.sb("xgB", [128, 8, 512], BF16)
    QlT = p.sb("QlT", [128, 8, 2, 512], BF16)
    QpeT = p.sb("QpeT", [128, 4, 512], BF16)
    ovTs = [p.sb("ovT%d" % i, [128, 8, 128], BF16) for i in range(2)]
    sti = [0]

    def newstat():
        s = stats[sti[0] % 4]
        sti[0] += 1
        return s

    for grp in [4, 0, 1, 2, 3]:
        if grp < 4:
            tiles = [4 * grp + i for i in range(4)]
            n = 512
            QlT_g, QpeT_g = QlT, QpeT
            cosb, sinb = cos_p, sin_p
        else:
            tiles = [ST]
            n = 16
            QlT_g, QpeT_g = QlT_s, QpeT_s
            cosb, sinb = cos_s, sin_s
        self.bank_rr = list(range(8))
        self.norm_tiles(tiles, gain, xg, [i * 128 for i in range(len(tiles))])
        m2 = p.mark()
        cqn = p.sb("cqn", [128, 384], BF16)
        cqT = p.sb("cqT", [128, 3, 512], BF16)
        qn = [p.sb("qn%d" % i, [128, 512], BF16) for i in range(2)]
        qpe = p.sb("qpe", [128, 8, 64], BF16)
        kvrow = [p.sb("kvrow%d" % i, [128, 320], F32) for i in range(2)]
        kpe2 = p.sb("kpe2", [128, 2, 64], BF16)
        ra = p.sb("ra", [128, 8, 32], F32)
        rb = p.sb("rb", [128, 8, 32], F32)
        for i, t in enumerate(tiles):
            np_ = 16 if t == ST else 128
            P_ = slice(0, np_)
            c0 = i * 128
            tt_ = 0 if t == ST else t
            st = newstat()
            pq = self.bank()
            self.mm(pq, pq[P_, 0:384], [(xg[:, k, c0:c0 + np_], Wdq[:, k, :]) for k in range(8)], [xg, Wdq])
            self.act(self.junk[P_, 0:384], pq[P_, 0:384], AF.Square, [pq], [self.junk, st], accum=st[P_, 0:1])
            self.act(st[P_, 1:2], st[P_, 0:1], AF.Sqrt, [st], [st], bias=self.eps_rms[P_], scale=1.0 / 384)
            p.op('dve', (lambda st, P_: (lambda e: e.reciprocal(st[P_, 2:3], st[P_, 1:2])))(st, P_), reads=[st], writes=[st])
            self.ts(cqn[P_, :], pq[P_, 0:384], st[P_, 2:3], None, ALU.mult, None, [pq, st], [cqn])
            pb = self.bank()
            pv = pb.ap.bitcast(BF16).rearrange("p (k n) -> p k n", k=8)
            for kc in range(3):
                p.tr(pb, pv[:, kc, 0:np_], cqn[P_, kc * 128:(kc + 1) * 128], self.identb[P_, 0:np_], [cqn, self.identb])
            self.tt(cqT[:, :, c0:c0 + np_], pv[:, 0:3, 0:np_], qg[:, 0:3].unsqueeze(2).broadcast_to([128, 3, np_]), ALU.mult, [pb, qg], [cqT])
            pp = self.bank()
            ppv = pp[P_, :].rearrange("p (h c) -> p h c", c=64)
            self.mm(pp, ppv, [(cqT[:, kc, c0:c0 + np_], Wuq[:, kc, :].rearrange("p (h c) -> p h c", c=192)[:, :, 128:192]) for kc in range(3)], [cqT, Wuq])
            cb_ = cosb[P_, tt_, :].unsqueeze(1).broadcast_to([np_, 8, 32])
            sb_ = sinb[P_, tt_, :].unsqueeze(1).broadcast_to([np_, 8, 32])
            self.rope_apply(np_, qpe[P_, :, 0:32], qpe[P_, :, 32:64], ppv[:, :, 0:32], ppv[:, :, 32:64], cb_, sb_,
                            ra[P_], rb[P_], [pp, cosb, sinb], [qpe], [ra, rb])
            pb = self.bank()
            pv = pb.ap.bitcast(BF16).rearrange("p (k n) -> p k n", k=8)
            for pr in range(4):
                p.tr(pb, pv[:, pr, 0:np_], qpe[P_, 2 * pr:2 * pr + 2, :].rearrange("p h c -> p (h c)"), self.identb[P_, 0:np_], [qpe, self.identb])
            self.cp(QpeT_g[:, :, c0:c0 + np_], pv[:, 0:4, 0:np_], [pb], [QpeT_g])
            kvr = kvrow[i % 2]
            pk = self.bank()
            self.mm(pk, pk[P_, 0:320], [(xg[:, k, c0:c0 + np_], Wdkv[:, k, :]) for k in range(8)], [xg, Wdkv])
            self.act(self.junk[P_, 0:256], pk[P_, 0:256], AF.Square, [pk], [self.junk, st], accum=st[P_, 3:4])
            self.act(st[P_, 4:5], st[P_, 3:4], AF.Sqrt, [st], [st], bias=self.eps_rms[P_], scale=1.0 / 256)
            p.op('dve', (lambda st, P_: (lambda e: e.reciprocal(st[P_, 5:6], st[P_, 4:5])))(st, P_), reads=[st], writes=[st])
            self.stt(kvr[P_, 0:256], pk[P_, 0:256], st[P_, 5:6], kvg[P_, :], ALU.mult, ALU.mult, [pk, st, kvg], [kvr])
            self.rope_apply(np_, kvr[P_, 256:288], kvr[P_, 288:320], pk[P_, 256:288], pk[P_, 288:320],
                            cosb[P_, tt_, :], sinb[P_, tt_, :], ra[P_, 0, :], rb[P_, 0, :], [pk, cosb, sinb], [kvr], [ra, rb])
            if t == ST:
                p.dma('sp', dout['mla_s'], kvr[P_, :], kvr, False)
                Vdst = Vs[0:16, 0, :]
                Vb = Vs
            else:
                p.dma('sp', dout['mla_p'][t * 128:(t + 1) * 128, :], kvr[:, :], kvr, False)
                Vdst = V[:, t, :]
                Vb = V
            self.cp(Vdst, kvr[P_, 0:256], [kvr], [Vb])
            self.cp(kpe2[P_, :, :], kvr[P_, 256:320].unsqueeze(1).broadcast_to([np_, 2, 64]), [kvr], [kpe2], eng='dve')
            pb = self.bank()
            pv = pb.ap.bitcast(BF16).rearrange("p (k n) -> p k n", k=8)
            for cc in range(2):
                p.tr(pb, pv[:, cc, 0:np_], Vdst[:, cc * 128:(cc + 1) * 128], self.identb[P_, 0:np_], [Vb, self.identb])
            p.tr(pb, pv[:, 2, 0:np_], kpe2[P_, :, :].rearrange("p a c -> p (a c)"), self.identb[P_, 0:np_], [kpe2, self.identb])
            if t == ST:
                self.cp(KTs[:, :, 0:16], pv[:, 0:3, 0:16], [pb], [KTs])
            else:
                self.cp(KT[:, :, t * 128:(t + 1) * 128], pv[:, 0:3, :], [pb], [KT])
        for h in range(8):
            pn = self.bank()
            self.mm(pn, pn[:, 0:n], [(Wuq[:, kc, h * 192:h * 192 + 128], cqT[:, kc, 0:n]) for kc in range(3)], [Wuq, cqT])
            q_ = qn[h % 2]
            self.cp(q_[:, 0:n], pn[:, 0:n], [pn], [q_])
            for cc in range(2):
                pl = self.bank()
                self.mm(pl, pl[:, 0:n], [(WukT[:, h, cc * 128:(cc + 1) * 128], q_[:, 0:n])], [WukT, q_])
                self.cp(QlT_g[:, h, cc, 0:n], pl[:, 0:n], [pl], [QlT_g])
        p.release(m2)
        if grp == 4:
            p.op('pool', lambda e: e.memset(Qpad_s[:, :, :], 0.0), writes=[Qpad_s])
            self.cp(Qpad_s[0:64, 0::2, :], QpeT_s[0:64, :, :], [QpeT_s], [Qpad_s], eng='dve')
            self.cp(Qpad_s[64:128, 1::2, :], QpeT_s[64:128, :, :], [QpeT_s], [Qpad_s], eng='dve')
            continue
        m3 = p.mark()
        Pbs = [p.sb("Pb%d" % i, [128, 2048], BF16) for i in range(2)]
        PTss = [p.sb("PTs%d" % i, [128, 16, 128], BF16) for i in range(2)]
        OTs = [p.sb("OTs%d" % i, [128, 2, 128], BF16) for i in range(2)]
        steps = []
        for i, t in enumerate(tiles):
            G_ = 4 if t < 4 else (2 if t < 8 else 1)
            for h0 in range(0, 8, G_):
                steps.append((i, t, list(range(h0, h0 + G_))))

        def stage1(k):
            i, t, hs = steps[k]
            R = k % 2
            base = 4 * R
            G = len(hs)
            nbh = 4 // G
            qc = i * 128
            nk = (t + 1) * 128
            nb = (nk + 511) // 512
            sbufs = []
            for hi, h in enumerate(hs):
                hp = h % 2
                for kb in range(nb):
                    n_ = min(512, nk - kb * 512)
                    bk = PB[base + hi * nbh + kb]
                    sbufs.append(bk)
                    last = (kb == nb - 1)
                    tr_ = [(bk[:, 0:n_], QlT[:, h, 0, qc:qc + 128], KT[:, 0, kb * 512:kb * 512 + n_], True, False),
                           (bk[:, 0:n_], QlT[:, h, 1, qc:qc + 128], KT[:, 1, kb * 512:kb * 512 + n_], False, False),
                           (bk[:, 0:n_], QpeT[hp * 64:(hp + 1) * 64, h // 2, qc:qc + 128], KT[hp * 64:(hp + 1) * 64, 2, kb * 512:kb * 512 + n_], False, not last)]
                    if last:
                        tr_.append((bk[:, n_ - 128:n_], self.identb[:, :], self.maskb[:, :], False, True))
                    self.mm_multi(bk, tr_, [QlT, QpeT, KT, self.identb, self.maskb])
            S3 = p.ps_all[:, base * 512:(base + 4) * 512].rearrange("p (g n) -> p g n", g=G)[:, :, 0:nk]
            Pb3 = Pbs[R].ap.rearrange("p (g n) -> p g n", g=G)[:, :, 0:nk]
            return sbufs, S3, nk, G, Pb3, Pbs[R]

        def stage1ew(info):
            sbufs, S3, nk, G, Pb3, Pb = info
            st = newstat()
            p.op('dve', (lambda st, S3, G: (lambda e: e.reduce_max(st[:, 0:G], S3, AX.X)))(st, S3, G), reads=sbufs, writes=[st])
            self.ts(st[:, 4:4 + G], st[:, 0:G], -MLA_SCALE, None, ALU.mult, None, [st], [st])
            for hi in range(G):
                self.act(Pb3[:, hi, :], S3[:, hi, :], AF.Exp, sbufs + [st], [Pb, st], bias=st[:, 4 + hi:5 + hi], scale=MLA_SCALE, accum=st[:, 8 + hi:9 + hi])
            p.op('dve', (lambda st, G: (lambda e: e.reciprocal(st[:, 12:12 + G], st[:, 8:8 + G])))(st, G), reads=[st], writes=[st])
            if G == 1:
                self.ts(Pb3[:, 0, :], Pb3[:, 0, :], st[:, 12:13], None, ALU.mult, None, [Pb, st], [Pb])
            else:
                self.tt(Pb3, Pb3, st[:, 12:12 + G].unsqueeze(2).broadcast_to([128, G, nk]), ALU.mult, [Pb, st], [Pb])

        def stage2(k):
            i, t, hs = steps[k]
            R = k % 2
            base = 4 * R
            G = len(hs)
            nk = (t + 1) * 128
            Pb = Pbs[R]
            Pb3 = Pb.ap.rearrange("p (g n) -> p g n", g=G)
            PTs = PTss[R]
            nkt = t + 1
            for hi, h in enumerate(hs):
                for k0 in range(0, nkt, 8):
                    k1 = min(nkt, k0 + 8)
                    pb = PB[base + k0 // 8]
                    pv = pb.ap.bitcast(BF16).rearrange("p (k n) -> p k n", k=8)
                    for kt in range(k0, k1):
                        p.tr(pb, pv[:, kt - k0, :], Pb3[:, hi, kt * 128:(kt + 1) * 128], self.identb[:, :], [Pb, self.identb])
                    self.cp(PTs[:, k0:k1, :], pv[:, 0:k1 - k0, :], [pb], [PTs])
                po = PB[base + 2]
                for cc in range(2):
                    self.mm(po, po[:, cc * 128:(cc + 1) * 128], [(V[:, kt, cc * 128:(cc + 1) * 128], PTs[:, kt, :]) for kt in range(nkt)], [V, PTs])
                ot = OTs[h % 2]
                self.cp(ot[:, :, :], po[:, 0:256].rearrange("p (c n) -> p c n", c=2), [po], [ot])
                pv_ = PB[base + 3]
                self.mm(pv_, pv_[:, 0:128], [(Wuv[:, cc, h * 128:(h + 1) * 128], ot[:, cc, :]) for cc in range(2)], [Wuv, ot])
                ovb = ovTs[t % 2]
                self.cp(ovb[:, h, :], pv_[:, 0:128], [pv_], [ovb])
                if h == 7:
                    for hf in range(2):
                        pb = PB[base + 2 + hf]
                        self.mm(pb, pb[:, :], [(ovb[:, hh_, :], Wo[:, hh_, hf * 512:(hf + 1) * 512]) for hh_ in range(8)], [ovb, Wo])
                        self.add_H(t, 128, hf, pb)

        stage1ew(stage1(0))
        for k in range(1, len(steps)):
            inf = stage1(k)
            stage2(k - 1)
            stage1ew(inf)
        stage2(len(steps) - 1)
        p.release(m3)
    p.release(mP)
    self.bank_rr = [6, 7]
    gbuf = [p.sb("gbuf%d" % i, [128, 16, 320], F32) for i in range(2)]
    kb16 = [p.sb("kb16_%d" % i, [128, 16, 384], BF16) for i in range(2)]
    KTsegs = [p.sb("KTseg%d" % i, [128, 3, 2048], BF16) for i in range(2)]
    Ps = p.sb("Ps", [32, 2048], BF16)
    PTq = p.sb("PTq", [128, 16, 32], BF16)
    Oacc = p.sb("Oacc", [32, 256], F32)
    Onb = p.sb("Onb", [32, 256], BF16)
    OTq = p.sb("OTq", [128, 2, 32], BF16)
    sn = p.sb("sn", [32, 16], F32)
    run = p.sb("run", [32, 4], F32)
    cache = din['cache']
    gi_ = 0
    kbank = [4, 5]
    kbi = 0
    R32 = slice(0, 32)

    def flash(S_ap, sbufs, n):
        st = newstat()
        p.op('dve', (lambda st: (lambda e: e.reduce_max(st[R32, 0:1], S_ap, AX.X)))(st), reads=sbufs, writes=[st])
        self.tt(st[R32, 1:2], run[:, 0:1], st[R32, 0:1], ALU.max, [run, st], [st])
        self.tt(st[R32, 2:3], run[:, 0:1], st[R32, 1:2], ALU.subtract, [run, st], [st])
        self.act(st[R32, 3:4], st[R32, 2:3], AF.Exp, [st], [st], scale=MLA_SCALE)
        self.ts(st[R32, 4:5], st[R32, 1:2], -MLA_SCALE, None, ALU.mult, None, [st], [st])
        self.act(Ps[:, 0:n], S_ap, AF.Exp, sbufs + [st], [Ps, st], bias=st[R32, 4:5], scale=MLA_SCALE, accum=st[R32, 5:6])
        self.stt(run[:, 1:2], run[:, 1:2], st[R32, 3:4], st[R32, 5:6], ALU.mult, ALU.add, [run, st], [run])
        self.cp(run[:, 0:1], st[R32, 1:2], [st], [run], eng='dve')
        return st

    Qs = p.sb("Qs", [128, 3, 4, 32], BF16)
    for c in range(2):
        self.cp(Qs[:, c, :, :].rearrange("p b (t h) -> p b t h", h=8), QlT_s[:, :, c, :].rearrange("p h (b t) -> p b t h", b=4), [QlT_s], [Qs], eng='dve')
    self.cp(Qs[:, 2, :, :].rearrange("p b (t h) -> p b t h", h=8), Qpad_s[:, :, :].rearrange("p h (b t) -> p b t h", b=4), [Qpad_s], [Qs], eng='dve')
    segs = [(b, s_) for b in range(4) for s_ in range(8)]

    def prep(k):
        b, s_ = segs[k]
        g = gbuf[k % 2]
        kb_ = kb16[k % 2]
        KTg = KTsegs[k % 2]

        def gfn(e, g=g, b=b, s_=s_):
            return e.indirect_dma_start(out=g[:, :, :].rearrange("p t c -> p (t c)"), out_offset=None, in_=cache[:, :],
                                        in_offset=bass.IndirectOffsetOnAxis(ap=idx[:, b, s_:s_ + 1], axis=0))
        p.dma_custom('pool', gfn, g, True, reads=[idx])
        self.cp(kb_[:, 0:8, 0:320], g[:, 0:8, :], [g], [kb_], eng='dve')
        self.cp(kb_[:, 8:16, 0:320], g[:, 8:16, :], [g], [kb_], eng='act')
        self.cp(kb_[:, :, 320:384], g[:, :, 256:320], [g], [kb_], eng='dve')

    def prepT(k):
        kb_ = kb16[k % 2]
        KTg = KTsegs[k % 2]
        for ch in range(3):
            for k0 in (0, 8):
                pb = PB[kbank[self.kbi % 2]]
                self.kbi += 1
                pv = pb.ap.bitcast(BF16).rearrange("p (k n) -> p k n", k=8)
                for kt in range(k0, k0 + 8):
                    p.tr(pb, pv[:, kt - k0, :], kb_[:, kt, ch * 128:(ch + 1) * 128], self.identb[:, :], [kb_, self.identb])
                self.cp(KTg[:, ch, k0 * 128:(k0 + 8) * 128], pv[:, :, :].rearrange("p k n -> p (k n)"), [pb], [KTg])

    def sc(k):
        b, s_ = segs[k]
        KTg = KTsegs[k % 2]
        if s_ == 0:
            p.op('pool', lambda e: e.memset(Oacc[:, :], 0.0), writes=[Oacc])
            p.op('pool', lambda e: e.memset(run[:, 0:1], -1e30), writes=[run])
            p.op('pool', lambda e: e.memset(run[:, 1:2], 0.0), writes=[run])
        Qc = [Qs[:, c, b, :] for c in range(3)]
        sbufs = []
        for kq in range(4):
            bk = PB[kq]
            sbufs.append(bk)
            self.mm(bk, bk[R32, 0:512], [(Qc[c], KTg[:, c, kq * 512:(kq + 1) * 512]) for c in range(3)], [Qs, KTg])
        return flash(p.ps_all[R32, 0:2048], sbufs, 2048)

    def pvs(k, st):
        b, s_ = segs[k]
        kb_ = kb16[k % 2]
        pb = self.bank()
        pv = pb.ap.bitcast(BF16)[:, 0:512].rearrange("p (k n) -> p k n", k=16)
        for kt in range(16):
            p.tr(pb, pv[:, kt, :], Ps[:, kt * 128:(kt + 1) * 128], self.identb[R32, 0:32], [Ps, self.identb])
        self.cp(PTq[:, :, :], pv[:, :, :], [pb], [PTq])
        po = self.bank()
        self.mm(po, po[R32, 0:256], [(PTq[:, kt, :], kb_[:, kt, 0:256]) for kt in range(16)], [PTq, kb_])
        self.stt(Oacc[:, :], Oacc[:, :], st[R32, 3:4], po[R32, 0:256], ALU.mult, ALU.add, [Oacc, st, po], [Oacc])

    def finish_b(b):
        Qc = [Qs[:, c, b, :] for c in range(3)]
        pbn = self.bank()
        self.mm(pbn, pbn[R32, 0:16], [(Qc[c], KTs[:, c, 0:16]) for c in range(3)], [Qs, KTs])
        self.tt(sn[:, :], pbn[R32, 0:16], mask4[:, b, :], ALU.add, [pbn, mask4], [sn])
        st = flash(sn[:, :], [sn], 16)
        pb = self.bank()
        pvn = pb.ap.bitcast(BF16)
        p.tr(pb, pvn[0:16, 0:32], Ps[:, 0:16], self.identb[R32, 0:32], [Ps, self.identb])
        self.cp(PTq[0:16, 0, :], pvn[0:16, 0:32], [pb], [PTq])
        po = self.bank()
        self.mm(po, po[R32, 0:256], [(PTq[0:16, 0, :], Vs[0:16, 0, :])], [PTq, Vs])
        self.stt(Oacc[:, :], Oacc[:, :], st[R32, 3:4], po[R32, 0:256], ALU.mult, ALU.add, [Oacc, st, po], [Oacc])
        st = newstat()
        p.op('dve', (lambda st: (lambda e: e.reciprocal(st[R32, 0:1], run[:, 1:2])))(st), reads=[run], writes=[st])
        self.ts(Onb[:, :], Oacc[:, :], st[R32, 0:1], None, ALU.mult, None, [Oacc, st], [Onb])
        pb = self.bank()
        pv = pb.ap.bitcast(BF16)[:, 0:64].rearrange("p (c n) -> p c n", c=2)
        for cc in range(2):
            p.tr(pb, pv[:, cc, :], Onb[:, cc * 128:(cc + 1) * 128], self.identb[R32, 0:32], [Onb, self.identb])
        self.cp(OTq[:, :, :], pv[:, :, :], [pb], [OTq])
        pvv = self.bank()
        for h in range(8):
            self.mm(pvv, pvv[:, h * 4:(h + 1) * 4], [(Wuv[:, cc, h * 128:(h + 1) * 128], OTq[:, cc, h::8]) for cc in range(2)], [Wuv, OTq])
        self.cp(ovT_s[:, :, b * 4:(b + 1) * 4], pvv[:, 0:32].rearrange("p (h t) -> p h t", h=8), [pvv], [ovT_s], eng='dve')

    self.kbi = 0
    prep(0)
    prepT(0)
    prep(1)
    for k in range(len(segs)):
        st = sc(k)
        if k + 1 < len(segs):
            prepT(k + 1)
        pvs(k, st)
        if segs[k][1] == 7:
            finish_b(segs[k][0])
        if k + 2 < len(segs):
            prep(k + 2)
    self.bank_rr = list(range(8))
    for hf in range(2):
        pb = self.bank()
        self.mm(pb, pb[0:16, :], [(ovT_s[:, h, 0:16], Wo[:, h, hf * 512:(hf + 1) * 512]) for h in range(8)], [ovT_s, Wo])
        self.add_H(ST, 16, hf, pb)
    p.release(m)


@_add_methods(K)
def c_setup(self):
    p = self.p
    din = self.din
    self.c_cw = [self.load_col("cw%d" % k, din['c_conv_w'][0, k, :], 10) for k in range(4)]
    self.c_cb = self.load_col("cb", din['c_conv_b'][0, :], 10)
    self.c_ba = self.load_col("ba", din['c_b_a'][0, :], 10)
    self.c_bi = self.load_col("bi", din['c_b_i'][0, :], 10)
    self.c_lam = self.load_col("lam", din['c_lambda'][0, :], 10)
    tail_p = self.c_tail_p = p.sb("tail_p", [128, 10, 1, 3], F32)
    hst_p = self.c_hst_p = p.sb("hst_p", [128, 10, 1], F32)
    tail_s = self.c_tail_s = p.sb("tail_s", [128, 10, 4, 3], F32)
    hst_s = self.c_hst_s = p.sb("hst_s", [128, 10, 4], F32)
    p.op('pool', lambda e: e.memset(tail_p[:, :, :, :], 0.0), writes=[tail_p])
    p.op('pool', lambda e: e.memset(hst_p[:, :, :], 0.0), writes=[hst_p])
    for b in range(4):
        for k in range(3):
            def fn(e, b=b, k=k):
                return e.dma_start(out=tail_s[:, :, b, k], in_=din['st_conv'][b, k, :].rearrange("(t p) -> p t", p=128), allow_slow_non_contiguous=True)
            p.dma_custom('sp', fn, tail_s, True)

        def fn2(e, b=b):
            return e.dma_start(out=hst_s[:, :, b], in_=din['st_h'][b, :].rearrange("(t p) -> p t", p=128), allow_slow_non_contiguous=True)
        p.dma_custom('sp', fn2, hst_s, True)


@_add_methods(K)
def c_state_out(self):
    p = self.p
    dout = self.dout
    tail_p, hst_p, tail_s, hst_s = self.c_tail_p, self.c_hst_p, self.c_tail_s, self.c_hst_s

    def st_out(dst, src, buf):
        def fn(e):
            return e.dma_start(out=dst.rearrange("(t p) -> p t", p=128), in_=src, allow_slow_non_contiguous=True)
        p.dma_custom('sp', fn, buf, False)
    st_out(dout['lru_h_p'], hst_p[:, :, 0], hst_p)
    for k in range(3):
        st_out(dout['conv_p'][k, :], tail_p[:, :, 0, k], tail_p)
    for b in range(4):
        st_out(dout['lru_h_s'][b, :], hst_s[:, :, b], hst_s)
        for k in range(3):
            st_out(dout['conv_s'][b, k, :], tail_s[:, :, b, k], tail_s)

DEPTH_RUN = 4
NC_RUN = 8
_CACHE = {}

W_SHAPES = {
    'norm_mix': (4, 1024), 'norm_ffn': (4, 1024), 'norm_out': (1024,),
    'a_w_in': (2, 1024, 2048), 'a_ln_g': (2, 1024), 'a_ln_b': (2, 1024), 'a_w_s': (2, 8, 128, 128),
    'a_b_s': (2, 8, 128), 'a_w_out': (2, 1024, 1024),
    'b_w_dq': (1, 1024, 384), 'b_q_norm': (1, 384), 'b_w_uq': (1, 384, 1536), 'b_w_dkv': (1, 1024, 320),
    'b_kv_norm': (1, 256), 'b_w_uk': (1, 256, 8, 128), 'b_w_uv': (1, 256, 8, 128), 'b_w_out': (1, 1024, 1024),
    'c_w_x': (1, 1024, 1280), 'c_w_gate': (1, 1024, 1280), 'c_conv_w': (1, 4, 1280), 'c_conv_b': (1, 1280),
    'c_w_a': (1, 10, 128, 128), 'c_b_a': (1, 1280), 'c_w_i': (1, 10, 128, 128), 'c_b_i': (1, 1280),
    'c_lambda': (1, 1280), 'c_w_out': (1, 1280, 1024),
    'f_w_gate': (4, 1024, 2816), 'f_w_up': (4, 1024, 2816), 'f_w_down': (4, 2816, 1024),
}


def build(depth):
    k = K(depth)
    p = k.p
    k.inp('xp', [2048, 1024])
    k.inp('xs', [16, 1024])
    k.inp('cache', [NPOOL * 8, TSEG * 320])
    k.inp('pt', [4, 128], I32)
    k.inp('st_h', [4, 1280])
    k.inp('st_conv', [4, 3, 1280])
    k.inp('invf', [32])
    for n, s in W_SHAPES.items():
        k.inp(n, s)
    k.outp('y_p', [2048, 1024])
    k.outp('y_s', [16, 1024])
    k.outp('chunk_v', [2, 16, 1024])
    k.outp('mla_p', [2048, 320])
    k.outp('mla_s', [16, 320])
    k.outp('lru_h_p', [1280])
    k.outp('lru_h_s', [4, 1280])
    k.outp('conv_p', [3, 1280])
    k.outp('conv_s', [4, 3, 1280])
    k.setup_consts()
    k.HB = [p.sb("H%d" % t, [128, 1024], F32) for t in range(NT)]
    k.xn_bufs = [p.sb("xn%d" % i, [128, 1024], BF16) for i in range(2)]
    k.xn_i = 0
    for t in range(16):
        p.dma('sp', k.HB[t][:, :], k.din['xp'][t * 128:(t + 1) * 128, :], k.HB[t], True)
    p.dma('sp', k.HB[ST][0:16, :], k.din['xs'], k.HB[ST], True)
    k.mid_hook = None
    if depth >= 3:
        k.c_setup()
    for layer in range(depth):
        kind, j = layer % 3, layer // 3
        if kind == 0:
            k.mixer_A(layer, j)
        elif kind == 1:
            k.mixer_B(layer, j)
        else:
            k.mixer_C(layer, j)
            k.mid_hook = k.c_state_out
        k.ffn(layer)
    k.final()
    p.finish()
    p.emit()
    return k


def kernel(**inputs):
    depth = DEPTH_RUN
    if depth not in _CACHE:
        _CACHE[depth] = build(depth)
    k = _CACHE[depth]
    f32 = np.float32
    xp = np.ascontiguousarray(np.asarray(inputs['x_prompt'], dtype=f32))
    xs = np.ascontiguousarray(np.asarray(inputs['x_sample'], dtype=f32))
    cache = np.ascontiguousarray(np.asarray(inputs['cache_mla'], dtype=f32)).reshape(NPOOL * 8, TSEG * 320)
    pt = np.ascontiguousarray(np.asarray(inputs['page_table']).astype(np.int32))
    sth = np.ascontiguousarray(np.asarray(inputs['state_lru_h'], dtype=f32))
    stc = np.ascontiguousarray(np.asarray(inputs['state_lru_conv'], dtype=f32))
    invf = (1.0 / (np.float32(10000.0) ** (np.arange(0, 64, 2, dtype=f32) / np.float32(64)))).astype(f32)
    ws = {n: np.ascontiguousarray(np.asarray(inputs[n], dtype=f32)) for n in W_SHAPES}
    in_maps = []
    for c in range(NC_RUN):
        d = dict(ws)
        d['xp'] = xp[c]
        d['xs'] = xs[4 * c:4 * c + 4].reshape(16, 1024)
        d['cache'] = cache
        d['pt'] = pt[4 * c:4 * c + 4]
        d['st_h'] = sth[0, 4 * c:4 * c + 4]
        d['st_conv'] = stc[0, 4 * c:4 * c + 4]
        d['invf'] = invf
        in_maps.append(d)
    res = run_bass_kernel_spmd(k.nc, in_maps, core_ids=list(range(NC_RUN)))
    R = list(res.results)
    while len(R) < 8:
        R.append(R[0])
    y_p = np.stack([R[c]['y_p'] for c in range(8)])
    y_s = np.concatenate([R[c]['y_s'].reshape(4, 4, 1024) for c in range(8)])
    chunk_v = np.concatenate([R[c]['chunk_v'].reshape(2, 4, 4, 1024) for c in range(8)], axis=1)
    mla_p = np.stack([R[c]['mla_p'] for c in range(8)])[None]
    mla_s = np.concatenate([R[c]['mla_s'].reshape(4, 4, 320) for c in range(8)])[None]
    lru_h_p = np.stack([R[c]['lru_h_p'] for c in range(8)])[None]
    lru_h_s = np.concatenate([R[c]['lru_h_s'] for c in range(8)])[None]
    conv_p = np.stack([R[c]['conv_p'] for c in range(8)])[None]
    conv_s = np.concatenate([R[c]['conv_s'] for c in range(8)])[None]
    outs = (y_p, y_s, chunk_v, mla_p, mla_s, lru_h_p, lru_h_s, conv_p, conv_s)
    return tuple(np.ascontiguousarray(o.astype(np.float32)) for o in outs)
```

```python
import numpy as np
import concourse.bass as bass
import concourse.mybir as mybir
from concourse.bass_utils import run_bass_kernel_spmd

F32 = mybir.dt.float32
BF16 = mybir.dt.bfloat16
I32 = mybir.dt.int32
U32 = mybir.dt.uint32
U8 = mybir.dt.uint8
AF = mybir.ActivationFunctionType
ALU = mybir.AluOpType
AX = mybir.AxisListType
DTSIZE = {F32: 4, BF16: 2, I32: 4, U32: 4, U8: 1}
ENGS = ['pe', 'act', 'dve', 'pool', 'sp']


class Op:
    __slots__ = ('eng', 'kind', 'fn', 'deps', 'needed', 'idx', 'sem', 'count', 'pos')

    def __init__(self, eng, kind, fn):
        self.eng = eng
        self.kind = kind
        self.fn = fn
        self.deps = []
        self.needed = False
        self.idx = 0
        self.sem = None
        self.count = 0
        self.pos = 0


class Buf:
    def __init__(self, name, ap, start, end):
        self.name = name
        self.ap = ap
        self.start = start
        self.end = end
        self.last_write = None
        self.readers = []
        self.dsem = None
        self.dcount = 0

    def __getitem__(self, k):
        return self.ap[k]


def _prune(lst):
    best = {}
    for r in lst:
        if r is None:
            continue
        if r.kind == 'c':
            key = ('e', r.eng)
            if key not in best or best[key].pos < r.pos:
                best[key] = r
        else:
            key = ('d', r.sem)
            if key not in best or best[key].count < r.count:
                best[key] = r
    return list(best.values())


class Prog:
    def __init__(self, nc, arena_bytes=206 * 1024, n_dsem=70):
        self.nc = nc
        self.arena = nc.alloc_sbuf_tensor("arena", [128, arena_bytes], U8)
        self.arena_bytes = arena_bytes
        self.top = 0
        self.live = []
        self.retired = []
        self.ops = {e: [] for e in ENGS}
        self.n_dsem = n_dsem
        self.free_dsem = [(i, 0) for i in range(n_dsem)]
        self.dma_bufs = []
        self.banks = []
        self.PB = []
        self.ps_all = nc.alloc_psum_tensor("ps_all", [128, 4096], F32)
        for i in range(8):
            self.PB.append(Buf("psb%d" % i, self.ps_all[:, i * 512:(i + 1) * 512], i * 2048, (i + 1) * 2048))
        self.maxtop = 0

    def sb(self, name, shape, dtype, align=64):
        nbytes = int(np.prod(shape[1:])) * DTSIZE[dtype]
        start = (self.top + align - 1) // align * align
        end = start + nbytes
        assert end <= self.arena_bytes, "SBUF arena overflow at %s: %d" % (name, end)
        self.top = end
        self.maxtop = max(self.maxtop, end)
        np_ = shape[0]
        v = self.arena[0:np_, start:end].bitcast(dtype)
        if len(shape) > 2:
            names = ["d%d" % i for i in range(len(shape) - 1)]
            pat = "p (" + " ".join(names) + ") -> p " + " ".join(names)
            kw = {names[i]: shape[i + 1] for i in range(len(names) - 1)}
            v = v.rearrange(pat, **kw)
        b = Buf(name, v, start, end)
        inh = []
        keep = []
        for o in self.retired:
            if o.start < end and start < o.end:
                inh.append(o.last_write)
                inh.extend(o.readers)
                if not (start <= o.start and o.end <= end):
                    keep.append(o)
            else:
                keep.append(o)
        self.retired = keep
        b.readers = _prune(inh)
        self.live.append(b)
        return b

    def mark(self):
        return (self.top, len(self.live))

    def release(self, m):
        top, n = m
        for b in self.live[n:]:
            self.retired.append(b)
            if b.dsem is not None:
                self.free_dsem.append((b.dsem, b.dcount))
                b.dsem_released = True
        self.live = self.live[:n]
        self.top = top

    def _dep_on(self, O, P):
        if P is None:
            return
        if P.kind == 'c':
            if P.eng == 'pe' and O.eng == 'pe' and O.kind == 'c':
                return
            P.needed = True
        O.deps.append(P)

    def _record(self, O, reads, writes):
        for b in reads:
            self._dep_on(O, b.last_write)
        for b in writes:
            self._dep_on(O, b.last_write)
            for r in b.readers:
                self._dep_on(O, r)
        for b in writes:
            b.last_write = O
            b.readers = []
        for b in reads:
            if b in writes:
                continue
            if O.kind == 'c':
                b.readers = [r for r in b.readers if not (r.kind == 'c' and r.eng == O.eng)]
            else:
                b.readers = [r for r in b.readers if not (r.kind == 'd' and r.sem == O.sem)]
            b.readers.append(O)
        O.pos = len(self.ops[O.eng])
        self.ops[O.eng].append(O)

    def op(self, eng, fn, reads=(), writes=()):
        O = Op(eng, 'c', fn)
        self._record(O, list(reads), list(writes))
        return O

    def _buf_dsem(self, b):
        if b.dsem is None:
            assert self.free_dsem, "out of dma semaphores"
            i, c = self.free_dsem.pop(0)
            b.dsem = i
            b.dcount = c
            if c > 0:
                m = Op('sp', 'd', None)
                m.sem = i
                m.count = c
                b.readers.append(m)
            self.dma_bufs.append(b)

    def dma(self, q, out_ap, in_ap, buf, is_write, reads=(), writes=(), **kw):
        self._buf_dsem(buf)
        buf.dcount += 1

        def fn(eng):
            return eng.dma_start(out=out_ap, in_=in_ap, **kw)
        O = Op(q, 'd', fn)
        O.sem = buf.dsem
        O.count = buf.dcount
        rd = list(reads)
        wr = list(writes)
        if is_write:
            wr.append(buf)
        else:
            rd.append(buf)
        self._record(O, rd, wr)
        return O

    def dma_custom(self, q, fn, buf, is_write, reads=(), writes=()):
        self._buf_dsem(buf)
        buf.dcount += 1
        O = Op(q, 'd', fn)
        O.sem = buf.dsem
        O.count = buf.dcount
        rd = list(reads)
        wr = list(writes)
        if is_write:
            wr.append(buf)
        else:
            rd.append(buf)
        self._record(O, rd, wr)
        return O

    def finish(self):
        O = Op('sp', 'c', lambda eng: eng.nop())
        seen = {}
        for b in self.dma_bufs:
            m = Op('sp', 'd', None)
            m.sem = b.dsem
            m.count = b.dcount
            if b.dsem not in seen or seen[b.dsem].count < m.count:
                seen[b.dsem] = m
        O.deps.extend(seen.values())
        O.pos = len(self.ops['sp'])
        self.ops['sp'].append(O)

    def mm(self, out_buf, out_ap, pairs, reads):
        pairs = list(pairs)

        def fn(pe):
            n = len(pairs)
            ins = None
            for i, (l, r) in enumerate(pairs):
                ins = pe.matmul(out_ap, l, r, start=(i == 0), stop=(i == n - 1))
            return ins
        return self.op('pe', fn, reads=reads, writes=[out_buf])

    def tr(self, out_buf, out_ap, in_ap, ident_ap, reads):
        def fn(pe):
            return pe.transpose(out_ap, in_ap, ident_ap)
        return self.op('pe', fn, reads=reads, writes=[out_buf])

    def emit(self):
        nc = self.nc
        for e in ENGS:
            n = 0
            for o in self.ops[e]:
                if o.kind == 'c' and o.needed:
                    n += 1
                    o.idx = n
        from contextlib import ExitStack
        with ExitStack() as st:
            esem = {e: st.enter_context(nc.semaphore("es_" + e)) for e in ENGS}
            dsem = [st.enter_context(nc.semaphore("ds_%d" % i)) for i in range(self.n_dsem)]
            block = st.enter_context(nc.Block())

            def run(ename, eng):
                waited = {}
                for o in self.ops[ename]:
                    need = {}
                    for p in o.deps:
                        if p.kind == 'c':
                            key = ('e', p.eng)
                            val = p.idx
                        else:
                            key = ('d', p.sem)
                            val = p.count * 16
                        if need.get(key, 0) < val:
                            need[key] = val
                    for key, val in need.items():
                        if waited.get(key, 0) >= val:
                            continue
                        waited[key] = val
                        sem = esem[key[1]] if key[0] == 'e' else dsem[key[1]]
                        eng.wait_ge(sem, val)
                    ins = o.fn(eng)
                    if o.kind == 'c':
                        if o.needed:
                            ins.then_inc(esem[ename], 1)
                    else:
                        ins.then_inc(dsem[o.sem], 16)

            @block.tensor
            def _(e):
                run('pe', e)

            @block.scalar
            def _(e):
                run('act', e)

            @block.vector
            def _(e):
                run('dve', e)

            @block.gpsimd
            def _(e):
                run('pool', e)

            @block.sync
            def _(e):
                run('sp', e)

import math

D = 1024
DFF = 2816
NT = 17
ST = 16
SEQ = 2048
NPAGES = 128
NPOOL = 5120
TSEG = 16
MLA_SCALE = (128 + 64) ** -0.5
NEG = -30000.0
TWO_PI = 2.0 * math.pi
C1 = 6.28125
C2 = TWO_PI - 6.28125


class K:
    def __init__(self, depth=4):
        self.depth = depth
        nc = bass.Bass("TRN2", target_bir_lowering=False)
        self.nc = nc
        self.p = Prog(nc)
        self.bank_rr = list(range(8))
        self.bank_i = 0
        self.ev_i = 0
        self.din = {}
        self.dout = {}

    def inp(self, name, shape, dt=F32):
        a = self.nc.dram_tensor(name, list(shape), dt, kind="ExternalInput").ap()
        self.din[name] = a
        return a

    def outp(self, name, shape):
        a = self.nc.dram_tensor(name, list(shape), F32, kind="ExternalOutput").ap()
        self.dout[name] = a
        return a

    def bank(self):
        b = self.p.PB[self.bank_rr[self.bank_i % len(self.bank_rr)]]
        self.bank_i += 1
        return b

    def act(self, out, in_, func, reads, writes, bias=None, scale=None, accum=None):
        kw = {}
        if bias is not None:
            kw['bias'] = bias
        if scale is not None:
            kw['scale'] = scale
        if accum is not None:
            kw['accum_out'] = accum
        return self.p.op('act', lambda e: e.activation(out, in_, func, **kw), reads=reads, writes=writes)

    def ts(self, out, in0, s1, s2, op0, op1, reads, writes, eng='dve'):
        if s2 is None:
            return self.p.op(eng, lambda e: e.tensor_scalar(out, in0, s1, None, op0), reads=reads, writes=writes)
        return self.p.op(eng, lambda e: e.tensor_scalar(out, in0, s1, s2, op0, op1), reads=reads, writes=writes)

    def tt(self, out, a, b, op, reads, writes, eng='dve'):
        return self.p.op(eng, lambda e: e.tensor_tensor(out, a, b, op), reads=reads, writes=writes)

    def stt(self, out, in0, scalar, in1, op0, op1, reads, writes, eng='dve'):
        return self.p.op(eng, lambda e: e.scalar_tensor_tensor(out, in0, scalar, in1, op0, op1), reads=reads, writes=writes)

    def cp(self, out, in_, reads, writes, eng=None):
        if eng is None:
            eng = 'act' if (self.ev_i % 2 == 0) else 'dve'
            self.ev_i += 1
        if eng == 'act':
            return self.p.op('act', lambda e: e.copy(out, in_), reads=reads, writes=writes)
        return self.p.op(eng, lambda e: e.tensor_copy(out, in_), reads=reads, writes=writes)

    def mm(self, ob, out_ap, pairs, reads):
        return self.p.mm(ob, out_ap, pairs, reads)

    def mm_multi(self, ob, triples, reads):
        triples = list(triples)

        def fn(pe):
            ins = None
            for (o, l, r, st, sp) in triples:
                ins = pe.matmul(o, l, r, start=st, stop=sp)
            return ins
        return self.p.op('pe', fn, reads=reads, writes=[ob])

    def load_w(self, name, src, ktiles, cols, q='pool', split=None):
        b = self.p.sb(name, [128, ktiles, cols], BF16)
        v = src.rearrange("(k p) n -> p k n", p=128)
        if split is None:
            split = max(1, (ktiles * cols * 128 * 4) // (2 << 20))
        split = min(split, ktiles)
        step = (ktiles + split - 1) // split
        for k0 in range(0, ktiles, step):
            k1 = min(ktiles, k0 + step)
            self.p.dma(q, b[:, k0:k1, :], v[:, k0:k1, :], b, True)
        return b

    def load_w_into(self, b, src, ktiles, q='pool'):
        v = src.rearrange("(k p) n -> p k n", p=128)
        self.p.dma(q, b[:, 0:ktiles, 0:v.shape[2]], v, b, True)

    def load_bc(self, name, row_ap, n, np_=128):
        b = self.p.sb(name, [128, n], F32)
        self.p.dma('sp', b[0:np_, :], row_ap.partition_broadcast(np_), b, True)
        return b

    def load_col(self, name, vec_ap, kt):
        b = self.p.sb(name, [128, kt], F32)

        def fn(e):
            return e.dma_start(out=b[:, :], in_=vec_ap.rearrange("(t p) -> p t", p=128), allow_slow_non_contiguous=True)
        self.p.dma_custom('sp', fn, b, True)
        return b

    def setup_consts(self):
        p = self.p
        ones = p.sb("ones", [128, 128], F32)
        identf = self.identf = p.sb("identf", [128, 128], F32)
        self.identb = p.sb("identb", [128, 128], BF16)
        p.op('pool', lambda e: e.memset(ones[:, :], 1.0), writes=[ones])
        p.op('pool', lambda e: e.affine_select(identf[:, :], ones[:, :], [[-1, 128]], ALU.is_equal, 0.0, base=0, channel_multiplier=1), reads=[ones], writes=[identf])
        p.op('pool', lambda e: e.tensor_copy(self.identb[:, :], identf[:, :]), reads=[identf], writes=[self.identb])
        zer = p.sb("zer", [128, 128], F32)
        maskf = p.sb("maskf", [128, 128], F32)
        self.maskb = p.sb("maskb", [128, 128], BF16)
        p.op('pool', lambda e: e.memset(zer[:, :], 0.0), writes=[zer])
        p.op('pool', lambda e: e.affine_select(maskf[:, :], zer[:, :], [[-1, 128]], ALU.is_ge, NEG, base=0, channel_multiplier=1), reads=[zer], writes=[maskf])
        p.op('pool', lambda e: e.tensor_copy(self.maskb[:, :], maskf[:, :]), reads=[maskf], writes=[self.maskb])
        self.mask_s = p.sb("mask_s", [128, 4], F32)
        p.op('pool', lambda e: e.affine_select(self.mask_s[:, :], zer[:, 0:4], [[-8, 4]], ALU.is_ge, NEG, base=0, channel_multiplier=1), reads=[zer], writes=[self.mask_s])
        self.cst = p.sb("cst", [128, 8], F32)
        vals = [1e-6, 1e-5, 1.0, 0.0, -0.5, 0.5, 0.0, 0.0]
        for i, v in enumerate(vals):
            p.op('pool', (lambda i, v: (lambda e: e.memset(self.cst[:, i:i + 1], v)))(i, v), writes=[self.cst])
        self.eps_rms = self.cst[:, 0:1]
        self.eps_ln = self.cst[:, 1:2]
        self.one_col = self.cst[:, 2:3]
        self.junk = p.sb("junk", [128, 1024], BF16)
        self.ss = p.sb("ss", [128, 24], F32)
        self.sd = p.sb("sd", [128, 24], F32)
        self.rstd = p.sb("rstd", [128, 24], F32)
        p.op('pool', lambda e: e.memset(self.ss[:, :], 1.0), writes=[self.ss])

    def rope_tables(self, name, np_, nt, base, chan_mul, tstep):
        p = self.p
        m = p.mark()
        posi = p.sb(name + "_pi", [128, nt], I32)
        posf = p.sb(name + "_pf", [128, nt], F32)
        ang = p.sb(name + "_ang", [128, nt, 32], F32)
        q = p.sb(name + "_q", [128, nt, 32], F32)
        qi = p.sb(name + "_qi", [128, nt, 32], I32)
        r = p.sb(name + "_r", [128, nt, 32], F32)
        t1 = p.sb(name + "_t1", [128, nt, 32], F32)
        rc = p.sb(name + "_rc", [128, nt, 32], F32)
        cos = self.p_persist(name + "_cos", [128, nt, 32], F32)
        sin = self.p_persist(name + "_sin", [128, nt, 32], F32)
        P = slice(0, np_)
        p.op('pool', lambda e: e.iota(posi[P, :], [[tstep, nt]], base=base, channel_multiplier=chan_mul), writes=[posi])
        p.op('dve', lambda e: e.tensor_copy(posf[P, :], posi[P, :]), reads=[posi], writes=[posf])
        invb = self.invf_bc
        self.tt(ang[P, :, :], posf[P, :].unsqueeze(2).broadcast_to([np_, nt, 32]),
                invb[P, :].unsqueeze(1).broadcast_to([np_, nt, 32]), ALU.mult, [posf, invb], [ang])

        def reduce_to(dst, src_buf, src, shift):
            self.ts(q[P], src, shift, 1.0 / TWO_PI, ALU.add, ALU.mult, [src_buf], [q])
            p.op('dve', lambda e: e.tensor_copy(qi[P], q[P]), reads=[q], writes=[qi])
            p.op('dve', lambda e: e.tensor_copy(q[P], qi[P]), reads=[qi], writes=[q])
            self.ts(t1[P], src, shift, None, ALU.add, None, [src_buf], [t1])
            self.stt(t1[P], q[P], -C1, t1[P], ALU.mult, ALU.add, [q, t1], [t1])
            self.stt(t1[P], q[P], -C2, t1[P], ALU.mult, ALU.add, [q, t1], [t1])
            self.ts(q[P], t1[P], math.pi, -TWO_PI, ALU.is_gt, ALU.mult, [t1], [q])
            self.tt(t1[P], t1[P], q[P], ALU.add, [t1, q], [t1])
            self.ts(q[P], t1[P], -math.pi, TWO_PI, ALU.is_lt, ALU.mult, [t1], [q])
            self.tt(t1[P], t1[P], q[P], ALU.add, [t1, q], [t1])
            self.ts(dst[P], t1[P], 3.141592, -3.141592, ALU.min, ALU.max, [t1], [dst])
        reduce_to(r, ang, ang[P], 0.0)
        self.act(sin[P], r[P], AF.Sin, [r], [sin])
        reduce_to(rc, ang, ang[P], math.pi / 2)
        self.act(cos[P], rc[P], AF.Sin, [rc], [cos])
        return cos, sin

    def p_persist(self, name, shape, dt):
        return self.p.sb(name, shape, dt)

    def norm_tiles(self, tiles, gain_bc, dst, col0s):
        p = self.p
        HB = self.HB
        for t in tiles:
            np_ = 16 if t == ST else 128
            self.act(self.junk[0:np_, :], HB[t][0:np_, :], AF.Square, [HB[t]], [self.junk, self.ss], accum=self.ss[0:np_, t:t + 1])
        lo, hi = min(tiles), max(tiles) + 1
        self.act(self.sd[:, lo:hi], self.ss[:, lo:hi], AF.Sqrt, [self.ss], [self.sd], bias=self.eps_rms, scale=1.0 / D)
        p.op('dve', lambda e: e.reciprocal(self.rstd[:, lo:hi], self.sd[:, lo:hi]), reads=[self.sd], writes=[self.rstd])
        for t, c0 in zip(tiles, col0s):
            np_ = 16 if t == ST else 128
            xn = self.xn_bufs[self.xn_i % 2]
            self.xn_i += 1
            self.stt(xn[0:np_, :], HB[t][0:np_, :], self.rstd[0:np_, t:t + 1], gain_bc[0:np_, :], ALU.mult, ALU.mult,
                     [HB[t], self.rstd, gain_bc], [xn])
            pb = self.bank()
            pv = pb.ap.bitcast(BF16).rearrange("p (k n) -> p k n", k=8)
            for k in range(8):
                p.tr(pb, pv[:, k, 0:np_], xn[0:np_, k * 128:(k + 1) * 128], self.identb[0:np_, 0:np_], [xn, self.identb])
            self.cp(dst[:, :, c0:c0 + np_], pv[:, :, 0:np_], [pb], [dst])

    def add_H(self, t, np_, hf, pb):
        HB = self.HB
        self.tt(HB[t][0:np_, hf * 512:(hf + 1) * 512], HB[t][0:np_, hf * 512:(hf + 1) * 512], pb[0:np_, :], ALU.add,
                [HB[t], pb], [HB[t]])

    def ffn(self, layer):
        p = self.p
        m = p.mark()
        Wg0 = p.sb("Wg0", [128, 8, 512], BF16)
        Wu0 = p.sb("Wu0", [128, 8, 512], BF16)
        Wd0 = p.sb("Wd0", [128, 4, 1024], BF16)
        gain = self.load_bc("gF", self.din['norm_ffn'][layer, :], D)
        xnT = p.sb("xnT", [128, 8, 2064], BF16)
        chunks = [(0, 4), (4, 4), (8, 4), (12, 4), (16, 3), (19, 3)]
        Wg = [Wg0, p.sb("Wg1", [128, 8, 512], BF16)]
        Wu = [Wu0, p.sb("Wu1", [128, 8, 512], BF16)]
        Wd = [Wd0, p.sb("Wd1", [128, 4, 1024], BF16)]
        actT = p.sb("actT", [128, 4, 2064], BF16)
        sg = [p.sb("sg%d" % i, [128, 512], F32) for i in range(2)]
        wg_d = self.din['f_w_gate'][layer]
        wu_d = self.din['f_w_up'][layer]
        wd_d = self.din['f_w_down'][layer]
        groups = [(g * 512, 512) for g in range(4)] + [(2048, 16)]

        def load(ci):
            f0, nf = chunks[ci]
            s = ci % 2
            self.load_w_into(Wg[s], wg_d[:, f0 * 128:(f0 + nf) * 128], 8)
            self.load_w_into(Wu[s], wu_d[:, f0 * 128:(f0 + nf) * 128], 8)
            self.load_w_into(Wd[s], wd_d[f0 * 128:(f0 + nf) * 128, :], nf)
        load(0)
        self.norm_tiles(list(range(NT)), gain, xnT, [t * 128 for t in range(NT)])
        si = 0
        for ci, (f0, nf) in enumerate(chunks):
            s = ci % 2
            if ci + 1 < len(chunks):
                load(ci + 1)
            if ci == 3 and self.mid_hook is not None:
                self.mid_hook()
                self.mid_hook = None
            for fi in range(nf):
                for (c0, n) in groups:
                    pg = self.bank()
                    pu = self.bank()
                    self.mm(pg, pg[:, 0:n], [(Wg[s][:, k, fi * 128:(fi + 1) * 128], xnT[:, k, c0:c0 + n]) for k in range(8)], [Wg[s], xnT])
                    self.mm(pu, pu[:, 0:n], [(Wu[s][:, k, fi * 128:(fi + 1) * 128], xnT[:, k, c0:c0 + n]) for k in range(8)], [Wu[s], xnT])
                    sgb = sg[si % 2]
                    si += 1
                    self.act(sgb[:, 0:n], pg[:, 0:n], AF.Silu, [pg], [sgb])
                    self.tt(actT[:, fi, c0:c0 + n], sgb[:, 0:n], pu[:, 0:n], ALU.mult, [sgb, pu], [actT])
            for t in range(NT):
                np_ = 16 if t == ST else 128
                for hf in range(2):
                    pb = self.bank()
                    self.mm(pb, pb[0:np_, :], [(actT[:, fi, t * 128:t * 128 + np_], Wd[s][:, fi, hf * 512:(hf + 1) * 512]) for fi in range(nf)], [actT, Wd[s]])
                    self.add_H(t, np_, hf, pb)
        p.release(m)

    def mixer_A(self, layer, j):
        p = self.p
        m = p.mark()
        Win = self.load_w("Win", self.din['a_w_in'][j], 8, 2048, split=4)
        Wout = self.load_w("WoA", self.din['a_w_out'][j], 8, 1024, split=2)
        gain = self.load_bc("gA", self.din['norm_mix'][layer, :], D)
        lng = self.load_bc("lng", self.din['a_ln_g'][j, :], D)
        lnb = self.load_bc("lnb", self.din['a_ln_b'][j, :], D)
        bsb = self.load_bc("bsb", self.din['a_b_s'][j].rearrange("g t -> (g t)"), 1024)
        bsv = bsb.ap.rearrange("p (g t) -> p g t", g=8)
        wsn = p.sb("wsn", [128, 8, 128], F32)
        wsm = p.sb("wsm", [128, 8, 128], F32)
        wsb = p.sb("wsb", [128, 8, 128], BF16)
        WsT = p.sb("WsT", [128, 8, 128], BF16)
        p.dma('sp', wsn[:, :, :], self.din['a_w_s'][j].rearrange("g t s -> t g s"), wsn, True)
        p.op('pool', lambda e: e.affine_select(wsm[:, :, :], wsn[:, :, :], [[0, 8], [-1, 128]], ALU.is_ge, 0.0, base=0, channel_multiplier=1), reads=[wsn], writes=[wsm])
        p.op('pool', lambda e: e.tensor_copy(wsb[:, :, :], wsm[:, :, :]), reads=[wsm], writes=[wsb])
        pb = self.bank()
        pv = pb.ap.bitcast(BF16).rearrange("p (k n) -> p k n", k=8)
        for g in range(8):
            p.tr(pb, pv[:, g, :], wsb[:, g, :], self.identb[:, :], [wsb, self.identb])
        self.cp(WsT[:, :, :], pv[:, :, :], [pb], [WsT])
        n4 = p.sb("n4", [4, 8, 4], F32)
        n44 = p.sb("n44", [4, 8, 4, 4], F32)
        wk32 = p.sb("wk32", [16, 8, 4, 4], F32)
        wkm = p.sb("wkm", [16, 8, 4, 4], F32)
        wkm2 = p.sb("wkm2", [16, 8, 4, 4], F32)
        Wblk = p.sb("Wblk", [16, 8, 16], BF16)
        p.dma('sp', n4[:, :, :], self.din['a_w_s'][j][:, 0:4, 0:4].rearrange("g t s -> t g s"), n4, True)
        p.op('dve', lambda e: e.tensor_copy(n44[:, :, :, :], n4[:, :, :].unsqueeze(2).broadcast_to([4, 8, 4, 4])), reads=[n4], writes=[n44])
        pbw = self.bank()
        for g in range(8):
            self.mm(pbw, pbw[0:16, g * 4:(g + 1) * 4], [(n44[0:4, g, :, :].rearrange("p b s -> p (b s)"), self.identf[0:4, 0:4])], [n44, self.identf])
        p.op('dve', lambda e: e.tensor_copy(wk32[:, :, :, :], pbw[0:16, 0:32].rearrange("p (g t) -> p g t", g=8).unsqueeze(2).broadcast_to([16, 8, 4, 4])), reads=[pbw], writes=[wk32])
        p.op('pool', lambda e: e.affine_select(wkm[:, :, :, :], wk32[:, :, :, :], [[0, 8], [-4, 4], [0, 4]], ALU.is_ge, 0.0, base=0, channel_multiplier=1), reads=[wk32], writes=[wkm])
        p.op('pool', lambda e: e.affine_select(wkm2[:, :, :, :], wkm[:, :, :, :], [[0, 8], [4, 4], [1, 4]], ALU.is_ge, 0.0, base=0, channel_multiplier=-1), reads=[wkm], writes=[wkm2])
        p.op('pool', lambda e: e.tensor_copy(Wblk[:, :, :], wkm2[:, :, :, :].rearrange("p g b t -> p g (b t)")), reads=[wkm2], writes=[Wblk])
        bss = p.sb("bss", [128, 8, 4, 4], F32)
        p.op('dve', lambda e: e.tensor_copy(bss[:, :, :, :], bsv[:, :, 0:4].unsqueeze(2).broadcast_to([128, 8, 4, 4])), reads=[bsb], writes=[bss])
        bssv = bss.ap.rearrange("p g b t -> p g (b t)")

        xg = p.sb("xgA", [128, 8, 512], BF16)
        uT = p.sb("uT", [128, 8, 512], BF16)
        v32 = [p.sb("v32_%d" % i, [128, 1024], F32) for i in range(2)]
        vtmp = p.sb("vtmp", [128, 1024], F32)
        vnb = [p.sb("vnb%d" % i, [128, 1024], BF16) for i in range(2)]
        gT = [p.sb("gT%d" % i, [128, 8, 128], BF16) for i in range(2)]
        mix = [p.sb("mix%d" % i, [128, 512], F32) for i in range(2)]
        st = p.sb("lnst", [128, 8], F32)
        ti_ = 0
        for grp in range(5):
            if grp < 4:
                tiles = [4 * grp + i for i in range(4)]
                n = 512
            else:
                tiles = [ST]
                n = 16
            self.norm_tiles(tiles, gain, xg, [i * 128 for i in range(len(tiles))])
            for ft in range(8):
                pb = self.bank()
                self.mm(pb, pb[:, 0:n], [(Win[:, k, ft * 128:(ft + 1) * 128], xg[:, k, 0:n]) for k in range(8)], [Win, xg])
                self.act(uT[:, ft, 0:n], pb[:, 0:n], AF.Gelu_apprx_tanh, [pb], [uT])
            def ctx(i, t):
                np_ = 16 if t == ST else 128
                c0 = i * 128
                k_ = (ti_base + i) % 2
                return np_, c0, v32[k_], vnb[k_], gT[k_]

            def stV(i, t):
                np_, c0, v, vb, g_ = ctx(i, t)
                for hf in range(2):
                    pb = self.bank()
                    self.mm(pb, pb[0:np_, :], [(xg[:, k, c0:c0 + np_], Win[:, k, 1024 + hf * 512:1024 + (hf + 1) * 512]) for k in range(8)], [xg, Win])
                    self.act(v[0:np_, hf * 512:(hf + 1) * 512], pb[0:np_, :], AF.Gelu_apprx_tanh, [pb], [v, st], accum=st[0:np_, hf:hf + 1])
                P_ = slice(0, np_)
                self.act(self.junk[P_, :], v[P_, :], AF.Square, [v], [self.junk, st], accum=st[P_, 2:3])
                self.tt(st[P_, 3:4], st[P_, 0:1], st[P_, 1:2], ALU.add, [st], [st])
                self.ts(st[P_, 3:4], st[P_, 3:4], 1.0 / D, None, ALU.mult, None, [st], [st])
                self.tt(st[P_, 4:5], st[P_, 3:4], st[P_, 3:4], ALU.mult, [st], [st])
                self.stt(st[P_, 5:6], st[P_, 2:3], 1.0 / D, st[P_, 4:5], ALU.mult, ALU.subtract, [st], [st])
                self.act(st[P_, 6:7], st[P_, 5:6], AF.Sqrt, [st], [st], bias=self.eps_ln[P_], scale=1.0)
                p.op('dve', (lambda P_: (lambda e: e.reciprocal(st[P_, 7:8], st[P_, 6:7])))(P_), reads=[st], writes=[st])
                self.ts(vtmp[P_, :], v[P_, :], st[P_, 3:4], st[P_, 7:8], ALU.subtract, ALU.mult, [v, st], [vtmp])
                self.tt(vtmp[P_, :], vtmp[P_, :], lng[P_, :], ALU.mult, [vtmp, lng], [vtmp])
                if t == ST:
                    self.tt(v[P_, :], vtmp[P_, :], lnb[P_, :], ALU.add, [vtmp, lnb], [v])
                    p.dma('sp', self.dout['chunk_v'][j], v[P_, :], v, False)
                    self.cp(vb[P_, :], v[P_, :], [v], [vb], eng='dve')
                else:
                    self.tt(vb[P_, :], vtmp[P_, :], lnb[P_, :], ALU.add, [vtmp, lnb], [vb])

            def stS(i, t):
                np_, c0, v, vb, g_ = ctx(i, t)
                for gq in range(2):
                    pb = self.bank()
                    mx = mix[gq]
                    for gi in range(4):
                        g = gq * 4 + gi
                        if t == ST:
                            self.mm(pb, pb[:, gi * 16:(gi + 1) * 16], [(vb[0:16, g * 128:(g + 1) * 128], Wblk[0:16, g, :])], [vb, Wblk])
                        else:
                            self.mm(pb, pb[:, gi * 128:(gi + 1) * 128], [(vb[:, g * 128:(g + 1) * 128], WsT[:, g, :])], [vb, WsT])
                    if t == ST:
                        pv3 = pb[:, 0:64].rearrange("p (g n) -> p g n", g=4)
                        mv3 = mx[:, 0:64].rearrange("p (g n) -> p g n", g=4)
                        self.tt(mv3, pv3, bssv[:, gq * 4:gq * 4 + 4, :], ALU.add, [pb, bss], [mx])
                        self.tt(g_[:, gq * 4:gq * 4 + 4, 0:16], mv3, uT[:, gq * 4:gq * 4 + 4, 0:16], ALU.mult, [mx, uT], [g_])
                    else:
                        pv3 = pb[:, :].rearrange("p (g n) -> p g n", g=4)
                        mv3 = mx[:, :].rearrange("p (g n) -> p g n", g=4)
                        self.tt(mv3, pv3, bsv[:, gq * 4:gq * 4 + 4, :], ALU.add, [pb, bsb], [mx])
                        self.tt(g_[:, gq * 4:gq * 4 + 4, :], mv3, uT[:, gq * 4:gq * 4 + 4, c0:c0 + 128], ALU.mult, [mx, uT], [g_])

            def stO(i, t):
                np_, c0, v, vb, g_ = ctx(i, t)
                for hf in range(2):
                    pb = self.bank()
                    self.mm(pb, pb[0:np_, :], [(g_[:, g, 0:np_], Wout[:, g, hf * 512:(hf + 1) * 512]) for g in range(8)], [g_, Wout])
                    self.add_H(t, np_, hf, pb)

            ti_base = ti_
            ti_ += len(tiles)
            nt_ = len(tiles)
            for step in range(nt_ + 2):
                if step < nt_:
                    stV(step, tiles[step])
                if 0 <= step - 1 < nt_:
                    stS(step - 1, tiles[step - 1])
                if 0 <= step - 2 < nt_:
                    stO(step - 2, tiles[step - 2])
        p.release(m)

    def final(self):
        p = self.p
        m = p.mark()
        gain = self.load_bc("gO", self.din['norm_out'], D)
        yb = [p.sb("yb%d" % i, [128, 1024], F32) for i in range(2)]
        HB = self.HB
        tiles = list(range(NT))
        for t in tiles:
            np_ = 16 if t == ST else 128
            self.act(self.junk[0:np_, :], HB[t][0:np_, :], AF.Square, [HB[t]], [self.junk, self.ss], accum=self.ss[0:np_, t:t + 1])
        self.act(self.sd[:, 0:NT], self.ss[:, 0:NT], AF.Sqrt, [self.ss], [self.sd], bias=self.eps_rms, scale=1.0 / D)
        p.op('dve', lambda e: e.reciprocal(self.rstd[:, 0:NT], self.sd[:, 0:NT]), reads=[self.sd], writes=[self.rstd])
        for t in tiles:
            np_ = 16 if t == ST else 128
            y = yb[t % 2]
            self.stt(y[0:np_, :], HB[t][0:np_, :], self.rstd[0:np_, t:t + 1], gain[0:np_, :], ALU.mult, ALU.mult,
                     [HB[t], self.rstd, gain], [y])
            if t == ST:
                p.dma('sp', self.dout['y_s'], y[0:16, :], y, False)
            else:
                p.dma('sp', self.dout['y_p'][t * 128:(t + 1) * 128, :], y[:, :], y, False)
        p.release(m)


def _add_methods(cls):
    def deco(f):
        setattr(cls, f.__name__, f)
        return f
    return deco


@_add_methods(K)
def mixer_C(self, layer, j):
    p = self.p
    din = self.din
    m = p.mark()
    Wx = self.load_w("Wx", din['c_w_x'][j], 8, 1280, split=2)
    Wg = self.load_w("WgC", din['c_w_gate'][j], 8, 1280, split=2)
    Wo = self.load_w("WoC", din['c_w_out'][j], 10, 1024, split=2)
    Wa = p.sb("Wa", [128, 10, 128], BF16)
    Wi = p.sb("Wi", [128, 10, 128], BF16)
    p.dma('pool', Wa[:, :, :], din['c_w_a'][j].rearrange("n i j -> i n j"), Wa, True)
    p.dma('pool', Wi[:, :, :], din['c_w_i'][j].rearrange("n i j -> i n j"), Wi, True)
    gain = self.load_bc("gC", din['norm_mix'][layer, :], D)
    cw, cb, ba, bi, lam = self.c_cw, self.c_cb, self.c_ba, self.c_bi, self.c_lam
    ex = p.sb("lam_e", [128, 10], F32)
    cl = p.sb("cl", [128, 10], F32)
    self.act(ex[:, :], lam[:, :], AF.Exp, [lam], [ex], scale=-1.0)
    self.act(cl[:, :], ex[:, :], AF.Ln, [ex], [cl], bias=self.one_col, scale=1.0)
    self.ts(cl[:, :], cl[:, :], -8.0, None, ALU.mult, None, [cl], [cl])
    tail_p, hst_p, tail_s, hst_s = self.c_tail_p, self.c_hst_p, self.c_tail_s, self.c_hst_s
    xg = p.sb("xgC", [128, 8, 512], BF16)
    ghT = p.sb("ghT", [128, 10, 512], BF16)
    NB = 2
    T = {nm: [p.sb("%s%d" % (nm, i), [128, 520], F32) for i in range(NB)] for nm in ['xx']}
    TH = {nm: [[p.sb("%s%d_%d" % (nm, i, hf), [128, 260], F32) for hf in range(2)] for i in range(NB)] for nm in ['gate', 'xc', 'gi', 'a', 's', 'hh']}
    xcb = [p.sb("xcb%d" % i, [128, 512], BF16) for i in range(NB)]
    xg_s = p.sb("xgCs", [128, 8, 16], BF16)
    ghT_s = p.sb("ghTs", [128, 10, 16], BF16)
    T_s = {'xx': [p.sb("xxs%d" % i, [128, 32], F32) for i in range(NB)]}
    TH_s = {nm: [[p.sb("%ss%d" % (nm, i), [128, 16], F32)] for i in range(NB)] for nm in ['gate', 'xc', 'gi', 'a', 's', 'hh']}
    xcb_s = [p.sb("xcbs%d" % i, [128, 16], BF16) for i in range(NB)]
    T_p, TH_p, xcb_p, xg_p, ghT_p = T, TH, xcb, xg, ghT

    def build_ctx(grp):
        if grp < 4:
            tiles = [4 * grp + i for i in range(4)]
            n, nseq, L = 512, 1, 512
            tail, hst = tail_p, hst_p
            T, TH, xcb, xg, ghT = T_p, TH_p, xcb_p, xg_p, ghT_p
        else:
            tiles = [ST]
            n, nseq, L = 16, 4, 4
            tail, hst = tail_s, hst_s
            T, TH, xcb, xg, ghT = T_s, TH_s, xcb_s, xg_s, ghT_s
        self.norm_tiles(tiles, gain, xg, [i * 128 for i in range(len(tiles))])
        if grp < 4:
            HV = [(0, 256), (256, 256)]
        else:
            HV = [(0, 16)]

        def stA(ft, s_):
            xx = T['xx'][s_]
            gate, xc = TH['gate'][s_], TH['xc'][s_]
            xb = xcb[s_]
            px = self.bank()
            self.mm(px, px[:, 0:n], [(Wx[:, k, ft * 128:(ft + 1) * 128], xg[:, k, 0:n]) for k in range(8)], [Wx, xg])
            pg = self.bank()
            self.mm(pg, pg[:, 0:n], [(Wg[:, k, ft * 128:(ft + 1) * 128], xg[:, k, 0:n]) for k in range(8)], [Wg, xg])
            for hv, (c0, w) in enumerate(HV):
                self.act(gate[hv][:, 0:w], pg[:, c0:c0 + w], AF.Gelu_apprx_tanh, [pg], [gate[hv]])
            xx3 = xx[:, 0:nseq * (L + 3)].rearrange("p (s l) -> p s l", s=nseq)
            self.cp(xx3[:, :, 0:3], tail[:, ft, :, :], [tail], [xx], eng='dve')
            if grp < 4:
                for hv, (c0, w) in enumerate(HV):
                    self.cp(xx[:, 3 + c0:3 + c0 + w], px[:, c0:c0 + w], [px], [xx], eng='dve')
            else:
                self.cp(xx3[:, :, 3:3 + L], px[:, 0:n].rearrange("p (s l) -> p s l", s=nseq), [px], [xx], eng='dve')

            def xin(hv, k):
                c0, w = HV[hv]
                if grp < 4:
                    return xx[:, c0 + k:c0 + k + w], xc[hv][:, 0:w]
                return xx3[:, :, k:k + L], xc[hv][:, 0:n].rearrange("p (s l) -> p s l", s=nseq)
            for hv in range(len(HV)):
                i_, o_ = xin(hv, 0)
                self.ts(o_, i_, cw[0][:, ft:ft + 1], cb[:, ft:ft + 1], ALU.mult, ALU.add, [xx, cw[0], cb], [xc[hv]])
            for k in range(1, 4):
                for hv in range(len(HV)):
                    i_, o_ = xin(hv, k)
                    self.stt(o_, i_, cw[k][:, ft:ft + 1], o_, ALU.mult, ALU.add, [xx, cw[k], xc[hv]], [xc[hv]])
            self.cp(tail[:, ft, :, :], xx3[:, :, L:L + 3], [xx], [tail], eng='dve')
            for hv, (c0, w) in enumerate(HV):
                self.cp(xb[:, c0:c0 + w], xc[hv][:, 0:w], [xc[hv]], [xb], eng='dve')

        def stB(ft, s_):
            gate, xc, gi, a, s, hh = [TH[nm][s_] for nm in ['gate', 'xc', 'gi', 'a', 's', 'hh']]
            xb = xcb[s_]
            pa = self.bank()
            self.mm(pa, pa[:, 0:n], [(Wa[:, ft, :], xb[:, 0:n])], [Wa, xb])
            pi_ = self.bank()
            self.mm(pi_, pi_[:, 0:n], [(Wi[:, ft, :], xb[:, 0:n])], [Wi, xb])
            for hv, (c0, w) in enumerate(HV):
                self.act(a[hv][:, 0:w], pa[:, c0:c0 + w], AF.Sigmoid, [pa, ba], [a[hv]], bias=ba[:, ft:ft + 1], scale=1.0)
            for hv, (c0, w) in enumerate(HV):
                self.act(gi[hv][:, 0:w], pi_[:, c0:c0 + w], AF.Sigmoid, [pi_, bi], [gi[hv]], bias=bi[:, ft:ft + 1], scale=1.0)
            for hv, (c0, w) in enumerate(HV):
                self.act(a[hv][:, 0:w], a[hv][:, 0:w], AF.Exp, [a[hv], cl], [a[hv]], scale=cl[:, ft:ft + 1])
            for hv, (c0, w) in enumerate(HV):
                self.act(s[hv][:, 0:w], a[hv][:, 0:w], AF.Square, [a[hv]], [s[hv]])
            for hv, (c0, w) in enumerate(HV):
                self.act(s[hv][:, 0:w], s[hv][:, 0:w], AF.Sqrt, [s[hv]], [s[hv]], bias=self.one_col, scale=-1.0)
            for hv, (c0, w) in enumerate(HV):
                self.tt(s[hv][:, 0:w], s[hv][:, 0:w], gi[hv][:, 0:w], ALU.mult, [s[hv], gi[hv]], [s[hv]])
            for hv, (c0, w) in enumerate(HV):
                self.tt(s[hv][:, 0:w], s[hv][:, 0:w], xc[hv][:, 0:w], ALU.mult, [s[hv], xc[hv]], [s[hv]])
            if grp < 4:
                for hv, (c0, w) in enumerate(HV):
                    ini = hst[:, ft, 0:1] if hv == 0 else hh[hv - 1][:, 255:256]
                    rd = [a[hv], s[hv], hst] if hv == 0 else [a[hv], s[hv], hh[hv - 1]]

                    def sfn(e, hv=hv, w=w, ini=ini):
                        return e.tensor_tensor_scan(hh[hv][:, 0:w], a[hv][:, 0:w], s[hv][:, 0:w], ini, ALU.mult, ALU.add)
                    p.op('dve', sfn, reads=rd, writes=[hh[hv]])
                self.cp(hst[:, ft, :], hh[1][:, 255:256], [hh[1]], [hst], eng='dve')
            else:
                for q in range(nseq):
                    def sfn(e, q=q, L=L, hst=hst, hh=hh, a=a, s=s, ft=ft):
                        return e.tensor_tensor_scan(hh[0][:, q * L:(q + 1) * L], a[0][:, q * L:(q + 1) * L], s[0][:, q * L:(q + 1) * L],
                                                    hst[:, ft, q:q + 1], ALU.mult, ALU.add)
                    p.op('dve', sfn, reads=[a[0], s[0], hst], writes=[hh[0]])
                hh3 = hh[0][:, 0:n].rearrange("p (s l) -> p s l", s=nseq)
                self.cp(hst[:, ft, :], hh3[:, :, L - 1], [hh[0]], [hst], eng='dve')
            for hv, (c0, w) in enumerate(HV):
                self.tt(ghT[:, ft, c0:c0 + w], gate[hv][:, 0:w], hh[hv][:, 0:w], ALU.mult, [gate[hv], hh[hv]], [ghT])

        def outp():
            for i, t in enumerate(tiles):
                np_ = 16 if t == ST else 128
                for hf in range(2):
                    pb = self.bank()
                    self.mm(pb, pb[0:np_, :], [(ghT[:, ft, i * 128:i * 128 + np_], Wo[:, ft, hf * 512:(hf + 1) * 512]) for ft in range(10)], [ghT, Wo])
                    self.add_H(t, np_, hf, pb)
        return stA, stB, outp

    sA, sB, sO = build_ctx(4)
    it = 0
    for grp in range(4):
        A_, B_, O_ = build_ctx(grp)
        sets = [(it + f) % NB for f in range(10)]
        it += 10
        ex = (grp == 0)
        A_(0, sets[0])
        if ex:
            sA(0, 0)
        for ft in range(1, 10):
            A_(ft, sets[ft])
            if ex:
                sA(ft, ft % NB)
            B_(ft - 1, sets[ft - 1])
            if ex:
                sB(ft - 1, (ft - 1) % NB)
        B_(9, sets[9])
        if ex:
            sB(9, 9 % NB)
        O_()
        if ex:
            sO()
    p.release(m)


@_add_methods(K)
def rope_apply(self, np_, d1, d2, x1, x2, c, s, ta, tb, rbufs, wbufs, tbufs):
    self.tt(ta, x1, c, ALU.mult, rbufs, [tbufs[0]])
    self.tt(tb, x2, s, ALU.mult, rbufs, [tbufs[1]])
    self.tt(d1, ta, tb, ALU.subtract, tbufs, wbufs)
    self.tt(ta, x1, s, ALU.mult, rbufs, [tbufs[0]])
    self.tt(tb, x2, c, ALU.mult, rbufs, [tbufs[1]])
    self.tt(d2, ta, tb, ALU.add, tbufs, wbufs)


@_add_methods(K)
def rope_tables2(self, name, np_, nt, posf):
    p = self.p
    cos = p.sb(name + "_cos", [128, nt, 32], F32)
    sin = p.sb(name + "_sin", [128, nt, 32], F32)
    m = p.mark()
    ang = p.sb(name + "_ang", [128, nt, 32], F32)
    q = p.sb(name + "_q", [128, nt, 32], F32)
    qi = p.sb(name + "_qi", [128, nt, 32], I32)
    t1 = p.sb(name + "_t1", [128, nt, 32], F32)
    r = p.sb(name + "_r", [128, nt, 32], F32)
    P = slice(0, np_)
    invb = self.invf_bc
    self.tt(ang[P, :, :], posf[P, :].unsqueeze(2).broadcast_to([np_, nt, 32]),
            invb[P, :].unsqueeze(1).broadcast_to([np_, nt, 32]), ALU.mult, [posf, invb], [ang])

    def reduce_to(dst, shift):
        self.ts(q[P], ang[P], shift, 1.0 / TWO_PI, ALU.add, ALU.mult, [ang], [q])
        p.op('dve', lambda e: e.tensor_copy(qi[P], q[P]), reads=[q], writes=[qi])
        p.op('dve', lambda e: e.tensor_copy(q[P], qi[P]), reads=[qi], writes=[q])
        self.ts(t1[P], ang[P], shift, None, ALU.add, None, [ang], [t1])
        self.stt(t1[P], q[P], -C1, t1[P], ALU.mult, ALU.add, [q, t1], [t1])
        self.stt(t1[P], q[P], -C2, t1[P], ALU.mult, ALU.add, [q, t1], [t1])
        self.ts(q[P], t1[P], math.pi, -TWO_PI, ALU.is_gt, ALU.mult, [t1], [q])
        self.tt(t1[P], t1[P], q[P], ALU.add, [t1, q], [t1])
        self.ts(q[P], t1[P], -math.pi, TWO_PI, ALU.is_lt, ALU.mult, [t1], [q])
        self.tt(t1[P], t1[P], q[P], ALU.add, [t1, q], [t1])
        self.ts(dst[P], t1[P], 3.141592, -3.141592, ALU.min, ALU.max, [t1], [dst])
    reduce_to(r, 0.0)
    self.act(sin[P], r[P], AF.Sin, [r], [sin])
    reduce_to(r, math.pi / 2)
    self.act(cos[P], r[P], AF.Sin, [r], [cos])
    p.release(m)
    return cos, sin


@_add_methods(K)
def mixer_B(self, layer, j):
    p = self.p
    din = self.din
    dout = self.dout
    PB = p.PB
    m = p.mark()
    Wuv = self.load_w("Wuv", din['b_w_uv'][j].rearrange("c h v -> c (h v)"), 2, 1024)
    Wo = self.load_w("WoB", din['b_w_out'][j], 8, 1024, split=2)
    QlT_s = p.sb("QlT_s", [128, 8, 2, 16], BF16)
    QpeT_s = p.sb("QpeT_s", [128, 4, 16], BF16)
    Qpad_s = p.sb("Qpad_s", [128, 8, 16], BF16)
    KTs = p.sb("KTs", [128, 3, 16], BF16)
    Vs = p.sb("Vs", [16, 1, 256], BF16)
    ovT_s = p.sb("ovT_s", [128, 8, 16], BF16)
    stats = [p.sb("stat%d" % i, [128, 16], F32) for i in range(4)]
    mask4 = p.sb("mask4", [32, 4, 16], F32)
    mtmp = p.sb("mtmp", [32, 16], F32)
    zer32 = p.sb("zer32", [32, 16], F32)
    p.op('pool', lambda e: e.memset(zer32[:, :], 0.0), writes=[zer32])
    for b in range(4):
        p.op('pool', (lambda b: (lambda e: e.affine_select(mtmp[:, :], zer32[:, :], [[0, 4], [-8, 4]], ALU.is_ge, NEG, base=0, channel_multiplier=1)))(b), reads=[zer32], writes=[mtmp])
        p.op('pool', (lambda b: (lambda e: e.affine_select(mask4[:, b, :], mtmp[:, :], [[1, 4], [0, 4]], ALU.is_equal, NEG, base=-b, channel_multiplier=0)))(b), reads=[mtmp], writes=[mask4])
    ptT = p.sb("ptT", [128, 4], I32)

    def ptfn(e):
        return e.dma_start(out=ptT[:, :], in_=din['pt'].rearrange("b j -> j b"), allow_slow_non_contiguous=True)
    p.dma_custom('sp', ptfn, ptT, True)
    ptf = p.sb("ptf", [128, 4], F32)
    si = p.sb("si", [128, 8], I32)
    sf = p.sb("sf", [128, 8], F32)
    idxf = p.sb("idxf", [128, 4, 8], F32)
    idx = p.sb("idx", [128, 4, 8], I32)
    p.op('dve', lambda e: e.tensor_copy(ptf[:, :], ptT[:, :]), reads=[ptT], writes=[ptf])
    self.ts(ptf[:, :], ptf[:, :], 8.0, None, ALU.mult, None, [ptf], [ptf])
    p.op('pool', lambda e: e.iota(si[:, :], [[1, 8]], base=0, channel_multiplier=0), writes=[si])
    p.op('dve', lambda e: e.tensor_copy(sf[:, :], si[:, :]), reads=[si], writes=[sf])
    self.tt(idxf[:, :, :], ptf[:, :].unsqueeze(2).broadcast_to([128, 4, 8]), sf[:, :].unsqueeze(1).broadcast_to([128, 4, 8]), ALU.add, [ptf, sf], [idxf])
    p.op('dve', lambda e: e.tensor_copy(idx[:, :, :], idxf[:, :, :]), reads=[idxf], writes=[idx])

    mP = p.mark()
    Wdq = self.load_w("Wdq", din['b_w_dq'][j], 8, 384)
    Wuq = self.load_w("Wuq", din['b_w_uq'][j], 3, 1536)
    Wdkv = self.load_w("Wdkv", din['b_w_dkv'][j], 8, 320)
    WukT = p.sb("WukT", [128, 8, 256], BF16)
    gain = self.load_bc("gB", din['norm_mix'][layer, :], D)
    kvg = self.load_bc("kvg", din['b_kv_norm'][j, :], 256)
    qg = self.load_col("qg", din['b_q_norm'][j, :], 3)
    self.invf_bc = self.load_bc("invf", din['invf'], 32)
    posi = p.sb("posi", [128, 16], I32)
    posf = p.sb("posf", [128, 16], F32)
    p.op('pool', lambda e: e.iota(posi[:, :], [[128, 16]], base=0, channel_multiplier=1), writes=[posi])
    p.op('dve', lambda e: e.tensor_copy(posf[:, :], posi[:, :]), reads=[posi], writes=[posf])
    cos_p, sin_p = self.rope_tables2("rp", 128, 16, posf)
    prow_i = p.sb("prow_i", [1, 16], I32)
    prow_f = p.sb("prow_f", [1, 16], F32)
    posf_s = p.sb("posf_s", [128, 1], F32)
    p.op('pool', lambda e: e.iota(prow_i[:, :], [[0, 4], [1, 4]], base=NPAGES * 128, channel_multiplier=0), writes=[prow_i])
    p.op('dve', lambda e: e.tensor_copy(prow_f[:, :], prow_i[:, :]), reads=[prow_i], writes=[prow_f])
    pbp = self.bank()
    self.mm(pbp, pbp[0:16, 0:1], [(prow_f[0:1, 0:16], self.identf[0:1, 0:1])], [prow_f, self.identf])
    self.cp(posf_s[0:16, :], pbp[0:16, 0:1], [pbp], [posf_s], eng='dve')
    cos_s, sin_s = self.rope_tables2("rs", 16, 1, posf_s)
    mW = p.mark()
    Wukn = self.load_w("Wukn", din['b_w_uk'][j].rearrange("c h n -> c (h n)"), 2, 1024)
    for cc in range(2):
        pb = self.bank()
        pv = pb.ap.bitcast(BF16).rearrange("p (k n) -> p k n", k=8)
        for h in range(8):
            p.tr(pb, pv[:, h, :], Wukn[:, cc, h * 128:(h + 1) * 128], self.identb[:, :], [Wukn, self.identb])
        self.cp(WukT[:, :, cc * 128:(cc + 1) * 128], pv[:, :, :], [pb], [WukT])
    p.release(mW)
    KT = p.sb("KT", [128, 3, 2048], BF16)
    V = p.sb("V", [128, 16, 256], BF16)
    xg = p.sb("xgB", [128, 8, 512], BF16)
    QlT = p.sb("QlT", [128, 8, 2, 512], BF16)
    QpeT = p.sb("QpeT", [128, 4, 512], BF16)
    ovTs = [p.sb("ovT%d" % i, [128, 8, 128], BF16) for i in range(2)]
    sti = [0]

    def newstat():
        s = stats[sti[0] % 4]
        sti[0] += 1
        return s

    for grp in [4, 0, 1, 2, 3]:
        if grp < 4:
            tiles = [4 * grp + i for i in range(4)]
            n = 512
            QlT_g, QpeT_g = QlT, QpeT
            cosb, sinb = cos_p, sin_p
        else:
            tiles = [ST]
            n = 16
            QlT_g, QpeT_g = QlT_s, QpeT_s
            cosb, sinb = cos_s, sin_s
        self.bank_rr = list(range(8))
        self.norm_tiles(tiles, gain, xg, [i * 128 for i in range(len(tiles))])
        m2 = p.mark()
        cqn = p.sb("cqn", [128, 384], BF16)
        cqT = p.sb("cqT", [128, 3, 512], BF16)
        qn = [p.sb("qn%d" % i, [128, 512], BF16) for i in range(2)]
        qpe = p.sb("qpe", [128, 8, 64], BF16)
        kvrow = [p.sb("kvrow%d" % i, [128, 320], F32) for i in range(2)]
        kpe2 = p.sb("kpe2", [128, 2, 64], BF16)
        ra = p.sb("ra", [128, 8, 32], F32)
        rb = p.sb("rb", [128, 8, 32], F32)
        for i, t in enumerate(tiles):
            np_ = 16 if t == ST else 128
            P_ = slice(0, np_)
            c0 = i * 128
            tt_ = 0 if t == ST else t
            st = newstat()
            pq = self.bank()
            self.mm(pq, pq[P_, 0:384], [(xg[:, k, c0:c0 + np_], Wdq[:, k, :]) for k in range(8)], [xg, Wdq])
            self.act(self.junk[P_, 0:384], pq[P_, 0:384], AF.Square, [pq], [self.junk, st], accum=st[P_, 0:1])
            self.act(st[P_, 1:2], st[P_, 0:1], AF.Sqrt, [st], [st], bias=self.eps_rms[P_], scale=1.0 / 384)
            p.op('dve', (lambda st, P_: (lambda e: e.reciprocal(st[P_, 2:3], st[P_, 1:2])))(st, P_), reads=[st], writes=[st])
            self.ts(cqn[P_, :], pq[P_, 0:384], st[P_, 2:3], None, ALU.mult, None, [pq, st], [cqn])
            pb = self.bank()
            pv = pb.ap.bitcast(BF16).rearrange("p (k n) -> p k n", k=8)
            for kc in range(3):
                p.tr(pb, pv[:, kc, 0:np_], cqn[P_, kc * 128:(kc + 1) * 128], self.identb[P_, 0:np_], [cqn, self.identb])
            self.tt(cqT[:, :, c0:c0 + np_], pv[:, 0:3, 0:np_], qg[:, 0:3].unsqueeze(2).broadcast_to([128, 3, np_]), ALU.mult, [pb, qg], [cqT])
            pp = self.bank()
            ppv = pp[P_, :].rearrange("p (h c) -> p h c", c=64)
            self.mm(pp, ppv, [(cqT[:, kc, c0:c0 + np_], Wuq[:, kc, :].rearrange("p (h c) -> p h c", c=192)[:, :, 128:192]) for kc in range(3)], [cqT, Wuq])
            cb_ = cosb[P_, tt_, :].unsqueeze(1).broadcast_to([np_, 8, 32])
            sb_ = sinb[P_, tt_, :].unsqueeze(1).broadcast_to([np_, 8, 32])
            self.rope_apply(np_, qpe[P_, :, 0:32], qpe[P_, :, 32:64], ppv[:, :, 0:32], ppv[:, :, 32:64], cb_, sb_,
                            ra[P_], rb[P_], [pp, cosb, sinb], [qpe], [ra, rb])
            pb = self.bank()
            pv = pb.ap.bitcast(BF16).rearrange("p (k n) -> p k n", k=8)
            for pr in range(4):
                p.tr(pb, pv[:, pr, 0:np_], qpe[P_, 2 * pr:2 * pr + 2, :].rearrange("p h c -> p (h c)"), self.identb[P_, 0:np_], [qpe, self.identb])
            self.cp(QpeT_g[:, :, c0:c0 + np_], pv[:, 0:4, 0:np_], [pb], [QpeT_g])
            kvr = kvrow[i % 2]
            pk = self.bank()
            self.mm(pk, pk[P_, 0:320], [(xg[:, k, c0:c0 + np_], Wdkv[:, k, :]) for k in range(8)], [xg, Wdkv])
            self.act(self.junk[P_, 0:256], pk[P_, 0:256], AF.Square, [pk], [self.junk, st], accum=st[P_, 3:4])
            self.act(st[P_, 4:5], st[P_, 3:4], AF.Sqrt, [st], [st], bias=self.eps_rms[P_], scale=1.0 / 256)
            p.op('dve', (lambda st, P_: (lambda e: e.reciprocal(st[P_, 5:6], st[P_, 4:5])))(st, P_), reads=[st], writes=[st])
            self.stt(kvr[P_, 0:256], pk[P_, 0:256], st[P_, 5:6], kvg[P_, :], ALU.mult, ALU.mult, [pk, st, kvg], [kvr])
            self.rope_apply(np_, kvr[P_, 256:288], kvr[P_, 288:320], pk[P_, 256:288], pk[P_, 288:320],
                            cosb[P_, tt_, :], sinb[P_, tt_, :], ra[P_, 0, :], rb[P_, 0, :], [pk, cosb, sinb], [kvr], [ra, rb])
            if t == ST:
                p.dma('sp', dout['mla_s'], kvr[P_, :], kvr, False)
                Vdst = Vs[0:16, 0, :]
                Vb = Vs
            else:
                p.dma('sp', dout['mla_p'][t * 128:(t + 1) * 128, :], kvr[:, :], kvr, False)
                Vdst = V[:, t, :]
                Vb = V
            self.cp(Vdst, kvr[P_, 0:256], [kvr], [Vb])
            self.cp(kpe2[P_, :, :], kvr[P_, 256:320].unsqueeze(1).broadcast_to([np_, 2, 64]), [kvr], [kpe2], eng='dve')
            pb = self.bank()
            pv = pb.ap.bitcast(BF16).rearrange("p (k n) -> p k n", k=8)
            for cc in range(2):
                p.tr(pb, pv[:, cc, 0:np_], Vdst[:, cc * 128:(cc + 1) * 128], self.identb[P_, 0:np_], [Vb, self.identb])
            p.tr(pb, pv[:, 2, 0:np_], kpe2[P_, :, :].rearrange("p a c -> p (a c)"), self.identb[P_, 0:np_], [kpe2, self.identb])
            if t == ST:
                self.cp(KTs[:, :, 0:16], pv[:, 0:3, 0:16], [pb], [KTs])
            else:
                self.cp(KT[:, :, t * 128:(t + 1) * 128], pv[:, 0:3, :], [pb], [KT])
        for h in range(8):
            pn = self.bank()
            self.mm(pn, pn[:, 0:n], [(Wuq[:, kc, h * 192:h * 192 + 128], cqT[:, kc, 0:n]) for kc in range(3)], [Wuq, cqT])
            q_ = qn[h % 2]
            self.cp(q_[:, 0:n], pn[:, 0:n], [pn], [q_])
            for cc in range(2):
                pl = self.bank()
                self.mm(pl, pl[:, 0:n], [(WukT[:, h, cc * 128:(cc + 1) * 128], q_[:, 0:n])], [WukT, q_])
                self.cp(QlT_g[:, h, cc, 0:n], pl[:, 0:n], [pl], [QlT_g])
        p.release(m2)
        if grp == 4:
            p.op('pool', lambda e: e.memset(Qpad_s[:, :, :], 0.0), writes=[Qpad_s])
            self.cp(Qpad_s[0:64, 0::2, :], QpeT_s[0:64, :, :], [QpeT_s], [Qpad_s], eng='dve')
            self.cp(Qpad_s[64:128, 1::2, :], QpeT_s[64:128, :, :], [QpeT_s], [Qpad_s], eng='dve')
            continue
        m3 = p.mark()
        Pbs = [p.sb("Pb%d" % i, [128, 2048], BF16) for i in range(2)]
        PTss = [p.sb("PTs%d" % i, [128, 16, 128], BF16) for i in range(2)]
        OTs = [p.sb("OTs%d" % i, [128, 2, 128], BF16) for i in range(2)]
        steps = []
        for i, t in enumerate(tiles):
            G_ = 4 if t < 4 else (2 if t < 8 else 1)
            for h0 in range(0, 8, G_):
                steps.append((i, t, list(range(h0, h0 + G_))))

        def stage1(k):
            i, t, hs = steps[k]
            R = k % 2
            base = 4 * R
            G = len(hs)
            nbh = 4 // G
            qc = i * 128
            nk = (t + 1) * 128
            nb = (nk + 511) // 512
            sbufs = []
            for hi, h in enumerate(hs):
                hp = h % 2
                for kb in range(nb):
                    n_ = min(512, nk - kb * 512)
                    bk = PB[base + hi * nbh + kb]
                    sbufs.append(bk)
                    last = (kb == nb - 1)
                    tr_ = [(bk[:, 0:n_], QlT[:, h, 0, qc:qc + 128], KT[:, 0, kb * 512:kb * 512 + n_], True, False),
                           (bk[:, 0:n_], QlT[:, h, 1, qc:qc + 128], KT[:, 1, kb * 512:kb * 512 + n_], False, False),
                           (bk[:, 0:n_], QpeT[hp * 64:(hp + 1) * 64, h // 2, qc:qc + 128], KT[hp * 64:(hp + 1) * 64, 2, kb * 512:kb * 512 + n_], False, not last)]
                    if last:
                        tr_.append((bk[:, n_ - 128:n_], self.identb[:, :], self.maskb[:, :], False, True))
                    self.mm_multi(bk, tr_, [QlT, QpeT, KT, self.identb, self.maskb])
            S3 = p.ps_all[:, base * 512:(base + 4) * 512].rearrange("p (g n) -> p g n", g=G)[:, :, 0:nk]
            Pb3 = Pbs[R].ap.rearrange("p (g n) -> p g n", g=G)[:, :, 0:nk]
            return sbufs, S3, nk, G, Pb3, Pbs[R]

        def stage1ew(info):
            sbufs, S3, nk, G, Pb3, Pb = info
            st = newstat()
            p.op('dve', (lambda st, S3, G: (lambda e: e.reduce_max(st[:, 0:G], S3, AX.X)))(st, S3, G), reads=sbufs, writes=[st])
            self.ts(st[:, 4:4 + G], st[:, 0:G], -MLA_SCALE, None, ALU.mult, None, [st], [st])
            for hi in range(G):
                self.act(Pb3[:, hi, :], S3[:, hi, :], AF.Exp, sbufs + [st], [Pb, st], bias=st[:, 4 + hi:5 + hi], scale=MLA_SCALE, accum=st[:, 8 + hi:9 + hi])
            p.op('dve', (lambda st, G: (lambda e: e.reciprocal(st[:, 12:12 + G], st[:, 8:8 + G])))(st, G), reads=[st], writes=[st])
            if G == 1:
                self.ts(Pb3[:, 0, :], Pb3[:, 0, :], st[:, 12:13], None, ALU.mult, None, [Pb, st], [Pb])
            else:
                self.tt(Pb3, Pb3, st[:, 12:12 + G].unsqueeze(2).broadcast_to([128, G, nk]), ALU.mult, [Pb, st], [Pb])

        def stage2(k):
            i, t, hs = steps[k]
            R = k % 2
            base = 4 * R
            G = len(hs)
            nk = (t + 1) * 128
            Pb = Pbs[R]
            Pb3 = Pb.ap.rearrange("p (g n) -> p g n", g=G)
            PTs = PTss[R]
            nkt = t + 1
            for hi, h in enumerate(hs):
                for k0 in range(0, nkt, 8):
                    k1 = min(nkt, k0 + 8)
                    pb = PB[base + k0 // 8]
                    pv = pb.ap.bitcast(BF16).rearrange("p (k n) -> p k n", k=8)
                    for kt in range(k0, k1):
                        p.tr(pb, pv[:, kt - k0, :], Pb3[:, hi, kt * 128:(kt + 1) * 128], self.identb[:, :], [Pb, self.identb])
                    self.cp(PTs[:, k0:k1, :], pv[:, 0:k1 - k0, :], [pb], [PTs])
                po = PB[base + 2]
                for cc in range(2):
                    self.mm(po, po[:, cc * 128:(cc + 1) * 128], [(V[:, kt, cc * 128:(cc + 1) * 128], PTs[:, kt, :]) for kt in range(nkt)], [V, PTs])
                ot = OTs[h % 2]
                self.cp(ot[:, :, :], po[:, 0:256].rearrange("p (c n) -> p c n", c=2), [po], [ot])
                pv_ = PB[base + 3]
                self.mm(pv_, pv_[:, 0:128], [(Wuv[:, cc, h * 128:(h + 1) * 128], ot[:, cc, :]) for cc in range(2)], [Wuv, ot])
                ovb = ovTs[t % 2]
                self.cp(ovb[:, h, :], pv_[:, 0:128], [pv_], [ovb])
                if h == 7:
                    for hf in range(2):
                        pb = PB[base + 2 + hf]
                        self.mm(pb, pb[:, :], [(ovb[:, hh_, :], Wo[:, hh_, hf * 512:(hf + 1) * 512]) for hh_ in range(8)], [ovb, Wo])
                        self.add_H(t, 128, hf, pb)

        stage1ew(stage1(0))
        for k in range(1, len(steps)):
            inf = stage1(k)
            stage2(k - 1)
            stage1ew(inf)
        stage2(len(steps) - 1)
        p.release(m3)
    p.release(mP)
    self.bank_rr = [6, 7]
    gbuf = [p.sb("gbuf%d" % i, [128, 16, 320], F32) for i in range(2)]
    kb16 = [p.sb("kb16_%d" % i, [128, 16, 384], BF16) for i in range(2)]
    KTsegs = [p.sb("KTseg%d" % i, [128, 3, 2048], BF16) for i in range(2)]
    Ps = p.sb("Ps", [32, 2048], BF16)
    PTq = p.sb("PTq", [128, 16, 32], BF16)
    Oacc = p.sb("Oacc", [32, 256], F32)
    Onb = p.sb("Onb", [32, 256], BF16)
    OTq = p.sb("OTq", [128, 2, 32], BF16)
    sn = p.sb("sn", [32, 16], F32)
    run = p.sb("run", [32, 4], F32)
    cache = din['cache']
    gi_ = 0
    kbank = [4, 5]
    kbi = 0
    R32 = slice(0, 32)

    def flash(S_ap, sbufs, n):
        st = newstat()
        p.op('dve', (lambda st: (lambda e: e.reduce_max(st[R32, 0:1], S_ap, AX.X)))(st), reads=sbufs, writes=[st])
        self.tt(st[R32, 1:2], run[:, 0:1], st[R32, 0:1], ALU.max, [run, st], [st])
        self.tt(st[R32, 2:3], run[:, 0:1], st[R32, 1:2], ALU.subtract, [run, st], [st])
        self.act(st[R32, 3:4], st[R32, 2:3], AF.Exp, [st], [st], scale=MLA_SCALE)
        self.ts(st[R32, 4:5], st[R32, 1:2], -MLA_SCALE, None, ALU.mult, None, [st], [st])
        self.act(Ps[:, 0:n], S_ap, AF.Exp, sbufs + [st], [Ps, st], bias=st[R32, 4:5], scale=MLA_SCALE, accum=st[R32, 5:6])
        self.stt(run[:, 1:2], run[:, 1:2], st[R32, 3:4], st[R32, 5:6], ALU.mult, ALU.add, [run, st], [run])
        self.cp(run[:, 0:1], st[R32, 1:2], [st], [run], eng='dve')
        return st

    Qs = p.sb("Qs", [128, 3, 4, 32], BF16)
    for c in range(2):
        self.cp(Qs[:, c, :, :].rearrange("p b (t h) -> p b t h", h=8), QlT_s[:, :, c, :].rearrange("p h (b t) -> p b t h", b=4), [QlT_s], [Qs], eng='dve')
    self.cp(Qs[:, 2, :, :].rearrange("p b (t h) -> p b t h", h=8), Qpad_s[:, :, :].rearrange("p h (b t) -> p b t h", b=4), [Qpad_s], [Qs], eng='dve')
    segs = [(b, s_) for b in range(4) for s_ in range(8)]

    def prep(k):
        b, s_ = segs[k]
        g = gbuf[k % 2]
        kb_ = kb16[k % 2]
        KTg = KTsegs[k % 2]

        def gfn(e, g=g, b=b, s_=s_):
            return e.indirect_dma_start(out=g[:, :, :].rearrange("p t c -> p (t c)"), out_offset=None, in_=cache[:, :],
                                        in_offset=bass.IndirectOffsetOnAxis(ap=idx[:, b, s_:s_ + 1], axis=0))
        p.dma_custom('pool', gfn, g, True, reads=[idx])
        self.cp(kb_[:, 0:8, 0:320], g[:, 0:8, :], [g], [kb_], eng='dve')
        self.cp(kb_[:, 8:16, 0:320], g[:, 8:16, :], [g], [kb_], eng='act')
        self.cp(kb_[:, :, 320:384], g[:, :, 256:320], [g], [kb_], eng='dve')

    def prepT(k):
        kb_ = kb16[k % 2]
        KTg = KTsegs[k % 2]
        for ch in range(3):
            for k0 in (0, 8):
                pb = PB[kbank[self.kbi % 2]]
                self.kbi += 1
                pv = pb.ap.bitcast(BF16).rearrange("p (k n) -> p k n", k=8)
                for kt in range(k0, k0 + 8):
                    p.tr(pb, pv[:, kt - k0, :], kb_[:, kt, ch * 128:(ch + 1) * 128], self.identb[:, :], [kb_, self.identb])
                self.cp(KTg[:, ch, k0 * 128:(k0 + 8) * 128], pv[:, :, :].rearrange("p k n -> p (k n)"), [pb], [KTg])

    def sc(k):
        b, s_ = segs[k]
        KTg = KTsegs[k % 2]
        if s_ == 0:
            p.op('pool', lambda e: e.memset(Oacc[:, :], 0.0), writes=[Oacc])
            p.op('pool', lambda e: e.memset(run[:, 0:1], -1e30), writes=[run])
            p.op('pool', lambda e: e.memset(run[:, 1:2], 0.0), writes=[run])
        Qc = [Qs[:, c, b, :] for c in range(3)]
        sbufs = []
        for kq in range(4):
            bk = PB[kq]
            sbufs.append(bk)
            self.mm(bk, bk[R32, 0:512], [(Qc[c], KTg[:, c, kq * 512:(kq + 1) * 512]) for c in range(3)], [Qs, KTg])
        return flash(p.ps_all[R32, 0:2048], sbufs, 2048)

    def pvs(k, st):
        b, s_ = segs[k]
        kb_ = kb16[k % 2]
        pb = self.bank()
        pv = pb.ap.bitcast(BF16)[:, 0:512].rearrange("p (k n) -> p k n", k=16)
        for kt in range(16):
            p.tr(pb, pv[:, kt, :], Ps[:, kt * 128:(kt + 1) * 128], self.identb[R32, 0:32], [Ps, self.identb])
        self.cp(PTq[:, :, :], pv[:, :, :], [pb], [PTq])
        po = self.bank()
        self.mm(po, po[R32, 0:256], [(PTq[:, kt, :], kb_[:, kt, 0:256]) for kt in range(16)], [PTq, kb_])
        self.stt(Oacc[:, :], Oacc[:, :], st[R32, 3:4], po[R32, 0:256], ALU.mult, ALU.add, [Oacc, st, po], [Oacc])

    def finish_b(b):
        Qc = [Qs[:, c, b, :] for c in range(3)]
        pbn = self.bank()
        self.mm(pbn, pbn[R32, 0:16], [(Qc[c], KTs[:, c, 0:16]) for c in range(3)], [Qs, KTs])
        self.tt(sn[:, :], pbn[R32, 0:16], mask4[:, b, :], ALU.add, [pbn, mask4], [sn])
        st = flash(sn[:, :], [sn], 16)
        pb = self.bank()
        pvn = pb.ap.bitcast(BF16)
        p.tr(pb, pvn[0:16, 0:32], Ps[:, 0:16], self.identb[R32, 0:32], [Ps, self.identb])
        self.cp(PTq[0:16, 0, :], pvn[0:16, 0:32], [pb], [PTq])
        po = self.bank()
        self.mm(po, po[R32, 0:256], [(PTq[0:16, 0, :], Vs[0:16, 0, :])], [PTq, Vs])
        self.stt(Oacc[:, :], Oacc[:, :], st[R32, 3:4], po[R32, 0:256], ALU.mult, ALU.add, [Oacc, st, po], [Oacc])
        st = newstat()
        p.op('dve', (lambda st: (lambda e: e.reciprocal(st[R32, 0:1], run[:, 1:2])))(st), reads=[run], writes=[st])
        self.ts(Onb[:, :], Oacc[:, :], st[R32, 0:1], None, ALU.mult, None, [Oacc, st], [Onb])
        pb = self.bank()
        pv = pb.ap.bitcast(BF16)[:, 0:64].rearrange("p (c n) -> p c n", c=2)
        for cc in range(2):
            p.tr(pb, pv[:, cc, :], Onb[:, cc * 128:(cc + 1) * 128], self.identb[R32, 0:32], [Onb, self.identb])
        self.cp(OTq[:, :, :], pv[:, :, :], [pb], [OTq])
        pvv = self.bank()
        for h in range(8):
            self.mm(pvv, pvv[:, h * 4:(h + 1) * 4], [(Wuv[:, cc, h * 128:(h + 1) * 128], OTq[:, cc, h::8]) for cc in range(2)], [Wuv, OTq])
        self.cp(ovT_s[:, :, b * 4:(b + 1) * 4], pvv[:, 0:32].rearrange("p (h t) -> p h t", h=8), [pvv], [ovT_s], eng='dve')

    self.kbi = 0
    prep(0)
    prepT(0)
    prep(1)
    for k in range(len(segs)):
        st = sc(k)
        if k + 1 < len(segs):
            prepT(k + 1)
        pvs(k, st)
        if segs[k][1] == 7:
            finish_b(segs[k][0])
        if k + 2 < len(segs):
            prep(k + 2)
    self.bank_rr = list(range(8))
    for hf in range(2):
        pb = self.bank()
        self.mm(pb, pb[0:16, :], [(ovT_s[:, h, 0:16], Wo[:, h, hf * 512:(hf + 1) * 512]) for h in range(8)], [ovT_s, Wo])
        self.add_H(ST, 16, hf, pb)
    p.release(m)


@_add_methods(K)
def c_setup(self):
    p = self.p
    din = self.din
    self.c_cw = [self.load_col("cw%d" % k, din['c_conv_w'][0, k, :], 10) for k in range(4)]
    self.c_cb = self.load_col("cb", din['c_conv_b'][0, :], 10)
    self.c_ba = self.load_col("ba", din['c_b_a'][0, :], 10)
    self.c_bi = self.load_col("bi", din['c_b_i'][0, :], 10)
    self.c_lam = self.load_col("lam", din['c_lambda'][0, :], 10)
    tail_p = self.c_tail_p = p.sb("tail_p", [128, 10, 1, 3], F32)
    hst_p = self.c_hst_p = p.sb("hst_p", [128, 10, 1], F32)
    tail_s = self.c_tail_s = p.sb("tail_s", [128, 10, 4, 3], F32)
    hst_s = self.c_hst_s = p.sb("hst_s", [128, 10, 4], F32)
    p.op('pool', lambda e: e.memset(tail_p[:, :, :, :], 0.0), writes=[tail_p])
    p.op('pool', lambda e: e.memset(hst_p[:, :, :], 0.0), writes=[hst_p])
    for b in range(4):
        for k in range(3):
            def fn(e, b=b, k=k):
                return e.dma_start(out=tail_s[:, :, b, k], in_=din['st_conv'][b, k, :].rearrange("(t p) -> p t", p=128), allow_slow_non_contiguous=True)
            p.dma_custom('sp', fn, tail_s, True)

        def fn2(e, b=b):
            return e.dma_start(out=hst_s[:, :, b], in_=din['st_h'][b, :].rearrange("(t p) -> p t", p=128), allow_slow_non_contiguous=True)
        p.dma_custom('sp', fn2, hst_s, True)


@_add_methods(K)
def c_state_out(self):
    p = self.p
    dout = self.dout
    tail_p, hst_p, tail_s, hst_s = self.c_tail_p, self.c_hst_p, self.c_tail_s, self.c_hst_s

    def st_out(dst, src, buf):
        def fn(e):
            return e.dma_start(out=dst.rearrange("(t p) -> p t", p=128), in_=src, allow_slow_non_contiguous=True)
        p.dma_custom('sp', fn, buf, False)
    st_out(dout['lru_h_p'], hst_p[:, :, 0], hst_p)
    for k in range(3):
        st_out(dout['conv_p'][k, :], tail_p[:, :, 0, k], tail_p)
    for b in range(4):
        st_out(dout['lru_h_s'][b, :], hst_s[:, :, b], hst_s)
        for k in range(3):
            st_out(dout['conv_s'][b, k, :], tail_s[:, :, b, k], tail_s)

DEPTH_RUN = 4
NC_RUN = 8
_CACHE = {}

W_SHAPES = {
    'norm_mix': (4, 1024), 'norm_ffn': (4, 1024), 'norm_out': (1024,),
    'a_w_in': (2, 1024, 2048), 'a_ln_g': (2, 1024), 'a_ln_b': (2, 1024), 'a_w_s': (2, 8, 128, 128),
    'a_b_s': (2, 8, 128), 'a_w_out': (2, 1024, 1024),
    'b_w_dq': (1, 1024, 384), 'b_q_norm': (1, 384), 'b_w_uq': (1, 384, 1536), 'b_w_dkv': (1, 1024, 320),
    'b_kv_norm': (1, 256), 'b_w_uk': (1, 256, 8, 128), 'b_w_uv': (1, 256, 8, 128), 'b_w_out': (1, 1024, 1024),
    'c_w_x': (1, 1024, 1280), 'c_w_gate': (1, 1024, 1280), 'c_conv_w': (1, 4, 1280), 'c_conv_b': (1, 1280),
    'c_w_a': (1, 10, 128, 128), 'c_b_a': (1, 1280), 'c_w_i': (1, 10, 128, 128), 'c_b_i': (1, 1280),
    'c_lambda': (1, 1280), 'c_w_out': (1, 1280, 1024),
    'f_w_gate': (4, 1024, 2816), 'f_w_up': (4, 1024, 2816), 'f_w_down': (4, 2816, 1024),
}


def build(depth):
    k = K(depth)
    p = k.p
    k.inp('xp', [2048, 1024])
    k.inp('xs', [16, 1024])
    k.inp('cache', [NPOOL * 8, TSEG * 320])
    k.inp('pt', [4, 128], I32)
    k.inp('st_h', [4, 1280])
    k.inp('st_conv', [4, 3, 1280])
    k.inp('invf', [32])
    for n, s in W_SHAPES.items():
        k.inp(n, s)
    k.outp('y_p', [2048, 1024])
    k.outp('y_s', [16, 1024])
    k.outp('chunk_v', [2, 16, 1024])
    k.outp('mla_p', [2048, 320])
    k.outp('mla_s', [16, 320])
    k.outp('lru_h_p', [1280])
    k.outp('lru_h_s', [4, 1280])
    k.outp('conv_p', [3, 1280])
    k.outp('conv_s', [4, 3, 1280])
    k.setup_consts()
    k.HB = [p.sb("H%d" % t, [128, 1024], F32) for t in range(NT)]
    k.xn_bufs = [p.sb("xn%d" % i, [128, 1024], BF16) for i in range(2)]
    k.xn_i = 0
    for t in range(16):
        p.dma('sp', k.HB[t][:, :], k.din['xp'][t * 128:(t + 1) * 128, :], k.HB[t], True)
    p.dma('sp', k.HB[ST][0:16, :], k.din['xs'], k.HB[ST], True)
    k.mid_hook = None
    if depth >= 3:
        k.c_setup()
    for layer in range(depth):
        kind, j = layer % 3, layer // 3
        if kind == 0:
            k.mixer_A(layer, j)
        elif kind == 1:
            k.mixer_B(layer, j)
        else:
            k.mixer_C(layer, j)
            k.mid_hook = k.c_state_out
        k.ffn(layer)
    k.final()
    p.finish()
    p.emit()
    return k


def kernel(**inputs):
    depth = DEPTH_RUN
    if depth not in _CACHE:
        _CACHE[depth] = build(depth)
    k = _CACHE[depth]
    f32 = np.float32
    xp = np.ascontiguousarray(np.asarray(inputs['x_prompt'], dtype=f32))
    xs = np.ascontiguousarray(np.asarray(inputs['x_sample'], dtype=f32))
    cache = np.ascontiguousarray(np.asarray(inputs['cache_mla'], dtype=f32)).reshape(NPOOL * 8, TSEG * 320)
    pt = np.ascontiguousarray(np.asarray(inputs['page_table']).astype(np.int32))
    sth = np.ascontiguousarray(np.asarray(inputs['state_lru_h'], dtype=f32))
    stc = np.ascontiguousarray(np.asarray(inputs['state_lru_conv'], dtype=f32))
    invf = (1.0 / (np.float32(10000.0) ** (np.arange(0, 64, 2, dtype=f32) / np.float32(64)))).astype(f32)
    ws = {n: np.ascontiguousarray(np.asarray(inputs[n], dtype=f32)) for n in W_SHAPES}
    in_maps = []
    for c in range(NC_RUN):
        d = dict(ws)
        d['xp'] = xp[c]
        d['xs'] = xs[4 * c:4 * c + 4].reshape(16, 1024)
        d['cache'] = cache
        d['pt'] = pt[4 * c:4 * c + 4]
        d['st_h'] = sth[0, 4 * c:4 * c + 4]
        d['st_conv'] = stc[0, 4 * c:4 * c + 4]
        d['invf'] = invf
        in_maps.append(d)
    res = run_bass_kernel_spmd(k.nc, in_maps, core_ids=list(range(NC_RUN)))
    R = list(res.results)
    while len(R) < 8:
        R.append(R[0])
    y_p = np.stack([R[c]['y_p'] for c in range(8)])
    y_s = np.concatenate([R[c]['y_s'].reshape(4, 4, 1024) for c in range(8)])
    chunk_v = np.concatenate([R[c]['chunk_v'].reshape(2, 4, 4, 1024) for c in range(8)], axis=1)
    mla_p = np.stack([R[c]['mla_p'] for c in range(8)])[None]
    mla_s = np.concatenate([R[c]['mla_s'].reshape(4, 4, 320) for c in range(8)])[None]
    lru_h_p = np.stack([R[c]['lru_h_p'] for c in range(8)])[None]
    lru_h_s = np.concatenate([R[c]['lru_h_s'] for c in range(8)])[None]
    conv_p = np.stack([R[c]['conv_p'] for c in range(8)])[None]
    conv_s = np.concatenate([R[c]['conv_s'] for c in range(8)])[None]
    outs = (y_p, y_s, chunk_v, mla_p, mla_s, lru_h_p, lru_h_s, conv_p, conv_s)
    return tuple(np.ascontiguousarray(o.astype(np.float32)) for o in outs)
```

```python
import numpy as np
import concourse.bass as bass
import concourse.mybir as mybir
from concourse.bass_utils import run_bass_kernel_spmd

F32 = mybir.dt.float32
BF16 = mybir.dt.bfloat16
I32 = mybir.dt.int32
U32 = mybir.dt.uint32
U8 = mybir.dt.uint8
AF = mybir.ActivationFunctionType
ALU = mybir.AluOpType
AX = mybir.AxisListType
DTSIZE = {F32: 4, BF16: 2, I32: 4, U32: 4, U8: 1}
ENGS = ['pe', 'act', 'dve', 'pool', 'sp']


class Op:
    __slots__ = ('eng', 'kind', 'fn', 'deps', 'needed', 'idx', 'sem', 'count', 'pos')

    def __init__(self, eng, kind, fn):
        self.eng = eng
        self.kind = kind
        self.fn = fn
        self.deps = []
        self.needed = False
        self.idx = 0
        self.sem = None
        self.count = 0
        self.pos = 0


class Buf:
    def __init__(self, name, ap, start, end):
        self.name = name
        self.ap = ap
        self.start = start
        self.end = end
        self.last_write = None
        self.readers = []
        self.dsem = None
        self.dcount = 0

    def __getitem__(self, k):
        return self.ap[k]


def _prune(lst):
    best = {}
    for r in lst:
        if r is None:
            continue
        if r.kind == 'c':
            key = ('e', r.eng)
            if key not in best or best[key].pos < r.pos:
                best[key] = r
        else:
            key = ('d', r.sem)
            if key not in best or best[key].count < r.count:
                best[key] = r
    return list(best.values())


class Prog:
    def __init__(self, nc, arena_bytes=206 * 1024, n_dsem=70):
        self.nc = nc
        self.arena = nc.alloc_sbuf_tensor("arena", [128, arena_bytes], U8)
        self.arena_bytes = arena_bytes
        self.top = 0
        self.live = []
        self.retired = []
        self.ops = {e: [] for e in ENGS}
        self.n_dsem = n_dsem
        self.free_dsem = [(i, 0) for i in range(n_dsem)]
        self.dma_bufs = []
        self.banks = []
        self.PB = []
        self.ps_all = nc.alloc_psum_tensor("ps_all", [128, 4096], F32)
        for i in range(8):
            self.PB.append(Buf("psb%d" % i, self.ps_all[:, i * 512:(i + 1) * 512], i * 2048, (i + 1) * 2048))
        self.maxtop = 0

    def sb(self, name, shape, dtype, align=64):
        nbytes = int(np.prod(shape[1:])) * DTSIZE[dtype]
        start = (self.top + align - 1) // align * align
        end = start + nbytes
        assert end <= self.arena_bytes, "SBUF arena overflow at %s: %d" % (name, end)
        self.top = end
        self.maxtop = max(self.maxtop, end)
        np_ = shape[0]
        v = self.arena[0:np_, start:end].bitcast(dtype)
        if len(shape) > 2:
            names = ["d%d" % i for i in range(len(shape) - 1)]
            pat = "p (" + " ".join(names) + ") -> p " + " ".join(names)
            kw = {names[i]: shape[i + 1] for i in range(len(names) - 1)}
            v = v.rearrange(pat, **kw)
        b = Buf(name, v, start, end)
        inh = []
        keep = []
        for o in self.retired:
            if o.start < end and start < o.end:
                inh.append(o.last_write)
                inh.extend(o.readers)
                if not (start <= o.start and o.end <= end):
                    keep.append(o)
            else:
                keep.append(o)
        self.retired = keep
        b.readers = _prune(inh)
        self.live.append(b)
        return b

    def mark(self):
        return (self.top, len(self.live))

    def release(self, m):
        top, n = m
        for b in self.live[n:]:
            self.retired.append(b)
            if b.dsem is not None:
                self.free_dsem.append((b.dsem, b.dcount))
                b.dsem_released = True
        self.live = self.live[:n]
        self.top = top

    def _dep_on(self, O, P):
        if P is None:
            return
        if P.kind == 'c':
            if P.eng == 'pe' and O.eng == 'pe' and O.kind == 'c':
                return
            P.needed = True
        O.deps.append(P)

    def _record(self, O, reads, writes):
        for b in reads:
            self._dep_on(O, b.last_write)
        for b in writes:
            self._dep_on(O, b.last_write)
            for r in b.readers:
                self._dep_on(O, r)
        for b in writes:
            b.last_write = O
            b.readers = []
        for b in reads:
            if b in writes:
                continue
            if O.kind == 'c':
                b.readers = [r for r in b.readers if not (r.kind == 'c' and r.eng == O.eng)]
            else:
                b.readers = [r for r in b.readers if not (r.kind == 'd' and r.sem == O.sem)]
            b.readers.append(O)
        O.pos = len(self.ops[O.eng])
        self.ops[O.eng].append(O)

    def op(self, eng, fn, reads=(), writes=()):
        O = Op(eng, 'c', fn)
        self._record(O, list(reads), list(writes))
        return O

    def _buf_dsem(self, b):
        if b.dsem is None:
            assert self.free_dsem, "out of dma semaphores"
            i, c = self.free_dsem.pop(0)
            b.dsem = i
            b.dcount = c
            if c > 0:
                m = Op('sp', 'd', None)
                m.sem = i
                m.count = c
                b.readers.append(m)
            self.dma_bufs.append(b)

    def dma(self, q, out_ap, in_ap, buf, is_write, reads=(), writes=(), **kw):
        self._buf_dsem(buf)
        buf.dcount += 1

        def fn(eng):
            return eng.dma_start(out=out_ap, in_=in_ap, **kw)
        O = Op(q, 'd', fn)
        O.sem = buf.dsem
        O.count = buf.dcount
        rd = list(reads)
        wr = list(writes)
        if is_write:
            wr.append(buf)
        else:
            rd.append(buf)
        self._record(O, rd, wr)
        return O

    def dma_custom(self, q, fn, buf, is_write, reads=(), writes=()):
        self._buf_dsem(buf)
        buf.dcount += 1
        O = Op(q, 'd', fn)
        O.sem = buf.dsem
        O.count = buf.dcount
        rd = list(reads)
        wr = list(writes)
        if is_write:
            wr.append(buf)
        else:
            rd.append(buf)
        self._record(O, rd, wr)
        return O

    def finish(self):
        O = Op('sp', 'c', lambda eng: eng.nop())
        seen = {}
        for b in self.dma_bufs:
            m = Op('sp', 'd', None)
            m.sem = b.dsem
            m.count = b.dcount
            if b.dsem not in seen or seen[b.dsem].count < m.count:
                seen[b.dsem] = m
        O.deps.extend(seen.values())
        O.pos = len(self.ops['sp'])
        self.ops['sp'].append(O)

    def mm(self, out_buf, out_ap, pairs, reads):
        pairs = list(pairs)

        def fn(pe):
            n = len(pairs)
            ins = None
            for i, (l, r) in enumerate(pairs):
                ins = pe.matmul(out_ap, l, r, start=(i == 0), stop=(i == n - 1))
            return ins
        return self.op('pe', fn, reads=reads, writes=[out_buf])

    def tr(self, out_buf, out_ap, in_ap, ident_ap, reads):
        def fn(pe):
            return pe.transpose(out_ap, in_ap, ident_ap)
        return self.op('pe', fn, reads=reads, writes=[out_buf])

    def emit(self):
        nc = self.nc
        for e in ENGS:
            n = 0
            for o in self.ops[e]:
                if o.kind == 'c' and o.needed:
                    n += 1
                    o.idx = n
        from contextlib import ExitStack
        with ExitStack() as st:
            esem = {e: st.enter_context(nc.semaphore("es_" + e)) for e in ENGS}
            dsem = [st.enter_context(nc.semaphore("ds_%d" % i)) for i in range(self.n_dsem)]
            block = st.enter_context(nc.Block())

            def run(ename, eng):
                waited = {}
                for o in self.ops[ename]:
                    need = {}
                    for p in o.deps:
                        if p.kind == 'c':
                            key = ('e', p.eng)
                            val = p.idx
                        else:
                            key = ('d', p.sem)
                            val = p.count * 16
                        if need.get(key, 0) < val:
                            need[key] = val
                    for key, val in need.items():
                        if waited.get(key, 0) >= val:
                            continue
                        waited[key] = val
                        sem = esem[key[1]] if key[0] == 'e' else dsem[key[1]]
                        eng.wait_ge(sem, val)
                    ins = o.fn(eng)
                    if o.kind == 'c':
                        if o.needed:
                            ins.then_inc(esem[ename], 1)
                    else:
                        ins.then_inc(dsem[o.sem], 16)

            @block.tensor
            def _(e):
                run('pe', e)

            @block.scalar
            def _(e):
                run('act', e)

            @block.vector
            def _(e):
                run('dve', e)

            @block.gpsimd
            def _(e):
                run('pool', e)

            @block.sync
            def _(e):
                run('sp', e)

import math

D = 1024
DFF = 2816
NT = 17
ST = 16
SEQ = 2048
NPAGES = 128
NPOOL = 5120
TSEG = 16
MLA_SCALE = (128 + 64) ** -0.5
NEG = -30000.0
TWO_PI = 2.0 * math.pi
C1 = 6.28125
C2 = TWO_PI - 6.28125


class K:
    def __init__(self, depth=4):
        self.depth = depth
        nc = bass.Bass("TRN2", target_bir_lowering=False)
        self.nc = nc
        self.p = Prog(nc)
        self.bank_rr = list(range(8))
        self.bank_i = 0
        self.ev_i = 0
        self.din = {}
        self.dout = {}

    def inp(self, name, shape, dt=F32):
        a = self.nc.dram_tensor(name, list(shape), dt, kind="ExternalInput").ap()
        self.din[name] = a
        return a

    def outp(self, name, shape):
        a = self.nc.dram_tensor(name, list(shape), F32, kind="ExternalOutput").ap()
        self.dout[name] = a
        return a

    def bank(self):
        b = self.p.PB[self.bank_rr[self.bank_i % len(self.bank_rr)]]
        self.bank_i += 1
        return b

    def act(self, out, in_, func, reads, writes, bias=None, scale=None, accum=None):
        kw = {}
        if bias is not None:
            kw['bias'] = bias
        if scale is not None:
            kw['scale'] = scale
        if accum is not None:
            kw['accum_out'] = accum
        return self.p.op('act', lambda e: e.activation(out, in_, func, **kw), reads=reads, writes=writes)

    def ts(self, out, in0, s1, s2, op0, op1, reads, writes, eng='dve'):
        if s2 is None:
            return self.p.op(eng, lambda e: e.tensor_scalar(out, in0, s1, None, op0), reads=reads, writes=writes)
        return self.p.op(eng, lambda e: e.tensor_scalar(out, in0, s1, s2, op0, op1), reads=reads, writes=writes)

    def tt(self, out, a, b, op, reads, writes, eng='dve'):
        return self.p.op(eng, lambda e: e.tensor_tensor(out, a, b, op), reads=reads, writes=writes)

    def stt(self, out, in0, scalar, in1, op0, op1, reads, writes, eng='dve'):
        return self.p.op(eng, lambda e: e.scalar_tensor_tensor(out, in0, scalar, in1, op0, op1), reads=reads, writes=writes)

    def cp(self, out, in_, reads, writes, eng=None):
        if eng is None:
            eng = 'act' if (self.ev_i % 2 == 0) else 'dve'
            self.ev_i += 1
        if eng == 'act':
            return self.p.op('act', lambda e: e.copy(out, in_), reads=reads, writes=writes)
        return self.p.op(eng, lambda e: e.tensor_copy(out, in_), reads=reads, writes=writes)

    def mm(self, ob, out_ap, pairs, reads):
        return self.p.mm(ob, out_ap, pairs, reads)

    def mm_multi(self, ob, triples, reads):
        triples = list(triples)

        def fn(pe):
            ins = None
            for (o, l, r, st, sp) in triples:
                ins = pe.matmul(o, l, r, start=st, stop=sp)
            return ins
        return self.p.op('pe', fn, reads=reads, writes=[ob])

    def load_w(self, name, src, ktiles, cols, q='pool', split=None):
        b = self.p.sb(name, [128, ktiles, cols], BF16)
        v = src.rearrange("(k p) n -> p k n", p=128)
        if split is None:
            split = max(1, (ktiles * cols * 128 * 4) // (2 << 20))
        split = min(split, ktiles)
        step = (ktiles + split - 1) // split
        for k0 in range(0, ktiles, step):
            k1 = min(ktiles, k0 + step)
            self.p.dma(q, b[:, k0:k1, :], v[:, k0:k1, :], b, True)
        return b

    def load_w_into(self, b, src, ktiles, q='pool'):
        v = src.rearrange("(k p) n -> p k n", p=128)
        self.p.dma(q, b[:, 0:ktiles, 0:v.shape[2]], v, b, True)

    def load_bc(self, name, row_ap, n, np_=128):
        b = self.p.sb(name, [128, n], F32)
        self.p.dma('sp', b[0:np_, :], row_ap.partition_broadcast(np_), b, True)
        return b

    def load_col(self, name, vec_ap, kt):
        b = self.p.sb(name, [128, kt], F32)

        def fn(e):
            return e.dma_start(out=b[:, :], in_=vec_ap.rearrange("(t p) -> p t", p=128), allow_slow_non_contiguous=True)
        self.p.dma_custom('sp', fn, b, True)
        return b

    def setup_consts(self):
        p = self.p
        ones = p.sb("ones", [128, 128], F32)
        identf = self.identf = p.sb("identf", [128, 128], F32)
        self.identb = p.sb("identb", [128, 128], BF16)
        p.op('pool', lambda e: e.memset(ones[:, :], 1.0), writes=[ones])
        p.op('pool', lambda e: e.affine_select(identf[:, :], ones[:, :], [[-1, 128]], ALU.is_equal, 0.0, base=0, channel_multiplier=1), reads=[ones], writes=[identf])
        p.op('pool', lambda e: e.tensor_copy(self.identb[:, :], identf[:, :]), reads=[identf], writes=[self.identb])
        zer = p.sb("zer", [128, 128], F32)
        maskf = p.sb("maskf", [128, 128], F32)
        self.maskb = p.sb("maskb", [128, 128], BF16)
        p.op('pool', lambda e: e.memset(zer[:, :], 0.0), writes=[zer])
        p.op('pool', lambda e: e.affine_select(maskf[:, :], zer[:, :], [[-1, 128]], ALU.is_ge, NEG, base=0, channel_multiplier=1), reads=[zer], writes=[maskf])
        p.op('pool', lambda e: e.tensor_copy(self.maskb[:, :], maskf[:, :]), reads=[maskf], writes=[self.maskb])
        self.mask_s = p.sb("mask_s", [128, 4], F32)
        p.op('pool', lambda e: e.affine_select(self.mask_s[:, :], zer[:, 0:4], [[-8, 4]], ALU.is_ge, NEG, base=0, channel_multiplier=1), reads=[zer], writes=[self.mask_s])
        self.cst = p.sb("cst", [128, 8], F32)
        vals = [1e-6, 1e-5, 1.0, 0.0, -0.5, 0.5, 0.0, 0.0]
        for i, v in enumerate(vals):
            p.op('pool', (lambda i, v: (lambda e: e.memset(self.cst[:, i:i + 1], v)))(i, v), writes=[self.cst])
        self.eps_rms = self.cst[:, 0:1]
        self.eps_ln = self.cst[:, 1:2]
        self.one_col = self.cst[:, 2:3]
        self.junk = p.sb("junk", [128, 1024], BF16)
        self.ss = p.sb("ss", [128, 24], F32)
        self.sd = p.sb("sd", [128, 24], F32)
        self.rstd = p.sb("rstd", [128, 24], F32)
        p.op('pool', lambda e: e.memset(self.ss[:, :], 1.0), writes=[self.ss])

    def rope_tables(self, name, np_, nt, base, chan_mul, tstep):
        p = self.p
        m = p.mark()
        posi = p.sb(name + "_pi", [128, nt], I32)
        posf = p.sb(name + "_pf", [128, nt], F32)
        ang = p.sb(name + "_ang", [128, nt, 32], F32)
        q = p.sb(name + "_q", [128, nt, 32], F32)
        qi = p.sb(name + "_qi", [128, nt, 32], I32)
        r = p.sb(name + "_r", [128, nt, 32], F32)
        t1 = p.sb(name + "_t1", [128, nt, 32], F32)
        rc = p.sb(name + "_rc", [128, nt, 32], F32)
        cos = self.p_persist(name + "_cos", [128, nt, 32], F32)
        sin = self.p_persist(name + "_sin", [128, nt, 32], F32)
        P = slice(0, np_)
        p.op('pool', lambda e: e.iota(posi[P, :], [[tstep, nt]], base=base, channel_multiplier=chan_mul), writes=[posi])
        p.op('dve', lambda e: e.tensor_copy(posf[P, :], posi[P, :]), reads=[posi], writes=[posf])
        invb = self.invf_bc
        self.tt(ang[P, :, :], posf[P, :].unsqueeze(2).broadcast_to([np_, nt, 32]),
                invb[P, :].unsqueeze(1).broadcast_to([np_, nt, 32]), ALU.mult, [posf, invb], [ang])

        def reduce_to(dst, src_buf, src, shift):
            self.ts(q[P], src, shift, 1.0 / TWO_PI, ALU.add, ALU.mult, [src_buf], [q])
            p.op('dve', lambda e: e.tensor_copy(qi[P], q[P]), reads=[q], writes=[qi])
            p.op('dve', lambda e: e.tensor_copy(q[P], qi[P]), reads=[qi], writes=[q])
            self.ts(t1[P], src, shift, None, ALU.add, None, [src_buf], [t1])
            self.stt(t1[P], q[P], -C1, t1[P], ALU.mult, ALU.add, [q, t1], [t1])
            self.stt(t1[P], q[P], -C2, t1[P], ALU.mult, ALU.add, [q, t1], [t1])
            self.ts(q[P], t1[P], math.pi, -TWO_PI, ALU.is_gt, ALU.mult, [t1], [q])
            self.tt(t1[P], t1[P], q[P], ALU.add, [t1, q], [t1])
            self.ts(q[P], t1[P], -math.pi, TWO_PI, ALU.is_lt, ALU.mult, [t1], [q])
            self.tt(t1[P], t1[P], q[P], ALU.add, [t1, q], [t1])
            self.ts(dst[P], t1[P], 3.141592, -3.141592, ALU.min, ALU.max, [t1], [dst])
        reduce_to(r, ang, ang[P], 0.0)
        self.act(sin[P], r[P], AF.Sin, [r], [sin])
        reduce_to(rc, ang, ang[P], math.pi / 2)
        self.act(cos[P], rc[P], AF.Sin, [rc], [cos])
        return cos, sin

    def p_persist(self, name, shape, dt):
        return self.p.sb(name, shape, dt)

    def norm_tiles(self, tiles, gain_bc, dst, col0s):
        p = self.p
        HB = self.HB
        for t in tiles:
            np_ = 16 if t == ST else 128
            self.act(self.junk[0:np_, :], HB[t][0:np_, :], AF.Square, [HB[t]], [self.junk, self.ss], accum=self.ss[0:np_, t:t + 1])
        lo, hi = min(tiles), max(tiles) + 1
        self.act(self.sd[:, lo:hi], self.ss[:, lo:hi], AF.Sqrt, [self.ss], [self.sd], bias=self.eps_rms, scale=1.0 / D)
        p.op('dve', lambda e: e.reciprocal(self.rstd[:, lo:hi], self.sd[:, lo:hi]), reads=[self.sd], writes=[self.rstd])
        for t, c0 in zip(tiles, col0s):
            np_ = 16 if t == ST else 128
            xn = self.xn_bufs[self.xn_i % 2]
            self.xn_i += 1
            self.stt(xn[0:np_, :], HB[t][0:np_, :], self.rstd[0:np_, t:t + 1], gain_bc[0:np_, :], ALU.mult, ALU.mult,
                     [HB[t], self.rstd, gain_bc], [xn])
            pb = self.bank()
            pv = pb.ap.bitcast(BF16).rearrange("p (k n) -> p k n", k=8)
            for k in range(8):
                p.tr(pb, pv[:, k, 0:np_], xn[0:np_, k * 128:(k + 1) * 128], self.identb[0:np_, 0:np_], [xn, self.identb])
            self.cp(dst[:, :, c0:c0 + np_], pv[:, :, 0:np_], [pb], [dst])

    def add_H(self, t, np_, hf, pb):
        HB = self.HB
        self.tt(HB[t][0:np_, hf * 512:(hf + 1) * 512], HB[t][0:np_, hf * 512:(hf + 1) * 512], pb[0:np_, :], ALU.add,
                [HB[t], pb], [HB[t]])

    def ffn(self, layer):
        p = self.p
        m = p.mark()
        Wg0 = p.sb("Wg0", [128, 8, 512], BF16)
        Wu0 = p.sb("Wu0", [128, 8, 512], BF16)
        Wd0 = p.sb("Wd0", [128, 4, 1024], BF16)
        gain = self.load_bc("gF", self.din['norm_ffn'][layer, :], D)
        xnT = p.sb("xnT", [128, 8, 2064], BF16)
        chunks = [(0, 4), (4, 4), (8, 4), (12, 4), (16, 3), (19, 3)]
        Wg = [Wg0, p.sb("Wg1", [128, 8, 512], BF16)]
        Wu = [Wu0, p.sb("Wu1", [128, 8, 512], BF16)]
        Wd = [Wd0, p.sb("Wd1", [128, 4, 1024], BF16)]
        actT = p.sb("actT", [128, 4, 2064], BF16)
        sg = [p.sb("sg%d" % i, [128, 512], F32) for i in range(2)]
        wg_d = self.din['f_w_gate'][layer]
        wu_d = self.din['f_w_up'][layer]
        wd_d = self.din['f_w_down'][layer]
        groups = [(g * 512, 512) for g in range(4)] + [(2048, 16)]

        def load(ci):
            f0, nf = chunks[ci]
            s = ci % 2
            self.load_w_into(Wg[s], wg_d[:, f0 * 128:(f0 + nf) * 128], 8)
            self.load_w_into(Wu[s], wu_d[:, f0 * 128:(f0 + nf) * 128], 8)
            self.load_w_into(Wd[s], wd_d[f0 * 128:(f0 + nf) * 128, :], nf)
        load(0)
        self.norm_tiles(list(range(NT)), gain, xnT, [t * 128 for t in range(NT)])
        si = 0
        for ci, (f0, nf) in enumerate(chunks):
            s = ci % 2
            if ci + 1 < len(chunks):
                load(ci + 1)
            if ci == 3 and self.mid_hook is not None:
                self.mid_hook()
                self.mid_hook = None
            for fi in range(nf):
                for (c0, n) in groups:
                    pg = self.bank()
                    pu = self.bank()
                    self.mm(pg, pg[:, 0:n], [(Wg[s][:, k, fi * 128:(fi + 1) * 128], xnT[:, k, c0:c0 + n]) for k in range(8)], [Wg[s], xnT])
                    self.mm(pu, pu[:, 0:n], [(Wu[s][:, k, fi * 128:(fi + 1) * 128], xnT[:, k, c0:c0 + n]) for k in range(8)], [Wu[s], xnT])
                    sgb = sg[si % 2]
                    si += 1
                    self.act(sgb[:, 0:n], pg[:, 0:n], AF.Silu, [pg], [sgb])
                    self.tt(actT[:, fi, c0:c0 + n], sgb[:, 0:n], pu[:, 0:n], ALU.mult, [sgb, pu], [actT])
            for t in range(NT):
                np_ = 16 if t == ST else 128
                for hf in range(2):
                    pb = self.bank()
                    self.mm(pb, pb[0:np_, :], [(actT[:, fi, t * 128:t * 128 + np_], Wd[s][:, fi, hf * 512:(hf + 1) * 512]) for fi in range(nf)], [actT, Wd[s]])
                    self.add_H(t, np_, hf, pb)
        p.release(m)

    def mixer_A(self, layer, j):
        p = self.p
        m = p.mark()
        Win = self.load_w("Win", self.din['a_w_in'][j], 8, 2048, split=4)
        Wout = self.load_w("WoA", self.din['a_w_out'][j], 8, 1024, split=2)
        gain = self.load_bc("gA", self.din['norm_mix'][layer, :], D)
        lng = self.load_bc("lng", self.din['a_ln_g'][j, :], D)
        lnb = self.load_bc("lnb", self.din['a_ln_b'][j, :], D)
        bsb = self.load_bc("bsb", self.din['a_b_s'][j].rearrange("g t -> (g t)"), 1024)
        bsv = bsb.ap.rearrange("p (g t) -> p g t", g=8)
        wsn = p.sb("wsn", [128, 8, 128], F32)
        wsm = p.sb("wsm", [128, 8, 128], F32)
        wsb = p.sb("wsb", [128, 8, 128], BF16)
        WsT = p.sb("WsT", [128, 8, 128], BF16)
        p.dma('sp', wsn[:, :, :], self.din['a_w_s'][j].rearrange("g t s -> t g s"), wsn, True)
        p.op('pool', lambda e: e.affine_select(wsm[:, :, :], wsn[:, :, :], [[0, 8], [-1, 128]], ALU.is_ge, 0.0, base=0, channel_multiplier=1), reads=[wsn], writes=[wsm])
        p.op('pool', lambda e: e.tensor_copy(wsb[:, :, :], wsm[:, :, :]), reads=[wsm], writes=[wsb])
        pb = self.bank()
        pv = pb.ap.bitcast(BF16).rearrange("p (k n) -> p k n", k=8)
        for g in range(8):
            p.tr(pb, pv[:, g, :], wsb[:, g, :], self.identb[:, :], [wsb, self.identb])
        self.cp(WsT[:, :, :], pv[:, :, :], [pb], [WsT])
        n4 = p.sb("n4", [4, 8, 4], F32)
        n44 = p.sb("n44", [4, 8, 4, 4], F32)
        wk32 = p.sb("wk32", [16, 8, 4, 4], F32)
        wkm = p.sb("wkm", [16, 8, 4, 4], F32)
        wkm2 = p.sb("wkm2", [16, 8, 4, 4], F32)
        Wblk = p.sb("Wblk", [16, 8, 16], BF16)
        p.dma('sp', n4[:, :, :], self.din['a_w_s'][j][:, 0:4, 0:4].rearrange("g t s -> t g s"), n4, True)
        p.op('dve', lambda e: e.tensor_copy(n44[:, :, :, :], n4[:, :, :].unsqueeze(2).broadcast_to([4, 8, 4, 4])), reads=[n4], writes=[n44])
        pbw = self.bank()
        for g in range(8):
            self.mm(pbw, pbw[0:16, g * 4:(g + 1) * 4], [(n44[0:4, g, :, :].rearrange("p b s -> p (b s)"), self.identf[0:4, 0:4])], [n44, self.identf])
        p.op('dve', lambda e: e.tensor_copy(wk32[:, :, :, :], pbw[0:16, 0:32].rearrange("p (g t) -> p g t", g=8).unsqueeze(2).broadcast_to([16, 8, 4, 4])), reads=[pbw], writes=[wk32])
        p.op('pool', lambda e: e.affine_select(wkm[:, :, :, :], wk32[:, :, :, :], [[0, 8], [-4, 4], [0, 4]], ALU.is_ge, 0.0, base=0, channel_multiplier=1), reads=[wk32], writes=[wkm])
        p.op('pool', lambda e: e.affine_select(wkm2[:, :, :, :], wkm[:, :, :, :], [[0, 8], [4, 4], [1, 4]], ALU.is_ge, 0.0, base=0, channel_multiplier=-1), reads=[wkm], writes=[wkm2])
        p.op('pool', lambda e: e.tensor_copy(Wblk[:, :, :], wkm2[:, :, :, :].rearrange("p g b t -> p g (b t)")), reads=[wkm2], writes=[Wblk])
        bss = p.sb("bss", [128, 8, 4, 4], F32)
        p.op('dve', lambda e: e.tensor_copy(bss[:, :, :, :], bsv[:, :, 0:4].unsqueeze(2).broadcast_to([128, 8, 4, 4])), reads=[bsb], writes=[bss])
        bssv = bss.ap.rearrange("p g b t -> p g (b t)")

        xg = p.sb("xgA", [128, 8, 512], BF16)
        uT = p.sb("uT", [128, 8, 512], BF16)
        v32 = [p.sb("v32_%d" % i, [128, 1024], F32) for i in range(2)]
        vtmp = p.sb("vtmp", [128, 1024], F32)
        vnb = [p.sb("vnb%d" % i, [128, 1024], BF16) for i in range(2)]
        gT = [p.sb("gT%d" % i, [128, 8, 128], BF16) for i in range(2)]
        mix = [p.sb("mix%d" % i, [128, 512], F32) for i in range(2)]
        st = p.sb("lnst", [128, 8], F32)
        ti_ = 0
        for grp in range(5):
            if grp < 4:
                tiles = [4 * grp + i for i in range(4)]
                n = 512
            else:
                tiles = [ST]
                n = 16
            self.norm_tiles(tiles, gain, xg, [i * 128 for i in range(len(tiles))])
            for ft in range(8):
                pb = self.bank()
                self.mm(pb, pb[:, 0:n], [(Win[:, k, ft * 128:(ft + 1) * 128], xg[:, k, 0:n]) for k in range(8)], [Win, xg])
                self.act(uT[:, ft, 0:n], pb[:, 0:n], AF.Gelu_apprx_tanh, [pb], [uT])
            def ctx(i, t):
                np_ = 16 if t == ST else 128
                c0 = i * 128
                k_ = (ti_base + i) % 2
                return np_, c0, v32[k_], vnb[k_], gT[k_]

            def stV(i, t):
                np_, c0, v, vb, g_ = ctx(i, t)
                for hf in range(2):
                    pb = self.bank()
                    self.mm(pb, pb[0:np_, :], [(xg[:, k, c0:c0 + np_], Win[:, k, 1024 + hf * 512:1024 + (hf + 1) * 512]) for k in range(8)], [xg, Win])
                    self.act(v[0:np_, hf * 512:(hf + 1) * 512], pb[0:np_, :], AF.Gelu_apprx_tanh, [pb], [v, st], accum=st[0:np_, hf:hf + 1])
                P_ = slice(0, np_)
                self.act(self.junk[P_, :], v[P_, :], AF.Square, [v], [self.junk, st], accum=st[P_, 2:3])
                self.tt(st[P_, 3:4], st[P_, 0:1], st[P_, 1:2], ALU.add, [st], [st])
                self.ts(st[P_, 3:4], st[P_, 3:4], 1.0 / D, None, ALU.mult, None, [st], [st])
                self.tt(st[P_, 4:5], st[P_, 3:4], st[P_, 3:4], ALU.mult, [st], [st])
                self.stt(st[P_, 5:6], st[P_, 2:3], 1.0 / D, st[P_, 4:5], ALU.mult, ALU.subtract, [st], [st])
                self.act(st[P_, 6:7], st[P_, 5:6], AF.Sqrt, [st], [st], bias=self.eps_ln[P_], scale=1.0)
                p.op('dve', (lambda P_: (lambda e: e.reciprocal(st[P_, 7:8], st[P_, 6:7])))(P_), reads=[st], writes=[st])
                self.ts(vtmp[P_, :], v[P_, :], st[P_, 3:4], st[P_, 7:8], ALU.subtract, ALU.mult, [v, st], [vtmp])
                self.tt(vtmp[P_, :], vtmp[P_, :], lng[P_, :], ALU.mult, [vtmp, lng], [vtmp])
                if t == ST:
                    self.tt(v[P_, :], vtmp[P_, :], lnb[P_, :], ALU.add, [vtmp, lnb], [v])
                    p.dma('sp', self.dout['chunk_v'][j], v[P_, :], v, False)
                    self.cp(vb[P_, :], v[P_, :], [v], [vb], eng='dve')
                else:
                    self.tt(vb[P_, :], vtmp[P_, :], lnb[P_, :], ALU.add, [vtmp, lnb], [vb])

            def stS(i, t):
                np_, c0, v, vb, g_ = ctx(i, t)
                for gq in range(2):
                    pb = self.bank()
                    mx = mix[gq]
                    for gi in range(4):
                        g = gq * 4 + gi
                        if t == ST:
                            self.mm(pb, pb[:, gi * 16:(gi + 1) * 16], [(vb[0:16, g * 128:(g + 1) * 128], Wblk[0:16, g, :])], [vb, Wblk])
                        else:
                            self.mm(pb, pb[:, gi * 128:(gi + 1) * 128], [(vb[:, g * 128:(g + 1) * 128], WsT[:, g, :])], [vb, WsT])
                    if t == ST:
                        pv3 = pb[:, 0:64].rearrange("p (g n) -> p g n", g=4)
                        mv3 = mx[:, 0:64].rearrange("p (g n) -> p g n", g=4)
                        self.tt(mv3, pv3, bssv[:, gq * 4:gq * 4 + 4, :], ALU.add, [pb, bss], [mx])
                        self.tt(g_[:, gq * 4:gq * 4 + 4, 0:16], mv3, uT[:, gq * 4:gq * 4 + 4, 0:16], ALU.mult, [mx, uT], [g_])
                    else:
                        pv3 = pb[:, :].rearrange("p (g n) -> p g n", g=4)
                        mv3 = mx[:, :].rearrange("p (g n) -> p g n", g=4)
                        self.tt(mv3, pv3, bsv[:, gq * 4:gq * 4 + 4, :], ALU.add, [pb, bsb], [mx])
                        self.tt(g_[:, gq * 4:gq * 4 + 4, :], mv3, uT[:, gq * 4:gq * 4 + 4, c0:c0 + 128], ALU.mult, [mx, uT], [g_])

            def stO(i, t):
                np_, c0, v, vb, g_ = ctx(i, t)
                for hf in range(2):
                    pb = self.bank()
                    self.mm(pb, pb[0:np_, :], [(g_[:, g, 0:np_], Wout[:, g, hf * 512:(hf + 1) * 512]) for g in range(8)], [g_, Wout])
                    self.add_H(t, np_, hf, pb)

            ti_base = ti_
            ti_ += len(tiles)
            nt_ = len(tiles)
            for step in range(nt_ + 2):
                if step < nt_:
                    stV(step, tiles[step])
                if 0 <= step - 1 < nt_:
                    stS(step - 1, tiles[step - 1])
                if 0 <= step - 2 < nt_:
                    stO(step - 2, tiles[step - 2])
        p.release(m)

    def final(self):
        p = self.p
        m = p.mark()
        gain = self.load_bc("gO", self.din['norm_out'], D)
        yb = [p.sb("yb%d" % i, [128, 1024], F32) for i in range(2)]
        HB = self.HB
        tiles = list(range(NT))
        for t in tiles:
            np_ = 16 if t == ST else 128
            self.act(self.junk[0:np_, :], HB[t][0:np_, :], AF.Square, [HB[t]], [self.junk, self.ss], accum=self.ss[0:np_, t:t + 1])
        self.act(self.sd[:, 0:NT], self.ss[:, 0:NT], AF.Sqrt, [self.ss], [self.sd], bias=self.eps_rms, scale=1.0 / D)
        p.op('dve', lambda e: e.reciprocal(self.rstd[:, 0:NT], self.sd[:, 0:NT]), reads=[self.sd], writes=[self.rstd])
        for t in tiles:
            np_ = 16 if t == ST else 128
            y = yb[t % 2]
            self.stt(y[0:np_, :], HB[t][0:np_, :], self.rstd[0:np_, t:t + 1], gain[0:np_, :], ALU.mult, ALU.mult,
                     [HB[t], self.rstd, gain], [y])
            if t == ST:
                p.dma('sp', self.dout['y_s'], y[0:16, :], y, False)
            else:
                p.dma('sp', self.dout['y_p'][t * 128:(t + 1) * 128, :], y[:, :], y, False)
        p.release(m)


def _add_methods(cls):
    def deco(f):
        setattr(cls, f.__name__, f)
        return f
    return deco


@_add_methods(K)
def mixer_C(self, layer, j):
    p = self.p
    din = self.din
    m = p.mark()
    Wx = self.load_w("Wx", din['c_w_x'][j], 8, 1280, split=2)
    Wg = self.load_w("WgC", din['c_w_gate'][j], 8, 1280, split=2)
    Wo = self.load_w("WoC", din['c_w_out'][j], 10, 1024, split=2)
    Wa = p.sb("Wa", [128, 10, 128], BF16)
    Wi = p.sb("Wi", [128, 10, 128], BF16)
    p.dma('pool', Wa[:, :, :], din['c_w_a'][j].rearrange("n i j -> i n j"), Wa, True)
    p.dma('pool', Wi[:, :, :], din['c_w_i'][j].rearrange("n i j -> i n j"), Wi, True)
    gain = self.load_bc("gC", din['norm_mix'][layer, :], D)
    cw, cb, ba, bi, lam = self.c_cw, self.c_cb, self.c_ba, self.c_bi, self.c_lam
    ex = p.sb("lam_e", [128, 10], F32)
    cl = p.sb("cl", [128, 10], F32)
    self.act(ex[:, :], lam[:, :], AF.Exp, [lam], [ex], scale=-1.0)
    self.act(cl[:, :], ex[:, :], AF.Ln, [ex], [cl], bias=self.one_col, scale=1.0)
    self.ts(cl[:, :], cl[:, :], -8.0, None, ALU.mult, None, [cl], [cl])
    hcl = p.sb("hcl", [128, 10], F32)
    hba = p.sb("hba", [128, 10], F32)
    hbi = p.sb("hbi", [128, 10], F32)
    self.ts(hcl[:, :], cl[:, :], 0.5, None, ALU.mult, None, [cl], [hcl])
    self.ts(hba[:, :], ba[:, :], 0.5, None, ALU.mult, None, [ba], [hba])
    self.ts(hbi[:, :], bi[:, :], 0.5, None, ALU.mult, None, [bi], [hbi])
    tail_p, hst_p, tail_s, hst_s = self.c_tail_p, self.c_hst_p, self.c_tail_s, self.c_hst_s
    xg = p.sb("xgC", [128, 8, 512], BF16)
    ghT = p.sb("ghT", [128, 10, 512], BF16)
    NB = 2
    T = {nm: [p.sb("%s%d" % (nm, i), [128, 520], F32) for i in range(NB)] for nm in ['xx']}
    TH = {nm: [[p.sb("%s%d_%d" % (nm, i, hf), [128, 260], F32) for hf in range(2)] for i in range(NB)] for nm in ['gate', 'xc', 'gi', 'a', 's', 'hh']}
    xcb = [p.sb("xcb%d" % i, [128, 512], BF16) for i in range(NB)]
    xg_s = p.sb("xgCs", [128, 8, 16], BF16)
    ghT_s = p.sb("ghTs", [128, 10, 16], BF16)
    T_s = {'xx': [p.sb("xxs%d" % i, [128, 32], F32) for i in range(NB)]}
    TH_s = {nm: [[p.sb("%ss%d" % (nm, i), [128, 16], F32)] for i in range(NB)] for nm in ['gate', 'xc', 'gi', 'a', 's', 'hh']}
    xcb_s = [p.sb("xcbs%d" % i, [128, 16], BF16) for i in range(NB)]
    T_p, TH_p, xcb_p, xg_p, ghT_p = T, TH, xcb, xg, ghT

    def build_ctx(grp):
        if grp < 4:
            tiles = [4 * grp + i for i in range(4)]
            n, nseq, L = 512, 1, 512
            tail, hst = tail_p, hst_p
            T, TH, xcb, xg, ghT = T_p, TH_p, xcb_p, xg_p, ghT_p
        else:
            tiles = [ST]
            n, nseq, L = 16, 4, 4
            tail, hst = tail_s, hst_s
            T, TH, xcb, xg, ghT = T_s, TH_s, xcb_s, xg_s, ghT_s
        self.norm_tiles(tiles, gain, xg, [i * 128 for i in range(len(tiles))])
        if grp < 4:
            HV = [(0, 256), (256, 256)]
        else:
            HV = [(0, 16)]

        def stA(ft, s_):
            xx = T['xx'][s_]
            gate, xc = TH['gate'][s_], TH['xc'][s_]
            xb = xcb[s_]
            px = self.bank()
            self.mm(px, px[:, 0:n], [(Wx[:, k, ft * 128:(ft + 1) * 128], xg[:, k, 0:n]) for k in range(8)], [Wx, xg])
            pg = self.bank()
            self.mm(pg, pg[:, 0:n], [(Wg[:, k, ft * 128:(ft + 1) * 128], xg[:, k, 0:n]) for k in range(8)], [Wg, xg])
            for hv, (c0, w) in enumerate(HV):
                self.act(gate[hv][:, 0:w], pg[:, c0:c0 + w], AF.Gelu_apprx_tanh, [pg], [gate[hv]])
            xx3 = xx[:, 0:nseq * (L + 3)].rearrange("p (s l) -> p s l", s=nseq)
            self.cp(xx3[:, :, 0:3], tail[:, ft, :, :], [tail], [xx], eng='dve')
            if grp < 4:
                for hv, (c0, w) in enumerate(HV):
                    self.cp(xx[:, 3 + c0:3 + c0 + w], px[:, c0:c0 + w], [px], [xx], eng='dve')
            else:
                self.cp(xx3[:, :, 3:3 + L], px[:, 0:n].rearrange("p (s l) -> p s l", s=nseq), [px], [xx], eng='dve')

            def xin(hv, k):
                c0, w = HV[hv]
                if grp < 4:
                    return xx[:, c0 + k:c0 + k + w], xc[hv][:, 0:w]
                return xx3[:, :, k:k + L], xc[hv][:, 0:n].rearrange("p (s l) -> p s l", s=nseq)
            for hv in range(len(HV)):
                i_, o_ = xin(hv, 0)
                self.ts(o_, i_, cw[0][:, ft:ft + 1], cb[:, ft:ft + 1], ALU.mult, ALU.add, [xx, cw[0], cb], [xc[hv]])
            for k in range(1, 4):
                for hv in range(len(HV)):
                    i_, o_ = xin(hv, k)
                    self.stt(o_, i_, cw[k][:, ft:ft + 1], o_, ALU.mult, ALU.add, [xx, cw[k], xc[hv]], [xc[hv]])
            self.cp(tail[:, ft, :, :], xx3[:, :, L:L + 3], [xx], [tail], eng='dve')
            for hv, (c0, w) in enumerate(HV):
                self.cp(xb[:, c0:c0 + w], xc[hv][:, 0:w], [xc[hv]], [xb], eng='dve')

        def stB(ft, s_):
            gate, xc, gi, a, s, hh = [TH[nm][s_] for nm in ['gate', 'xc', 'gi', 'a', 's', 'hh']]
            xb = xcb[s_]
            pa = self.bank()
            self.mm(pa, pa[:, 0:n], [(Wa[:, ft, :], xb[:, 0:n])], [Wa, xb])
            pi_ = self.bank()
            self.mm(pi_, pi_[:, 0:n], [(Wi[:, ft, :], xb[:, 0:n])], [Wi, xb])
            for hv, (c0, w) in enumerate(HV):
                self.act(a[hv][:, 0:w], pa[:, c0:c0 + w], AF.Tanh, [pa, hba], [a[hv]], bias=hba[:, ft:ft + 1], scale=0.5)
            for hv, (c0, w) in enumerate(HV):
                self.act(gi[hv][:, 0:w], pi_[:, c0:c0 + w], AF.Tanh, [pi_, hbi], [gi[hv]], bias=hbi[:, ft:ft + 1], scale=0.5)
            for hv, (c0, w) in enumerate(HV):
                self.act(a[hv][:, 0:w], a[hv][:, 0:w], AF.Exp, [a[hv], hcl], [a[hv]], scale=hcl[:, ft:ft + 1], bias=hcl[:, ft:ft + 1])
            for hv, (c0, w) in enumerate(HV):
                self.act(s[hv][:, 0:w], a[hv][:, 0:w], AF.Square, [a[hv]], [s[hv]])
            for hv, (c0, w) in enumerate(HV):
                self.act(s[hv][:, 0:w], s[hv][:, 0:w], AF.Sqrt, [s[hv]], [s[hv]], bias=self.one_col, scale=-1.0)
            for hv, (c0, w) in enumerate(HV):
                self.stt(s[hv][:, 0:w], gi[hv][:, 0:w], 1.0, s[hv][:, 0:w], ALU.add, ALU.mult, [s[hv], gi[hv]], [s[hv]])
            for hv, (c0, w) in enumerate(HV):
                self.stt(s[hv][:, 0:w], s[hv][:, 0:w], 0.5, xc[hv][:, 0:w], ALU.mult, ALU.mult, [s[hv], xc[hv]], [s[hv]])
            if grp < 4:
                for hv, (c0, w) in enumerate(HV):
                    ini = hst[:, ft, 0:1] if hv == 0 else hh[hv - 1][:, 255:256]
                    rd = [a[hv], s[hv], hst] if hv == 0 else [a[hv], s[hv], hh[hv - 1]]

                    def sfn(e, hv=hv, w=w, ini=ini):
                        return e.tensor_tensor_scan(hh[hv][:, 0:w], a[hv][:, 0:w], s[hv][:, 0:w], ini, ALU.mult, ALU.add)
                    p.op('dve', sfn, reads=rd, writes=[hh[hv]])
                self.cp(hst[:, ft, :], hh[1][:, 255:256], [hh[1]], [hst], eng='dve')
            else:
                for q in range(nseq):
                    def sfn(e, q=q, L=L, hst=hst, hh=hh, a=a, s=s, ft=ft):
                        return e.tensor_tensor_scan(hh[0][:, q * L:(q + 1) * L], a[0][:, q * L:(q + 1) * L], s[0][:, q * L:(q + 1) * L],
                                                    hst[:, ft, q:q + 1], ALU.mult, ALU.add)
                    p.op('dve', sfn, reads=[a[0], s[0], hst], writes=[hh[0]])
                hh3 = hh[0][:, 0:n].rearrange("p (s l) -> p s l", s=nseq)
                self.cp(hst[:, ft, :], hh3[:, :, L - 1], [hh[0]], [hst], eng='dve')
            for hv, (c0, w) in enumerate(HV):
                self.tt(ghT[:, ft, c0:c0 + w], gate[hv][:, 0:w], hh[hv][:, 0:w], ALU.mult, [gate[hv], hh[hv]], [ghT])

        def outp():
            for i, t in enumerate(tiles):
                np_ = 16 if t == ST else 128
                for hf in range(2):
                    pb = self.bank()
                    self.mm(pb, pb[0:np_, :], [(ghT[:, ft, i * 128:i * 128 + np_], Wo[:, ft, hf * 512:(hf + 1) * 512]) for ft in range(10)], [ghT, Wo])
                    self.add_H(t, np_, hf, pb)
        return stA, stB, outp

    sA, sB, sO = build_ctx(4)
    it = 0
    for grp in range(4):
        A_, B_, O_ = build_ctx(grp)
        sets = [(it + f) % NB for f in range(10)]
        it += 10
        ex = (grp == 0)
        A_(0, sets[0])
        if ex:
            sA(0, 0)
        for ft in range(1, 10):
            A_(ft, sets[ft])
            if ex:
                sA(ft, ft % NB)
            B_(ft - 1, sets[ft - 1])
            if ex:
                sB(ft - 1, (ft - 1) % NB)
        B_(9, sets[9])
        if ex:
            sB(9, 9 % NB)
        O_()
        if ex:
            sO()
    p.release(m)


@_add_methods(K)
def rope_apply(self, np_, d1, d2, x1, x2, c, s, ta, tb, rbufs, wbufs, tbufs):
    self.tt(ta, x1, c, ALU.mult, rbufs, [tbufs[0]])
    self.tt(tb, x2, s, ALU.mult, rbufs, [tbufs[1]])
    self.tt(d1, ta, tb, ALU.subtract, tbufs, wbufs)
    self.tt(ta, x1, s, ALU.mult, rbufs, [tbufs[0]])
    self.tt(tb, x2, c, ALU.mult, rbufs, [tbufs[1]])
    self.tt(d2, ta, tb, ALU.add, tbufs, wbufs)


@_add_methods(K)
def rope_tables2(self, name, np_, nt, posf):
    p = self.p
    cos = p.sb(name + "_cos", [128, nt, 32], F32)
    sin = p.sb(name + "_sin", [128, nt, 32], F32)
    m = p.mark()
    ang = p.sb(name + "_ang", [128, nt, 32], F32)
    q = p.sb(name + "_q", [128, nt, 32], F32)
    qi = p.sb(name + "_qi", [128, nt, 32], I32)
    t1 = p.sb(name + "_t1", [128, nt, 32], F32)
    r = p.sb(name + "_r", [128, nt, 32], F32)
    P = slice(0, np_)
    invb = self.invf_bc
    self.tt(ang[P, :, :], posf[P, :].unsqueeze(2).broadcast_to([np_, nt, 32]),
            invb[P, :].unsqueeze(1).broadcast_to([np_, nt, 32]), ALU.mult, [posf, invb], [ang])

    def reduce_to(dst, shift):
        self.ts(q[P], ang[P], shift, 1.0 / TWO_PI, ALU.add, ALU.mult, [ang], [q])
        p.op('dve', lambda e: e.tensor_copy(qi[P], q[P]), reads=[q], writes=[qi])
        p.op('dve', lambda e: e.tensor_copy(q[P], qi[P]), reads=[qi], writes=[q])
        self.ts(t1[P], ang[P], shift, None, ALU.add, None, [ang], [t1])
        self.stt(t1[P], q[P], -C1, t1[P], ALU.mult, ALU.add, [q, t1], [t1])
        self.stt(t1[P], q[P], -C2, t1[P], ALU.mult, ALU.add, [q, t1], [t1])
        self.ts(q[P], t1[P], math.pi, -TWO_PI, ALU.is_gt, ALU.mult, [t1], [q])
        self.tt(t1[P], t1[P], q[P], ALU.add, [t1, q], [t1])
        self.ts(q[P], t1[P], -math.pi, TWO_PI, ALU.is_lt, ALU.mult, [t1], [q])
        self.tt(t1[P], t1[P], q[P], ALU.add, [t1, q], [t1])
        self.ts(dst[P], t1[P], 3.141592, -3.141592, ALU.min, ALU.max, [t1], [dst])
    reduce_to(r, 0.0)
    self.act(sin[P], r[P], AF.Sin, [r], [sin])
    reduce_to(r, math.pi / 2)
    self.act(cos[P], r[P], AF.Sin, [r], [cos])
    p.release(m)
    return cos, sin


@_add_methods(K)
def mixer_B(self, layer, j):
    p = self.p
    din = self.din
    dout = self.dout
    PB = p.PB
    m = p.mark()
    Wuv = self.load_w("Wuv", din['b_w_uv'][j].rearrange("c h v -> c (h v)"), 2, 1024)
    Wo = self.load_w("WoB", din['b_w_out'][j], 8, 1024, split=2)
    QlT_s = p.sb("QlT_s", [128, 8, 2, 16], BF16)
    QpeT_s = p.sb("QpeT_s", [128, 4, 16], BF16)
    Qpad_s = p.sb("Qpad_s", [128, 8, 16], BF16)
    KTs = p.sb("KTs", [128, 3, 16], BF16)
    Vs = p.sb("Vs", [16, 1, 256], BF16)
    ovT_s = p.sb("ovT_s", [128, 8, 16], BF16)
    stats = [p.sb("stat%d" % i, [128, 16], F32) for i in range(4)]
    mask4 = p.sb("mask4", [32, 4, 16], F32)
    mtmp = p.sb("mtmp", [32, 16], F32)
    zer32 = p.sb("zer32", [32, 16], F32)
    p.op('pool', lambda e: e.memset(zer32[:, :], 0.0), writes=[zer32])
    for b in range(4):
        p.op('pool', (lambda b: (lambda e: e.affine_select(mtmp[:, :], zer32[:, :], [[0, 4], [-8, 4]], ALU.is_ge, NEG, base=0, channel_multiplier=1)))(b), reads=[zer32], writes=[mtmp])
        p.op('pool', (lambda b: (lambda e: e.affine_select(mask4[:, b, :], mtmp[:, :], [[1, 4], [0, 4]], ALU.is_equal, NEG, base=-b, channel_multiplier=0)))(b), reads=[mtmp], writes=[mask4])
    ptT = p.sb("ptT", [128, 4], I32)

    def ptfn(e):
        return e.dma_start(out=ptT[:, :], in_=din['pt'].rearrange("b j -> j b"), allow_slow_non_contiguous=True)
    p.dma_custom('sp', ptfn, ptT, True)
    ptf = p.sb("ptf", [128, 4], F32)
    si = p.sb("si", [128, 8], I32)
    sf = p.sb("sf", [128, 8], F32)
    idxf = p.sb("idxf", [128, 4, 8], F32)
    idx = p.sb("idx", [128, 4, 8], I32)
    p.op('dve', lambda e: e.tensor_copy(ptf[:, :], ptT[:, :]), reads=[ptT], writes=[ptf])
    self.ts(ptf[:, :], ptf[:, :], 8.0, None, ALU.mult, None, [ptf], [ptf])
    p.op('pool', lambda e: e.iota(si[:, :], [[1, 8]], base=0, channel_multiplier=0), writes=[si])
    p.op('dve', lambda e: e.tensor_copy(sf[:, :], si[:, :]), reads=[si], writes=[sf])
    self.tt(idxf[:, :, :], ptf[:, :].unsqueeze(2).broadcast_to([128, 4, 8]), sf[:, :].unsqueeze(1).broadcast_to([128, 4, 8]), ALU.add, [ptf, sf], [idxf])
    p.op('dve', lambda e: e.tensor_copy(idx[:, :, :], idxf[:, :, :]), reads=[idxf], writes=[idx])

    mP = p.mark()
    Wdq = self.load_w("Wdq", din['b_w_dq'][j], 8, 384)
    Wuq = self.load_w("Wuq", din['b_w_uq'][j], 3, 1536)
    Wdkv = self.load_w("Wdkv", din['b_w_dkv'][j], 8, 320)
    WukT = p.sb("WukT", [128, 8, 256], BF16)
    gain = self.load_bc("gB", din['norm_mix'][layer, :], D)
    kvg = self.load_bc("kvg", din['b_kv_norm'][j, :], 256)
    qg = self.load_col("qg", din['b_q_norm'][j, :], 3)
    self.invf_bc = self.load_bc("invf", din['invf'], 32)
    posi = p.sb("posi", [128, 16], I32)
    posf = p.sb("posf", [128, 16], F32)
    p.op('pool', lambda e: e.iota(posi[:, :], [[128, 16]], base=0, channel_multiplier=1), writes=[posi])
    p.op('dve', lambda e: e.tensor_copy(posf[:, :], posi[:, :]), reads=[posi], writes=[posf])
    cos_p, sin_p = self.rope_tables2("rp", 128, 16, posf)
    prow_i = p.sb("prow_i", [1, 16], I32)
    prow_f = p.sb("prow_f", [1, 16], F32)
    posf_s = p.sb("posf_s", [128, 1], F32)
    p.op('pool', lambda e: e.iota(prow_i[:, :], [[0, 4], [1, 4]], base=NPAGES * 128, channel_multiplier=0), writes=[prow_i])
    p.op('dve', lambda e: e.tensor_copy(prow_f[:, :], prow_i[:, :]), reads=[prow_i], writes=[prow_f])
    pbp = self.bank()
    self.mm(pbp, pbp[0:16, 0:1], [(prow_f[0:1, 0:16], self.identf[0:1, 0:1])], [prow_f, self.identf])
    self.cp(posf_s[0:16, :], pbp[0:16, 0:1], [pbp], [posf_s], eng='dve')
    cos_s, sin_s = self.rope_tables2("rs", 16, 1, posf_s)
    mW = p.mark()
    Wukn = self.load_w("Wukn", din['b_w_uk'][j].rearrange("c h n -> c (h n)"), 2, 1024)
    for cc in range(2):
        pb = self.bank()
        pv = pb.ap.bitcast(BF16).rearrange("p (k n) -> p k n", k=8)
        for h in range(8):
            p.tr(pb, pv[:, h, :], Wukn[:, cc, h * 128:(h + 1) * 128], self.identb[:, :], [Wukn, self.identb])
        self.cp(WukT[:, :, cc * 128:(cc + 1) * 128], pv[:, :, :], [pb], [WukT])
    p.release(mW)
    KT = p.sb("KT", [128, 3, 2048], BF16)
    V = p.sb("V", [128, 16, 256], BF16)
    xg = p.sb("xgB", [128, 8, 512], BF16)
    QlT = p.sb("QlT", [128, 8, 2, 512], BF16)
    QpeT = p.sb("QpeT", [128, 4, 512], BF16)
    ovTs = [p.sb("ovT%d" % i, [128, 8, 128], BF16) for i in range(2)]
    sti = [0]

    def newstat():
        s = stats[sti[0] % 4]
        sti[0] += 1
        return s

    for grp in [4, 0, 1, 2, 3]:
        if grp < 4:
            tiles = [4 * grp + i for i in range(4)]
            n = 512
            QlT_g, QpeT_g = QlT, QpeT
            cosb, sinb = cos_p, sin_p
        else:
            tiles = [ST]
            n = 16
            QlT_g, QpeT_g = QlT_s, QpeT_s
            cosb, sinb = cos_s, sin_s
        self.bank_rr = list(range(8))
        self.norm_tiles(tiles, gain, xg, [i * 128 for i in range(len(tiles))])
        m2 = p.mark()
        cqn = p.sb("cqn", [128, 384], BF16)
        cqT = p.sb("cqT", [128, 3, 512], BF16)
        qn = [p.sb("qn%d" % i, [128, 512], BF16) for i in range(2)]
        qpe = p.sb("qpe", [128, 8, 64], BF16)
        kvrow = [p.sb("kvrow%d" % i, [128, 320], F32) for i in range(2)]
        kpe2 = p.sb("kpe2", [128, 2, 64], BF16)
        ra = p.sb("ra", [128, 8, 32], F32)
        rb = p.sb("rb", [128, 8, 32], F32)
        for i, t in enumerate(tiles):
            np_ = 16 if t == ST else 128
            P_ = slice(0, np_)
            c0 = i * 128
            tt_ = 0 if t == ST else t
            st = newstat()
            pq = self.bank()
            self.mm(pq, pq[P_, 0:384], [(xg[:, k, c0:c0 + np_], Wdq[:, k, :]) for k in range(8)], [xg, Wdq])
            self.act(self.junk[P_, 0:384], pq[P_, 0:384], AF.Square, [pq], [self.junk, st], accum=st[P_, 0:1])
            self.act(st[P_, 1:2], st[P_, 0:1], AF.Sqrt, [st], [st], bias=self.eps_rms[P_], scale=1.0 / 384)
            p.op('dve', (lambda st, P_: (lambda e: e.reciprocal(st[P_, 2:3], st[P_, 1:2])))(st, P_), reads=[st], writes=[st])
            self.ts(cqn[P_, :], pq[P_, 0:384], st[P_, 2:3], None, ALU.mult, None, [pq, st], [cqn])
            pb = self.bank()
            pv = pb.ap.bitcast(BF16).rearrange("p (k n) -> p k n", k=8)
            for kc in range(3):
                p.tr(pb, pv[:, kc, 0:np_], cqn[P_, kc * 128:(kc + 1) * 128], self.identb[P_, 0:np_], [cqn, self.identb])
            self.tt(cqT[:, :, c0:c0 + np_], pv[:, 0:3, 0:np_], qg[:, 0:3].unsqueeze(2).broadcast_to([128, 3, np_]), ALU.mult, [pb, qg], [cqT])
            pp = self.bank()
            ppv = pp[P_, :].rearrange("p (h c) -> p h c", c=64)
            self.mm(pp, ppv, [(cqT[:, kc, c0:c0 + np_], Wuq[:, kc, :].rearrange("p (h c) -> p h c", c=192)[:, :, 128:192]) for kc in range(3)], [cqT, Wuq])
            cb_ = cosb[P_, tt_, :].unsqueeze(1).broadcast_to([np_, 8, 32])
            sb_ = sinb[P_, tt_, :].unsqueeze(1).broadcast_to([np_, 8, 32])
            self.rope_apply(np_, qpe[P_, :, 0:32], qpe[P_, :, 32:64], ppv[:, :, 0:32], ppv[:, :, 32:64], cb_, sb_,
                            ra[P_], rb[P_], [pp, cosb, sinb], [qpe], [ra, rb])
            pb = self.bank()
            pv = pb.ap.bitcast(BF16).rearrange("p (k n) -> p k n", k=8)
            for pr in range(4):
                p.tr(pb, pv[:, pr, 0:np_], qpe[P_, 2 * pr:2 * pr + 2, :].rearrange("p h c -> p (h c)"), self.identb[P_, 0:np_], [qpe, self.identb])
            self.cp(QpeT_g[:, :, c0:c0 + np_], pv[:, 0:4, 0:np_], [pb], [QpeT_g])
            kvr = kvrow[i % 2]
            pk = self.bank()
            self.mm(pk, pk[P_, 0:320], [(xg[:, k, c0:c0 + np_], Wdkv[:, k, :]) for k in range(8)], [xg, Wdkv])
            self.act(self.junk[P_, 0:256], pk[P_, 0:256], AF.Square, [pk], [self.junk, st], accum=st[P_, 3:4])
            self.act(st[P_, 4:5], st[P_, 3:4], AF.Sqrt, [st], [st], bias=self.eps_rms[P_], scale=1.0 / 256)
            p.op('dve', (lambda st, P_: (lambda e: e.reciprocal(st[P_, 5:6], st[P_, 4:5])))(st, P_), reads=[st], writes=[st])
            self.stt(kvr[P_, 0:256], pk[P_, 0:256], st[P_, 5:6], kvg[P_, :], ALU.mult, ALU.mult, [pk, st, kvg], [kvr])
            self.rope_apply(np_, kvr[P_, 256:288], kvr[P_, 288:320], pk[P_, 256:288], pk[P_, 288:320],
                            cosb[P_, tt_, :], sinb[P_, tt_, :], ra[P_, 0, :], rb[P_, 0, :], [pk, cosb, sinb], [kvr], [ra, rb])
            if t == ST:
                p.dma('sp', dout['mla_s'], kvr[P_, :], kvr, False)
                Vdst = Vs[0:16, 0, :]
                Vb = Vs
            else:
                p.dma('sp', dout['mla_p'][t * 128:(t + 1) * 128, :], kvr[:, :], kvr, False)
                Vdst = V[:, t, :]
                Vb = V
            self.cp(Vdst, kvr[P_, 0:256], [kvr], [Vb])
            self.cp(kpe2[P_, :, :], kvr[P_, 256:320].unsqueeze(1).broadcast_to([np_, 2, 64]), [kvr], [kpe2], eng='dve')
            pb = self.bank()
            pv = pb.ap.bitcast(BF16).rearrange("p (k n) -> p k n", k=8)
            for cc in range(2):
                p.tr(pb, pv[:, cc, 0:np_], Vdst[:, cc * 128:(cc + 1) * 128], self.identb[P_, 0:np_], [Vb, self.identb])
            p.tr(pb, pv[:, 2, 0:np_], kpe2[P_, :, :].rearrange("p a c -> p (a c)"), self.identb[P_, 0:np_], [kpe2, self.identb])
            if t == ST:
                self.cp(KTs[:, :, 0:16], pv[:, 0:3, 0:16], [pb], [KTs])
            else:
                self.cp(KT[:, :, t * 128:(t + 1) * 128], pv[:, 0:3, :], [pb], [KT])
        for h in range(8):
            pn = self.bank()
            self.mm(pn, pn[:, 0:n], [(Wuq[:, kc, h * 192:h * 192 + 128], cqT[:, kc, 0:n]) for kc in range(3)], [Wuq, cqT])
            q_ = qn[h % 2]
            self.cp(q_[:, 0:n], pn[:, 0:n], [pn], [q_])
            for cc in range(2):
                pl = self.bank()
                self.mm(pl, pl[:, 0:n], [(WukT[:, h, cc * 128:(cc + 1) * 128], q_[:, 0:n])], [WukT, q_])
                self.cp(QlT_g[:, h, cc, 0:n], pl[:, 0:n], [pl], [QlT_g])
        p.release(m2)
        if grp == 4:
            p.op('pool', lambda e: e.memset(Qpad_s[:, :, :], 0.0), writes=[Qpad_s])
            self.cp(Qpad_s[0:64, 0::2, :], QpeT_s[0:64, :, :], [QpeT_s], [Qpad_s], eng='dve')
            self.cp(Qpad_s[64:128, 1::2, :], QpeT_s[64:128, :, :], [QpeT_s], [Qpad_s], eng='dve')
            continue
        m3 = p.mark()
        Pbs = [p.sb("Pb%d" % i, [128, 2048], BF16) for i in range(2)]
        PTss = [p.sb("PTs%d" % i, [128, 16, 128], BF16) for i in range(2)]
        OTs = [p.sb("OTs%d" % i, [128, 2, 128], BF16) for i in range(2)]
        steps = []
        for i, t in enumerate(tiles):
            G_ = 4 if t < 4 else (2 if t < 8 else 1)
            for h0 in range(0, 8, G_):
                steps.append((i, t, list(range(h0, h0 + G_))))

        def stage1(k):
            i, t, hs = steps[k]
            R = k % 2
            base = 4 * R
            G = len(hs)
            nbh = 4 // G
            qc = i * 128
            nk = (t + 1) * 128
            nb = (nk + 511) // 512
            sbufs = []
            for hi, h in enumerate(hs):
                hp = h % 2
                for kb in range(nb):
                    n_ = min(512, nk - kb * 512)
                    bk = PB[base + hi * nbh + kb]
                    sbufs.append(bk)
                    last = (kb == nb - 1)
                    tr_ = [(bk[:, 0:n_], QlT[:, h, 0, qc:qc + 128], KT[:, 0, kb * 512:kb * 512 + n_], True, False),
                           (bk[:, 0:n_], QlT[:, h, 1, qc:qc + 128], KT[:, 1, kb * 512:kb * 512 + n_], False, False),
                           (bk[:, 0:n_], QpeT[hp * 64:(hp + 1) * 64, h // 2, qc:qc + 128], KT[hp * 64:(hp + 1) * 64, 2, kb * 512:kb * 512 + n_], False, not last)]
                    if last:
                        tr_.append((bk[:, n_ - 128:n_], self.identb[:, :], self.maskb[:, :], False, True))
                    self.mm_multi(bk, tr_, [QlT, QpeT, KT, self.identb, self.maskb])
            S3 = p.ps_all[:, base * 512:(base + 4) * 512].rearrange("p (g n) -> p g n", g=G)[:, :, 0:nk]
            Pb3 = Pbs[R].ap.rearrange("p (g n) -> p g n", g=G)[:, :, 0:nk]
            return sbufs, S3, nk, G, Pb3, Pbs[R]

        def stage1ew(info):
            sbufs, S3, nk, G, Pb3, Pb = info
            st = newstat()
            p.op('dve', (lambda st, S3, G: (lambda e: e.reduce_max(st[:, 0:G], S3, AX.X)))(st, S3, G), reads=sbufs, writes=[st])
            self.ts(st[:, 4:4 + G], st[:, 0:G], -MLA_SCALE, None, ALU.mult, None, [st], [st])
            for hi in range(G):
                self.act(Pb3[:, hi, :], S3[:, hi, :], AF.Exp, sbufs + [st], [Pb, st], bias=st[:, 4 + hi:5 + hi], scale=MLA_SCALE, accum=st[:, 8 + hi:9 + hi])
            p.op('dve', (lambda st, G: (lambda e: e.reciprocal(st[:, 12:12 + G], st[:, 8:8 + G])))(st, G), reads=[st], writes=[st])
            if G == 1:
                self.ts(Pb3[:, 0, :], Pb3[:, 0, :], st[:, 12:13], None, ALU.mult, None, [Pb, st], [Pb])
            else:
                self.tt(Pb3, Pb3, st[:, 12:12 + G].unsqueeze(2).broadcast_to([128, G, nk]), ALU.mult, [Pb, st], [Pb])

        def stage2(k):
            i, t, hs = steps[k]
            R = k % 2
            base = 4 * R
            G = len(hs)
            nk = (t + 1) * 128
            Pb = Pbs[R]
            Pb3 = Pb.ap.rearrange("p (g n) -> p g n", g=G)
            PTs = PTss[R]
            nkt = t + 1
            for hi, h in enumerate(hs):
                for k0 in range(0, nkt, 8):
                    k1 = min(nkt, k0 + 8)
                    pb = PB[base + k0 // 8]
                    pv = pb.ap.bitcast(BF16).rearrange("p (k n) -> p k n", k=8)
                    for kt in range(k0, k1):
                        p.tr(pb, pv[:, kt - k0, :], Pb3[:, hi, kt * 128:(kt + 1) * 128], self.identb[:, :], [Pb, self.identb])
                    self.cp(PTs[:, k0:k1, :], pv[:, 0:k1 - k0, :], [pb], [PTs])
                po = PB[base + 2]
                for cc in range(2):
                    self.mm(po, po[:, cc * 128:(cc + 1) * 128], [(V[:, kt, cc * 128:(cc + 1) * 128], PTs[:, kt, :]) for kt in range(nkt)], [V, PTs])
                ot = OTs[h % 2]
                self.cp(ot[:, :, :], po[:, 0:256].rearrange("p (c n) -> p c n", c=2), [po], [ot])
                pv_ = PB[base + 3]
                self.mm(pv_, pv_[:, 0:128], [(Wuv[:, cc, h * 128:(h + 1) * 128], ot[:, cc, :]) for cc in range(2)], [Wuv, ot])
                ovb = ovTs[t % 2]
                self.cp(ovb[:, h, :], pv_[:, 0:128], [pv_], [ovb])
                if h == 7:
                    for hf in range(2):
                        pb = PB[base + 2 + hf]
                        self.mm(pb, pb[:, :], [(ovb[:, hh_, :], Wo[:, hh_, hf * 512:(hf + 1) * 512]) for hh_ in range(8)], [ovb, Wo])
                        self.add_H(t, 128, hf, pb)

        stage1ew(stage1(0))
        for k in range(1, len(steps)):
            inf = stage1(k)
            stage2(k - 1)
            stage1ew(inf)
        stage2(len(steps) - 1)
        p.release(m3)
    p.release(mP)
    self.bank_rr = [4, 5, 6, 7]
    gbuf = [p.sb("gbuf%d" % i, [128, 16, 320], F32) for i in range(2)]
    kb16 = [p.sb("kb16_%d" % i, [128, 16, 384], BF16) for i in range(2)]
    KTsegs = [p.sb("KTseg%d" % i, [128, 3, 2048], BF16) for i in range(2)]
    Ps = p.sb("Ps", [32, 2048], BF16)
    PTq = p.sb("PTq", [128, 16, 32], BF16)
    Oacc = p.sb("Oacc", [32, 256], F32)
    Onb = p.sb("Onb", [32, 256], BF16)
    OTq = p.sb("OTq", [128, 2, 32], BF16)
    sn = p.sb("sn", [32, 16], F32)
    run = p.sb("run", [32, 4], F32)
    cache = din['cache']
    gi_ = 0
    kbank = [4, 5, 6, 7]
    kbi = 0
    R32 = slice(0, 32)

    def flash(S_ap, sbufs, n):
        st = newstat()
        p.op('dve', (lambda st: (lambda e: e.reduce_max(st[R32, 0:1], S_ap, AX.X)))(st), reads=sbufs, writes=[st])
        self.tt(st[R32, 1:2], run[:, 0:1], st[R32, 0:1], ALU.max, [run, st], [st])
        self.tt(st[R32, 2:3], run[:, 0:1], st[R32, 1:2], ALU.subtract, [run, st], [st])
        self.act(st[R32, 3:4], st[R32, 2:3], AF.Exp, [st], [st], scale=MLA_SCALE)
        self.ts(st[R32, 4:5], st[R32, 1:2], -MLA_SCALE, None, ALU.mult, None, [st], [st])
        self.act(Ps[:, 0:n], S_ap, AF.Exp, sbufs + [st], [Ps, st], bias=st[R32, 4:5], scale=MLA_SCALE, accum=st[R32, 5:6])
        self.stt(run[:, 1:2], run[:, 1:2], st[R32, 3:4], st[R32, 5:6], ALU.mult, ALU.add, [run, st], [run])
        self.cp(run[:, 0:1], st[R32, 1:2], [st], [run], eng='dve')
        return st

    Qs = p.sb("Qs", [128, 3, 4, 32], BF16)
    for c in range(2):
        self.cp(Qs[:, c, :, :].rearrange("p b (t h) -> p b t h", h=8), QlT_s[:, :, c, :].rearrange("p h (b t) -> p b t h", b=4), [QlT_s], [Qs], eng='dve')
    self.cp(Qs[:, 2, :, :].rearrange("p b (t h) -> p b t h", h=8), Qpad_s[:, :, :].rearrange("p h (b t) -> p b t h", b=4), [Qpad_s], [Qs], eng='dve')
    segs = [(b, s_) for b in range(4) for s_ in range(8)]

    def prep(k):
        b, s_ = segs[k]
        g = gbuf[k % 2]
        kb_ = kb16[k % 2]
        KTg = KTsegs[k % 2]

        def gfn(e, g=g, b=b, s_=s_):
            return e.indirect_dma_start(out=g[:, :, :].rearrange("p t c -> p (t c)"), out_offset=None, in_=cache[:, :],
                                        in_offset=bass.IndirectOffsetOnAxis(ap=idx[:, b, s_:s_ + 1], axis=0))
        p.dma_custom('pool', gfn, g, True, reads=[idx])
        self.cp(kb_[:, 0:8, 0:320], g[:, 0:8, :], [g], [kb_], eng='dve')
        self.cp(kb_[:, 8:16, 0:320], g[:, 8:16, :], [g], [kb_], eng='act')
        self.cp(kb_[:, :, 320:384], g[:, :, 256:320], [g], [kb_], eng='dve')

    def prepT(k):
        kb_ = kb16[k % 2]
        KTg = KTsegs[k % 2]
        for ch in range(3):
            for k0 in (0, 8):
                pb = PB[kbank[self.kbi % 4]]
                self.kbi += 1
                pv = pb.ap.bitcast(BF16).rearrange("p (k n) -> p k n", k=8)
                for kt in range(k0, k0 + 8):
                    p.tr(pb, pv[:, kt - k0, :], kb_[:, kt, ch * 128:(ch + 1) * 128], self.identb[:, :], [kb_, self.identb])
                self.cp(KTg[:, ch, k0 * 128:(k0 + 8) * 128], pv[:, :, :].rearrange("p k n -> p (k n)"), [pb], [KTg])

    def sc(k):
        b, s_ = segs[k]
        KTg = KTsegs[k % 2]
        if s_ == 0:
            p.op('pool', lambda e: e.memset(Oacc[:, :], 0.0), writes=[Oacc])
            p.op('pool', lambda e: e.memset(run[:, 0:1], -1e30), writes=[run])
            p.op('pool', lambda e: e.memset(run[:, 1:2], 0.0), writes=[run])
        Qc = [Qs[:, c, b, :] for c in range(3)]
        sbufs = []
        for kq in range(4):
            bk = PB[kq]
            sbufs.append(bk)
            self.mm(bk, bk[R32, 0:512], [(Qc[c], KTg[:, c, kq * 512:(kq + 1) * 512]) for c in range(3)], [Qs, KTg])
        return flash(p.ps_all[R32, 0:2048], sbufs, 2048)

    def pvs(k, st):
        b, s_ = segs[k]
        kb_ = kb16[k % 2]
        pb = self.bank()
        pv = pb.ap.bitcast(BF16)[:, 0:512].rearrange("p (k n) -> p k n", k=16)
        for kt in range(16):
            p.tr(pb, pv[:, kt, :], Ps[:, kt * 128:(kt + 1) * 128], self.identb[R32, 0:32], [Ps, self.identb])
        self.cp(PTq[:, :, :], pv[:, :, :], [pb], [PTq])
        po = self.bank()
        self.mm(po, po[R32, 0:256], [(PTq[:, kt, :], kb_[:, kt, 0:256]) for kt in range(16)], [PTq, kb_])
        self.stt(Oacc[:, :], Oacc[:, :], st[R32, 3:4], po[R32, 0:256], ALU.mult, ALU.add, [Oacc, st, po], [Oacc])

    def finish_b(b):
        Qc = [Qs[:, c, b, :] for c in range(3)]
        pbn = self.bank()
        self.mm(pbn, pbn[R32, 0:16], [(Qc[c], KTs[:, c, 0:16]) for c in range(3)], [Qs, KTs])
        self.tt(sn[:, :], pbn[R32, 0:16], mask4[:, b, :], ALU.add, [pbn, mask4], [sn])
        st = flash(sn[:, :], [sn], 16)
        pb = self.bank()
        pvn = pb.ap.bitcast(BF16)
        p.tr(pb, pvn[0:16, 0:32], Ps[:, 0:16], self.identb[R32, 0:32], [Ps, self.identb])
        self.cp(PTq[0:16, 0, :], pvn[0:16, 0:32], [pb], [PTq])
        po = self.bank()
        self.mm(po, po[R32, 0:256], [(PTq[0:16, 0, :], Vs[0:16, 0, :])], [PTq, Vs])
        self.stt(Oacc[:, :], Oacc[:, :], st[R32, 3:4], po[R32, 0:256], ALU.mult, ALU.add, [Oacc, st, po], [Oacc])
        st = newstat()
        p.op('dve', (lambda st: (lambda e: e.reciprocal(st[R32, 0:1], run[:, 1:2])))(st), reads=[run], writes=[st])
        self.ts(Onb[:, :], Oacc[:, :], st[R32, 0:1], None, ALU.mult, None, [Oacc, st], [Onb])
        pb = self.bank()
        pv = pb.ap.bitcast(BF16)[:, 0:64].rearrange("p (c n) -> p c n", c=2)
        for cc in range(2):
            p.tr(pb, pv[:, cc, :], Onb[:, cc * 128:(cc + 1) * 128], self.identb[R32, 0:32], [Onb, self.identb])
        self.cp(OTq[:, :, :], pv[:, :, :], [pb], [OTq])
        pvv = self.bank()
        for h in range(8):
            self.mm(pvv, pvv[:, h * 4:(h + 1) * 4], [(Wuv[:, cc, h * 128:(h + 1) * 128], OTq[:, cc, h::8]) for cc in range(2)], [Wuv, OTq])
        self.cp(ovT_s[:, :, b * 4:(b + 1) * 4], pvv[:, 0:32].rearrange("p (h t) -> p h t", h=8), [pvv], [ovT_s], eng='dve')

    self.kbi = 0
    prep(0)
    prepT(0)
    prep(1)
    for k in range(len(segs)):
        st = sc(k)
        if k + 1 < len(segs):
            prepT(k + 1)
        pvs(k, st)
        if segs[k][1] == 7:
            finish_b(segs[k][0])
        if k + 2 < len(segs):
            prep(k + 2)
    self.bank_rr = list(range(8))
    for hf in range(2):
        pb = self.bank()
        self.mm(pb, pb[0:16, :], [(ovT_s[:, h, 0:16], Wo[:, h, hf * 512:(hf + 1) * 512]) for h in range(8)], [ovT_s, Wo])
        self.add_H(ST, 16, hf, pb)
    p.release(m)


@_add_methods(K)
def c_setup(self):
    p = self.p
    din = self.din
    self.c_cw = [self.load_col("cw%d" % k, din['c_conv_w'][0, k, :], 10) for k in range(4)]
    self.c_cb = self.load_col("cb", din['c_conv_b'][0, :], 10)
    self.c_ba = self.load_col("ba", din['c_b_a'][0, :], 10)
    self.c_bi = self.load_col("bi", din['c_b_i'][0, :], 10)
    self.c_lam = self.load_col("lam", din['c_lambda'][0, :], 10)
    tail_p = self.c_tail_p = p.sb("tail_p", [128, 10, 1, 3], F32)
    hst_p = self.c_hst_p = p.sb("hst_p", [128, 10, 1], F32)
    tail_s = self.c_tail_s = p.sb("tail_s", [128, 10, 4, 3], F32)
    hst_s = self.c_hst_s = p.sb("hst_s", [128, 10, 4], F32)
    p.op('pool', lambda e: e.memset(tail_p[:, :, :, :], 0.0), writes=[tail_p])
    p.op('pool', lambda e: e.memset(hst_p[:, :, :], 0.0), writes=[hst_p])
    for b in range(4):
        for k in range(3):
            def fn(e, b=b, k=k):
                return e.dma_start(out=tail_s[:, :, b, k], in_=din['st_conv'][b, k, :].rearrange("(t p) -> p t", p=128), allow_slow_non_contiguous=True)
            p.dma_custom('sp', fn, tail_s, True)

        def fn2(e, b=b):
            return e.dma_start(out=hst_s[:, :, b], in_=din['st_h'][b, :].rearrange("(t p) -> p t", p=128), allow_slow_non_contiguous=True)
        p.dma_custom('sp', fn2, hst_s, True)


@_add_methods(K)
def c_state_out(self):
    p = self.p
    dout = self.dout
    tail_p, hst_p, tail_s, hst_s = self.c_tail_p, self.c_hst_p, self.c_tail_s, self.c_hst_s

    def st_out(dst, src, buf):
        def fn(e):
            return e.dma_start(out=dst.rearrange("(t p) -> p t", p=128), in_=src, allow_slow_non_contiguous=True)
        p.dma_custom('sp', fn, buf, False)
    st_out(dout['lru_h_p'], hst_p[:, :, 0], hst_p)
    for k in range(3):
        st_out(dout['conv_p'][k, :], tail_p[:, :, 0, k], tail_p)
    for b in range(4):
        st_out(dout['lru_h_s'][b, :], hst_s[:, :, b], hst_s)
        for k in range(3):
            st_out(dout['conv_s'][b, k, :], tail_s[:, :, b, k], tail_s)

DEPTH_RUN = 4
NC_RUN = 8
_CACHE = {}

W_SHAPES = {
    'norm_mix': (4, 1024), 'norm_ffn': (4, 1024), 'norm_out': (1024,),
    'a_w_in': (2, 1024, 2048), 'a_ln_g': (2, 1024), 'a_ln_b': (2, 1024), 'a_w_s': (2, 8, 128, 128),
    'a_b_s': (2, 8, 128), 'a_w_out': (2, 1024, 1024),
    'b_w_dq': (1, 1024, 384), 'b_q_norm': (1, 384), 'b_w_uq': (1, 384, 1536), 'b_w_dkv': (1, 1024, 320),
    'b_kv_norm': (1, 256), 'b_w_uk': (1, 256, 8, 128), 'b_w_uv': (1, 256, 8, 128), 'b_w_out': (1, 1024, 1024),
    'c_w_x': (1, 1024, 1280), 'c_w_gate': (1, 1024, 1280), 'c_conv_w': (1, 4, 1280), 'c_conv_b': (1, 1280),
    'c_w_a': (1, 10, 128, 128), 'c_b_a': (1, 1280), 'c_w_i': (1, 10, 128, 128), 'c_b_i': (1, 1280),
    'c_lambda': (1, 1280), 'c_w_out': (1, 1280, 1024),
    'f_w_gate': (4, 1024, 2816), 'f_w_up': (4, 1024, 2816), 'f_w_down': (4, 2816, 1024),
}


def build(depth):
    k = K(depth)
    p = k.p
    k.inp('xp', [2048, 1024])
    k.inp('xs', [16, 1024])
    k.inp('cache', [NPOOL * 8, TSEG * 320])
    k.inp('pt', [4, 128], I32)
    k.inp('st_h', [4, 1280])
    k.inp('st_conv', [4, 3, 1280])
    k.inp('invf', [32])
    for n, s in W_SHAPES.items():
        k.inp(n, s)
    k.outp('y_p', [2048, 1024])
    k.outp('y_s', [16, 1024])
    k.outp('chunk_v', [2, 16, 1024])
    k.outp('mla_p', [2048, 320])
    k.outp('mla_s', [16, 320])
    k.outp('lru_h_p', [1280])
    k.outp('lru_h_s', [4, 1280])
    k.outp('conv_p', [3, 1280])
    k.outp('conv_s', [4, 3, 1280])
    k.setup_consts()
    k.HB = [p.sb("H%d" % t, [128, 1024], F32) for t in range(NT)]
    k.xn_bufs = [p.sb("xn%d" % i, [128, 1024], BF16) for i in range(2)]
    k.xn_i = 0
    for t in range(16):
        p.dma('sp', k.HB[t][:, :], k.din['xp'][t * 128:(t + 1) * 128, :], k.HB[t], True)
    p.dma('sp', k.HB[ST][0:16, :], k.din['xs'], k.HB[ST], True)
    k.mid_hook = None
    if depth >= 3:
        k.c_setup()
    for layer in range(depth):
        kind, j = layer % 3, layer // 3
        if kind == 0:
            k.mixer_A(layer, j)
        elif kind == 1:
            k.mixer_B(layer, j)
        else:
            k.mixer_C(layer, j)
            k.mid_hook = k.c_state_out
        k.ffn(layer)
    k.final()
    p.finish()
    p.emit()
    return k


def kernel(**inputs):
    depth = DEPTH_RUN
    if depth not in _CACHE:
        _CACHE[depth] = build(depth)
    k = _CACHE[depth]
    f32 = np.float32
    xp = np.ascontiguousarray(np.asarray(inputs['x_prompt'], dtype=f32))
    xs = np.ascontiguousarray(np.asarray(inputs['x_sample'], dtype=f32))
    cache = np.ascontiguousarray(np.asarray(inputs['cache_mla'], dtype=f32)).reshape(NPOOL * 8, TSEG * 320)
    pt = np.ascontiguousarray(np.asarray(inputs['page_table']).astype(np.int32))
    sth = np.ascontiguousarray(np.asarray(inputs['state_lru_h'], dtype=f32))
    stc = np.ascontiguousarray(np.asarray(inputs['state_lru_conv'], dtype=f32))
    invf = (1.0 / (np.float32(10000.0) ** (np.arange(0, 64, 2, dtype=f32) / np.float32(64)))).astype(f32)
    ws = {n: np.ascontiguousarray(np.asarray(inputs[n], dtype=f32)) for n in W_SHAPES}
    in_maps = []
    for c in range(NC_RUN):
        d = dict(ws)
        d['xp'] = xp[c]
        d['xs'] = xs[4 * c:4 * c + 4].reshape(16, 1024)
        d['cache'] = cache
        d['pt'] = pt[4 * c:4 * c + 4]
        d['st_h'] = sth[0, 4 * c:4 * c + 4]
        d['st_conv'] = stc[0, 4 * c:4 * c + 4]
        d['invf'] = invf
        in_maps.append(d)
    res = run_bass_kernel_spmd(k.nc, in_maps, core_ids=list(range(NC_RUN)))
    R = list(res.results)
    while len(R) < 8:
        R.append(R[0])
    y_p = np.stack([R[c]['y_p'] for c in range(8)])
    y_s = np.concatenate([R[c]['y_s'].reshape(4, 4, 1024) for c in range(8)])
    chunk_v = np.concatenate([R[c]['chunk_v'].reshape(2, 4, 4, 1024) for c in range(8)], axis=1)
    mla_p = np.stack([R[c]['mla_p'] for c in range(8)])[None]
    mla_s = np.concatenate([R[c]['mla_s'].reshape(4, 4, 320) for c in range(8)])[None]
    lru_h_p = np.stack([R[c]['lru_h_p'] for c in range(8)])[None]
    lru_h_s = np.concatenate([R[c]['lru_h_s'] for c in range(8)])[None]
    conv_p = np.stack([R[c]['conv_p'] for c in range(8)])[None]
    conv_s = np.concatenate([R[c]['conv_s'] for c in range(8)])[None]
    outs = (y_p, y_s, chunk_v, mla_p, mla_s, lru_h_p, lru_h_s, conv_p, conv_s)
    return tuple(np.ascontiguousarray(o.astype(np.float32)) for o in outs)
```
